# Optimizing a Trainium2 kernel written in Bass

```python
import math
import jax, jax.numpy as jnp
from jax import lax
import numpy as np

D_MODEL = 1024
BATCH = 2
SEQ = 16384
DEPTH = 2

N_MIXERS = 2
ATTN_HEADS = 8
ATTN_HEAD_DIM = D_MODEL // ATTN_HEADS // 2
ATTN_V_DIM = 2 * ATTN_HEAD_DIM
Q_BLOCK = 128
SGU_WIDTH = 2 * D_MODEL
SGU_GROUPS = 8
SGU_GROUP_DIM = SGU_WIDTH // SGU_GROUPS
CHUNK = 128
FFN_DIM = 2816
CONV_WIDTH = 3
NORM_EPS = 1e-6
LN_EPS = 1e-5

N_ATTN_LAYERS = (DEPTH + N_MIXERS - 1) // N_MIXERS
N_SGU_LAYERS = DEPTH // N_MIXERS

kernel_name = "hybrid_diffattn_chunksgu_convffn"


def _alibi_slopes(n_heads):
    return jnp.asarray(2.0 ** (-8.0 * np.arange(1, n_heads + 1) / n_heads), dtype=jnp.float32)


def rms_norm(x, gain):
    xf = x.astype(jnp.float32)
    y = xf * lax.rsqrt(jnp.mean(xf * xf, axis=-1, keepdims=True) + NORM_EPS) * gain.astype(jnp.float32)
    return y.astype(x.dtype)


def diff_attention(h, w_qkv, lq1, lk1, lq2, lk2, subln, w_o, layer_idx):
    B, S, _ = h.shape
    H, DH, VD = ATTN_HEADS, ATTN_HEAD_DIM, ATTN_V_DIM
    qkv = h @ w_qkv
    q, k, v = jnp.split(qkv, [H * 2 * DH, 2 * H * 2 * DH], axis=-1)
    q = q.reshape(B, S, H, 2, DH).transpose(0, 2, 3, 1, 4)
    k = k.reshape(B, S, H, 2, DH).transpose(0, 2, 3, 1, 4)
    v = v.reshape(B, S, H, VD).transpose(0, 2, 1, 3)

    lam_init = 0.8 - 0.6 * math.exp(-0.3 * layer_idx)
    f32 = jnp.float32
    lam = (jnp.exp(jnp.sum(lq1.astype(f32) * lk1.astype(f32)))
           - jnp.exp(jnp.sum(lq2.astype(f32) * lk2.astype(f32))) + lam_init)
    slopes = _alibi_slopes(H)
    s_pos = jnp.arange(S)
    n_qb = S // Q_BLOCK
    q_blocks = q.reshape(B, H, 2, n_qb, Q_BLOCK, DH).transpose(3, 0, 1, 2, 4, 5)
    scale = DH ** -0.5

    def one_block(args):
        qb, bi = args
        t_pos = bi * Q_BLOCK + jnp.arange(Q_BLOCK)
        dist = (t_pos[:, None] - s_pos[None, :]).astype(f32)
        bias = jnp.where(dist[None] >= 0, -slopes[:, None, None] * dist[None], -jnp.inf)
        scores = jnp.einsum('bhmqd,bhmkd->bhmqk', qb, k).astype(f32) * scale + bias[None, :, None]
        p = jax.nn.softmax(scores, axis=-1)
        p_diff = p[:, :, 0] - lam * p[:, :, 1]
        o = jnp.einsum('bhqk,bhkd->bhqd', p_diff.astype(v.dtype), v)
        return rms_norm(o, subln) * (1.0 - lam_init)

    out = lax.map(one_block, (q_blocks, jnp.arange(n_qb)))
    out = out.transpose(1, 0, 3, 2, 4).reshape(B, S, H * VD)
    return out @ w_o


def chunked_sgu(h, w_in, ln_g, ln_b, w_s, b_s, w_out):
    B, S, _ = h.shape
    z = jax.nn.gelu(h @ w_in)
    u, v = jnp.split(z, 2, axis=-1)
    vf = v.astype(jnp.float32)
    mu = jnp.mean(vf, axis=-1, keepdims=True)
    var = jnp.mean(jnp.square(vf - mu), axis=-1, keepdims=True)
    v = ((vf - mu) * lax.rsqrt(var + LN_EPS) * ln_g.astype(jnp.float32)
         + ln_b.astype(jnp.float32)).astype(h.dtype)
    n_c = S // CHUNK
    v = v.reshape(B, n_c, CHUNK, SGU_GROUPS, SGU_GROUP_DIM)
    causal = jnp.tril(jnp.ones((CHUNK, CHUNK), dtype=bool))
    w = jnp.where(causal[None], w_s, jnp.zeros_like(w_s))
    s = jnp.einsum('gts,bcsgd->bctgd', w, v) + b_s.T[None, None, :, :, None]
    y = u * s.reshape(B, S, SGU_WIDTH)
    return y @ w_out


def conv_ffn(h, w_up, conv_w, conv_b, w_down):
    a, g = jnp.split(h @ w_up, 2, axis=-1)
    S = a.shape[1]
    a_pad = jnp.pad(a, ((0, 0), (CONV_WIDTH - 1, 0), (0, 0)))
    a_conv = conv_b
    for j in range(CONV_WIDTH):
        a_conv = a_conv + a_pad[:, j:j + S] * conv_w[j]
    return (jax.nn.gelu(a_conv) * g) @ w_down


def setup_inputs(seed: int = 0) -> dict:
    key = jax.random.key(seed)
    ks = iter(jax.random.split(key, 32))
    f32 = jnp.float32
    D, E, F, G, C, DH = D_MODEL, SGU_WIDTH, FFN_DIM, SGU_GROUPS, CHUNK, ATTN_HEAD_DIM
    NA, NG, L = N_ATTN_LAYERS, N_SGU_LAYERS, DEPTH

    def nrm(shape, scale):
        return jax.random.normal(next(ks), shape, f32) * scale

    def gain(shape):
        return 1.0 + nrm(shape, 0.05)

    return {
        "x": nrm((BATCH, SEQ, D), 1.0),
        "attn_w_qkv": nrm((NA, D, 3 * D), D ** -0.5),
        "attn_lambda_q1": nrm((NA, DH), 0.1),
        "attn_lambda_k1": nrm((NA, DH), 0.1),
        "attn_lambda_q2": nrm((NA, DH), 0.1),
        "attn_lambda_k2": nrm((NA, DH), 0.1),
        "attn_subln": gain((NA, ATTN_V_DIM)),
        "attn_w_o": nrm((NA, D, D), D ** -0.5),
        "sgu_w_in": nrm((NG, D, 2 * E), D ** -0.5),
        "sgu_ln_g": gain((NG, E)),
        "sgu_ln_b": nrm((NG, E), 0.02),
        "sgu_w_s": nrm((NG, G, C, C), C ** -0.5),
        "sgu_b_s": gain((NG, G, C)),
        "sgu_w_out": nrm((NG, E, D), E ** -0.5),
        "norm_mix_pre": gain((L, D)),
        "norm_mix_post": gain((L, D)),
        "norm_ffn_pre": gain((L, D)),
        "norm_ffn_post": gain((L, D)),
        "ffn_w_up": nrm((L, D, 2 * F), D ** -0.5),
        "ffn_conv_w": nrm((L, CONV_WIDTH, F), CONV_WIDTH ** -0.5),
        "ffn_conv_b": nrm((L, F), 0.02),
        "ffn_w_down": nrm((L, F, D), F ** -0.5),
    }


def reference(x, attn_w_qkv, attn_lambda_q1, attn_lambda_k1, attn_lambda_q2, attn_lambda_k2,
              attn_subln, attn_w_o, sgu_w_in, sgu_ln_g, sgu_ln_b, sgu_w_s, sgu_b_s, sgu_w_out,
              norm_mix_pre, norm_mix_post, norm_ffn_pre, norm_ffn_post,
              ffn_w_up, ffn_conv_w, ffn_conv_b, ffn_w_down):
    for i in range(DEPTH):
        j = i // N_MIXERS
        hn = rms_norm(x, norm_mix_pre[i])
        if i % N_MIXERS == 0:
            m = diff_attention(hn, attn_w_qkv[j], attn_lambda_q1[j], attn_lambda_k1[j],
                               attn_lambda_q2[j], attn_lambda_k2[j], attn_subln[j], attn_w_o[j], i)
        else:
            m = chunked_sgu(hn, sgu_w_in[j], sgu_ln_g[j], sgu_ln_b[j], sgu_w_s[j], sgu_b_s[j],
                            sgu_w_out[j])
        x = x + rms_norm(m, norm_mix_post[i])
        f = conv_ffn(rms_norm(x, norm_ffn_pre[i]), ffn_w_up[i], ffn_conv_w[i], ffn_conv_b[i],
                     ffn_w_down[i])
        x = x + rms_norm(f, norm_ffn_post[i])
    return x
```

```python
import contextlib
import numpy as np
import ml_dtypes
import concourse.bass as bass
import concourse.mybir as mybir
from concourse.bass_utils import run_bass_kernel_spmd

F32 = mybir.dt.float32
BF16 = mybir.dt.bfloat16
I32 = mybir.dt.int32
AF = mybir.ActivationFunctionType
ALU = mybir.AluOpType
AX = mybir.AxisListType

D = 1024
H = 8
DH = 64
FF = 2816
NFC = FF // 128
E = 2048
NEC = E // 128
NORM_EPS = 1e-6
LN_EPS = 1e-5
LAM_INIT = 0.2
WIN_A = 8
HALO = 512
NEG = -30000.0
_STOP = 99
_SKIP0 = True
_TESTF = 0
_ABL = 0
_OUTINT = False


class Buf:
    __slots__ = ("name", "lw", "rs")

    def __init__(self, name):
        self.name = name
        self.lw = None
        self.rs = []


class Op:
    __slots__ = ("eng", "fn", "dma", "key", "deps", "sig", "seq", "inc")

    def __init__(self, eng, fn, dma, key, inc):
        self.eng, self.fn, self.dma, self.key, self.inc = eng, fn, dma, key, inc
        self.deps = []
        self.sig = dma
        self.seq = 0


class Prog:
    ENGS = ("sp", "act", "dve", "pool", "pe")

    def __init__(self, nc, tag, semstack):
        self.nc = nc
        self.tag = tag
        self.semstack = semstack
        self.ops = []
        self.lastkey = {}

    def add(self, eng, fn, reads=(), writes=(), dma=False, key=None, inc=16):
        if dma and key is None:
            key = writes[0].name
        op = Op(eng, fn, dma, key, inc if dma else 1)
        hard, war = [], []
        for b in reads:
            if b.lw is not None:
                hard.append(b.lw)
        for b in writes:
            if b.lw is not None:
                hard.append(b.lw)
            war.extend(b.rs)
        deps = {}
        for d in hard:
            same = (d.eng == eng) and not d.dma and not dma
            if same and eng == "pe":
                continue
            deps[id(d)] = d
        for d in war:
            same = (d.eng == eng) and not d.dma and not dma
            if same:
                continue
            deps[id(d)] = d
        if dma and key in self.lastkey:
            d = self.lastkey[key]
            deps[id(d)] = d
        if dma:
            self.lastkey[key] = op
        for d in deps.values():
            d.sig = True
            op.deps.append(d)
        for b in reads:
            b.rs.append(op)
        for b in writes:
            b.lw = op
            b.rs = []
        self.ops.append(op)
        return op

    def emit(self):
        nc = self.nc
        st = self.semstack
        if True:
            esem = {e: st.enter_context(nc.semaphore(f"{self.tag}_{e}")) for e in self.ENGS}
            keys = []
            for op in self.ops:
                if op.dma and op.key not in keys:
                    keys.append(op.key)
            ksem = {k: st.enter_context(nc.semaphore(f"{self.tag}_k{i}")) for i, k in enumerate(keys)}
            cnt = {e: 0 for e in self.ENGS}
            kcnt = {k: 0 for k in keys}
            for op in self.ops:
                if op.dma:
                    kcnt[op.key] += op.inc
                    op.seq = kcnt[op.key]
                elif op.sig:
                    cnt[op.eng] += 1
                    op.seq = cnt[op.eng]
            per = {e: [o for o in self.ops if o.eng == e] for e in self.ENGS}

            def run(ename, e):
                waited = {}
                for op in per[ename]:
                    for d in op.deps:
                        sem = ksem[d.key] if d.dma else esem[d.eng]
                        sid = id(sem)
                        if waited.get(sid, 0) >= d.seq:
                            continue
                        e.wait_ge(sem, d.seq)
                        waited[sid] = d.seq
                    ins = op.fn(e)
                    if op.dma:
                        ins.then_inc(ksem[op.key], op.inc)
                    elif op.sig:
                        ins.then_inc(esem[op.eng], 1)
                if ename == "sp":
                    for k in keys:
                        if kcnt[k] > 0:
                            e.wait_ge(ksem[k], kcnt[k])

            with nc.Block() as block:
                @block.sync
                def _(e):
                    run("sp", e)

                @block.scalar
                def _(e):
                    run("act", e)

                @block.vector
                def _(e):
                    run("dve", e)

                @block.gpsimd
                def _(e):
                    run("pool", e)

                @block.tensor
                def _(e):
                    run("pe", e)


def build_program(S):
    NT = S // 512
    NKB = S // 128
    T = S // 4
    TL = T + HALO
    NL = TL // 512

    nc = bass.Bass("TRN2", target_bir_lowering=False)
    din = lambda n, s, d: nc.dram_tensor(n, s, d, kind="ExternalInput")
    xT_full = din("xT_full", [D, S], F32)
    xT_loc = din("xT_loc", [D, TL], F32)
    valid0 = din("valid0", [128, 512], F32)
    wqkvL = [[din(f"wqkv{s_}_{w_}", [D, 128], F32) for w_ in range(5)] for s_ in range(2)]
    kaugL = [din(f"kaug{s_}", [6, S], BF16) for s_ in range(2)]
    qaugL = [din(f"qaug{s_}", [6, S], BF16) for s_ in range(2)]
    maskd = din("maskd", [128, 4, 512], F32)
    lamv = din("lamv", [1, 4, 64], F32)
    subln = din("subln", [128, 1], F32)
    gains = din("gains", [128, 8, 8], F32)
    idxtab = din("idxtab", [128, NL * 8], I32)
    w_o = din("w_o", [128, 8, D], F32)
    w_upL = [din(f"w_up{l}", [128, 8, 2 * FF], F32) for l in range(2)]
    w_dnL = [din(f"w_dn{l}", [128, NFC, D], F32) for l in range(2)]
    convwL = [din(f"convw{l}", [128, NFC, 3], F32) for l in range(2)]
    convbL = [din(f"convb{l}", [128, NFC], F32) for l in range(2)]
    w_in = din("w_in", [128, 8, 2 * E], F32)
    lng = din("lng", [128, E], F32)
    lnb = din("lnb", [128, E], F32)
    wsT = din("wsT", [128, 8, 128], F32)
    tril = din("tril", [128, 128], F32)
    bsb = din("bsb", [128, 8, 128], F32)
    w_out = din("w_out", [128, NEC, D], F32)
    yT = nc.dram_tensor("yT", [D, T], F32, kind="ExternalOutput")

    ag_in = nc.dram_tensor("ag_in", [2 * NT * 128, 512], BF16)
    ag_out = nc.dram_tensor("ag_out", [4 * 2 * NT * 128, 512], BF16)
    xmid0 = nc.dram_tensor("xmid0", [D, TL], F32)
    x1s = nc.dram_tensor("x1s", [D, TL], F32)
    xmid1 = nc.dram_tensor("xmid1", [D, TL], F32)

    def fm(t):
        return t.ap().rearrange("(c p) t -> p c t", p=128)

    with contextlib.ExitStack() as top:
        _uid = iter(range(1 << 30))
        sb = lambda st, n, s, d: st.enter_context(nc.sbuf_tensor(f"{n}_u{next(_uid)}", s, d))
        ps = top.enter_context(nc.psum_tensor("ps", [128, 8 * 512], F32))
        PB = [Buf(f"ps{i}") for i in range(8)]

        def newP(tag):
            for b_ in PB:
                b_.lw = None
                b_.rs = []
            return Prog(nc, tag, top)

        def bank(i, rows=128, cols=512, c0=0):
            return ps[0:rows, i * 512 + c0:i * 512 + c0 + cols]

        ones_m = sb(top, "ones_m", [128, 128], BF16)
        ones_v = sb(top, "ones_v", [128, 128], BF16)
        ones_f = sb(top, "ones_f", [128, 128], F32)
        ones_b = sb(top, "ones_b", [128, 128], BF16)
        gn = sb(top, "gn", [128, 8, 8], F32)
        neglam = sb(top, "neglam", [128, 1], F32)
        subg = sb(top, "subg", [128, 1], F32)
        B_const = Buf("const")

        with contextlib.ExitStack() as st:
            P = newP("c0")
            lv = sb(st, "lv", [1, 4, 64], F32)
            pr = sb(st, "pr", [1, 2, 64], F32)
            sm = sb(st, "sm", [1, 2], F32)
            ex = sb(st, "ex", [1, 2], F32)
            nl = sb(st, "nl", [1, 1], F32)
            b_lv, b_pr, b_sm, b_ex, b_nl = (Buf(n) for n in ("lv", "pr", "sm", "ex", "nl"))
            b_gn, b_sub, b_ones = Buf("gn"), Buf("sub"), Buf("ones")
            P.add("sp", lambda e: e.dma_start(out=lv[:, :, :], in_=lamv.ap()), writes=[b_lv], dma=True, key="c")
            P.add("sp", lambda e: e.dma_start(out=gn[:, :, :], in_=gains.ap()), writes=[b_gn], dma=True, key="c")
            P.add("sp", lambda e: e.dma_start(out=subg[:, :], in_=subln.ap()), writes=[b_sub], dma=True, key="c")
            P.add("pool", lambda e: e.memset(ones_m[:, :], 1.0 / 1024.0), writes=[b_ones])
            P.add("pool", lambda e: e.memset(ones_v[:, :], 1.0 / 128.0), writes=[b_ones])
            P.add("pool", lambda e: e.memset(ones_f[:, :], 1.0), writes=[b_ones])
            P.add("pool", lambda e: e.memset(ones_b[:, :], 1.0), writes=[b_ones])
            P.add("dve", lambda e: e.tensor_tensor(out=pr[:, 0, :], in0=lv[:, 0, :], in1=lv[:, 1, :], op=ALU.mult),
                  reads=[b_lv], writes=[b_pr])
            P.add("dve", lambda e: e.tensor_tensor(out=pr[:, 1, :], in0=lv[:, 2, :], in1=lv[:, 3, :], op=ALU.mult),
                  reads=[b_lv], writes=[b_pr])
            P.add("dve", lambda e: e.reduce_sum(out=sm[:, 0:1], in_=pr[:, 0, :], axis=AX.X), reads=[b_pr], writes=[b_sm])
            P.add("dve", lambda e: e.reduce_sum(out=sm[:, 1:2], in_=pr[:, 1, :], axis=AX.X), reads=[b_pr], writes=[b_sm])
            P.add("act", lambda e: e.activation(out=ex[:, :], in_=sm[:, :], func=AF.Exp), reads=[b_sm], writes=[b_ex])
            P.add("dve", lambda e: e.scalar_tensor_tensor(out=nl[:, :], in0=ex[:, 1:2], scalar=-LAM_INIT, in1=ex[:, 0:1],
                                                         op0=ALU.add, op1=ALU.subtract), reads=[b_ex], writes=[b_nl])
            P.add("pe", lambda e: e.matmul(bank(0, 128, 1), lhsT=ones_f[0:1, :], rhs=nl[0:1, 0:1], start=True, stop=True),
                  reads=[b_nl, b_ones], writes=[PB[0]])
            P.add("dve", lambda e: e.tensor_copy(out=neglam[:, :], in_=bank(0, 128, 1)), reads=[PB[0]], writes=[B_const])
            P.add("dve", lambda e: e.tensor_scalar(out=subg[:, :], in0=subg[:, :], scalar1=1.0 - LAM_INIT, scalar2=None,
                                                   op0=ALU.mult), reads=[b_sub], writes=[b_sub])
            P.emit()

        def rstd_from_sq(P, sq_ap_fn, nch, ones, bankid, sq_buf, tmp, rstd, b_tmp, b_rstd, eps):
            for c in range(nch):
                P.add("pe", (lambda c: lambda e: e.matmul(bank(bankid), lhsT=ones[:, :], rhs=sq_ap_fn(c),
                                                            start=(c == 0), stop=(c == nch - 1)))(c),
                      reads=[sq_buf], writes=[PB[bankid]])
            P.add("act", lambda e: e.activation(out=tmp[:, :], in_=bank(bankid), func=AF.Sqrt, bias=eps, scale=1.0),
                  reads=[PB[bankid]], writes=[b_tmp])
            P.add("dve", lambda e: e.reciprocal(out=rstd[:, :], in_=tmp[:, :]), reads=[b_tmp], writes=[b_rstd])

        for slot in range(2):
            if _STOP < 1 + slot:
                return nc
            with contextlib.ExitStack() as st:
                KT = [sb(st, f"KT{m}", [70, S], BF16) for m in range(2)]
                Vt = sb(st, "Vt", [128, S], BF16)
                wq = [sb(st, f"wq{m}", [128, 8, 128], BF16) for m in range(2)]
                wk = [sb(st, f"wk{m}", [128, 8, 128], BF16) for m in range(2)]
                wv = sb(st, "wv", [128, 8, 128], BF16)
                stg = sb(st, "stg", [128, 8, 128], F32)
                xt = [sb(st, "xt0", [128, 8, 512], F32)] * 2
                xsq = sb(st, "xsq", [128, 8, 512], BF16)
                xn = [sb(st, f"xn{i}", [128, 8, 512], BF16) for i in range(2)]
                rtmp = sb(st, "rtmp", [128, 512], F32)
                rstd = sb(st, "rstd", [128, 512], F32)
                QT = [[sb(st, f"QT{m}_{i}", [70, 512], BF16) for i in range(2)] for m in range(2)]
                PT = [sb(st, f"PT{i}", [128, 1024], BF16) for i in range(3)]
                tmpS = [sb(st, f"tmpS{i}", [128, 1024], F32) for i in range(2)]
                acc = [sb(st, f"acc{m}", [128, 512], F32) for m in range(2)]
                mk = sb(st, "mk", [128, 4, 512], F32)
                rr = [sb(st, f"rr{m}", [128, 512], F32) for m in range(2)]
                tt = [sb(st, f"tt{m}", [128, 512], F32) for m in range(2)]
                dd = sb(st, "dd", [128, 512], F32)
                dsq = sb(st, "dsq", [128, 512], BF16)
                r2t = sb(st, "r2t", [128, 512], F32)
                r2 = sb(st, "r2", [128, 512], F32)
                oT = [sb(st, f"oT{i}", [128, 512], BF16) for i in range(2)]

                P = newP(f"a{slot}")
                b_KT = [Buf(f"KT{m}") for m in range(2)]
                b_KTa = Buf("KTaug")
                b_V = Buf("V")
                b_w = Buf("w")
                b_stg = Buf("stg")
                b_xt = [Buf("xt0")] * 2
                b_xsq = Buf("xsq")
                b_xn = [Buf(f"xn{i}") for i in range(2)]
                b_rtmp, b_rstd = Buf("rtmp"), Buf("rstd")
                b_QT = [[Buf(f"QT{m}_{i}") for i in range(2)] for m in range(2)]
                b_QTa = [[Buf(f"QTa{m}_{i}") for i in range(2)] for m in range(2)]
                b_PT = [Buf(f"PT{i}") for i in range(3)]
                b_tmpS = [Buf(f"tmpS{i}") for i in range(2)]
                b_acc = [Buf(f"acc{m}") for m in range(2)]
                b_mk = Buf("mk")
                b_rr = [Buf(f"rr{m}") for m in range(2)]
                b_tt = [Buf(f"tt{m}") for m in range(2)]
                b_dd, b_dsq, b_r2t, b_r2 = Buf("dd"), Buf("dsq"), Buf("r2t"), Buf("r2")
                b_oT = [Buf(f"oT{i}") for i in range(2)]

                dests = [wq[0], wq[1], wk[0], wk[1], wv]
                for wi in range(5):
                    P.add("sp", (lambda wi: lambda e: e.dma_start(
                        out=stg[:, :, :], in_=wqkvL[slot][wi].ap().rearrange("(c p) n -> p c n", p=128)))(wi),
                        writes=[b_stg], dma=True, key="ld")
                    for c in range(8):
                        sc2 = 0.125 if wi < 2 else 1.0
                        P.add("dve", (lambda wi, c, sc2: lambda e: e.tensor_scalar(
                            out=dests[wi][:, c, :], in0=stg[:, c, :], scalar1=gn[:, 0, c:c + 1], scalar2=sc2,
                            op0=ALU.mult, op1=ALU.mult))(wi, c, sc2), reads=[b_stg], writes=[b_w])
                for m in range(2):
                    P.add("sp", (lambda m: lambda e: e.dma_start(out=KT[m][64:70, :], in_=kaugL[slot].ap()))(m),
                          writes=[b_KTa], dma=True, key="ld")
                P.add("sp", lambda e: e.dma_start(out=mk[:, :, :], in_=maskd.ap()), writes=[b_mk], dma=True, key="ld")

                def prep(g):
                    i = g % 2
                    P.add("sp", lambda e: e.dma_start(out=xt[i][:, :, :], in_=fm(xT_full)[:, :, g * 512:(g + 1) * 512]),
                          writes=[b_xt[i]], dma=True)
                    P.add("act", lambda e: e.activation(out=xsq[:, :, :], in_=xt[i][:, :, :], func=AF.Square),
                          reads=[b_xt[i]], writes=[b_xsq])
                    rstd_from_sq(P, lambda c: xsq[:, c, :], 8, ones_m, 7, b_xsq, rtmp, rstd, b_rtmp, b_rstd, NORM_EPS)
                    for c in range(8):
                        P.add("dve", (lambda c: lambda e: e.tensor_tensor(out=xn[i][:, c, :], in0=xt[i][:, c, :],
                                                                         in1=rstd[:, :], op=ALU.mult))(c),
                              reads=[b_xt[i], b_rstd], writes=[b_xn[i]])
                    for m in range(2):
                        P.add("sp", (lambda m: lambda e: e.dma_start(out=QT[m][i][64:70, :],
                                                                      in_=qaugL[slot].ap()[:, g * 512:(g + 1) * 512]))(m),
                              writes=[b_QTa[m][i]], dma=True, key=f"qa{i}")

                def qkv_part(g, part):
                    i = g % 2
                    b = 7
                    if part < 4:
                        m, kind = part % 2, part // 2
                        w = wq[m] if kind == 0 else wk[m]
                        for c in range(8):
                            P.add("pe", (lambda w, c: lambda e: e.matmul(bank(b), lhsT=w[:, c, :], rhs=xn[i][:, c, :],
                                                                          start=(c == 0), stop=(c == 7)))(w, c),
                                  reads=[b_w, b_xn[i]], writes=[PB[b]])
                        if kind == 0:
                            P.add("act", (lambda m: lambda e: e.activation(out=QT[m][i][0:64, :], in_=bank(b, 64), func=AF.Copy))(m),
                                  reads=[PB[b]], writes=[b_QT[m][i]])
                        else:
                            P.add("act", (lambda m: lambda e: e.activation(
                                out=KT[m][0:64, g * 512:(g + 1) * 512], in_=bank(b, 64), func=AF.Copy))(m),
                                reads=[PB[b]], writes=[b_KT[m]])
                    else:
                        for j in range(4):
                            for c in range(8):
                                P.add("pe", (lambda j, c: lambda e: e.matmul(
                                    bank(b, 128, 128, j * 128), lhsT=xn[i][:, c, j * 128:(j + 1) * 128], rhs=wv[:, c, :],
                                    start=(c == 0), stop=(c == 7)))(j, c),
                                    reads=[b_w, b_xn[i]], writes=[PB[b]])
                        P.add("act", lambda e: e.activation(out=Vt[:, g * 512:(g + 1) * 512], in_=bank(b), func=AF.Copy),
                              reads=[PB[b]], writes=[b_V])

                ucount = [0]

                def attention(g, hooks):
                    i = g % 2
                    kb0 = max(0, 4 * g - WIN_A) if slot == 0 else 0
                    kbl = 4 * g + 3
                    kbs = list(range(kb0, kbl + 1))
                    us = []
                    for _ in kbs:
                        us.append(ucount[0])
                        ucount[0] += 1

                    def emit_qk(kb, u):
                        pb = u % 2
                        for m in range(2):
                            P.add("pe", (lambda m, kb, pb: lambda e: e.matmul(
                                bank(2 * pb + m), lhsT=KT[m][0:70, kb * 128:(kb + 1) * 128], rhs=QT[m][i][0:70, :],
                                start=True, stop=True))(m, kb, pb),
                                reads=[b_KT[m], b_KTa, b_QT[m][i], b_QTa[m][i]], writes=[PB[2 * pb + m]])

                    def emit_exp(kb, u):
                        pb = u % 2
                        pt = u % 3
                        if kb >= 4 * g:
                            v = kb - 4 * g
                            for m in range(2):
                                P.add("dve", (lambda m, pb, v: lambda e: e.tensor_tensor(
                                    out=tmpS[pb][:, m * 512:(m + 1) * 512], in0=bank(2 * pb + m), in1=mk[:, v, :],
                                    op=ALU.add))(m, pb, v),
                                    reads=[PB[2 * pb + m], b_mk], writes=[b_tmpS[pb]])
                            P.add("act", (lambda pb, pt: lambda e: e.activation(out=PT[pt][:, :], in_=tmpS[pb][:, :], func=AF.Exp))(pb, pt),
                                  reads=[b_tmpS[pb]], writes=[b_PT[pt]])
                        else:
                            W_ = 512 if _ABL == 4 else 1024
                            P.add("act", (lambda pb, pt, W_: lambda e: e.activation(
                                out=PT[pt][:, 0:W_], in_=ps[:, 2 * pb * 512:2 * pb * 512 + W_], func=AF.Exp))(pb, pt, W_),
                                reads=[PB[2 * pb], PB[2 * pb + 1]], writes=[b_PT[pt]])

                    def emit_pv(kb, u):
                        pt = u % 3
                        for m in range(2):
                            if _ABL == 5 and kb != kb0 and kb != kbl:
                                continue
                            P.add("pe", (lambda m, kb, pt: lambda e: e.matmul(
                                bank(4 + m), lhsT=Vt[:, kb * 128:(kb + 1) * 128], rhs=PT[pt][:, m * 512:(m + 1) * 512],
                                start=(kb == kb0), stop=(kb == kbl)))(m, kb, pt),
                                reads=[b_V, b_PT[pt]], writes=[PB[4 + m]])
                        if _ABL in (1, 3) and kb != kb0:
                            pass
                        elif kb == kb0:
                            P.add("dve", (lambda pt: lambda e: e.tensor_copy(out=acc[0][:, :], in_=PT[pt][:, 0:512]))(pt),
                                  reads=[b_PT[pt]], writes=[b_acc[0]])
                        else:
                            P.add("dve", (lambda pt: lambda e: e.tensor_tensor(
                                out=acc[0][:, :], in0=acc[0][:, :], in1=PT[pt][:, 0:512], op=ALU.add))(pt),
                                reads=[b_PT[pt], b_acc[0]], writes=[b_acc[0]])
                        if not (_ABL in (2, 3) and kb != kb0):
                            P.add("pe", (lambda kb, pt: lambda e: e.matmul(
                                bank(6), lhsT=ones_b[:, :], rhs=PT[pt][:, 512:1024], start=(kb == kb0), stop=(kb == kbl or _ABL in (2, 3))))(kb, pt),
                                reads=[b_PT[pt]], writes=[PB[6]])

                    emit_qk(kbs[0], us[0])
                    for n_, kb in enumerate(kbs):
                        if n_ + 1 < len(kbs):
                            emit_qk(kbs[n_ + 1], us[n_ + 1])
                        emit_exp(kb, us[n_])
                        emit_pv(kb, us[n_])
                        if n_ >= 1 and hooks:
                            hooks.pop(0)()
                    while hooks:
                        hooks.pop(0)()

                def finalize1(g):
                    P.add("pe", lambda e: e.matmul(bank(7), lhsT=ones_f[:, :], rhs=acc[0][:, :], start=True, stop=True),
                          reads=[b_acc[0]], writes=[PB[7]])
                    for m in range(2):
                        P.add("act", (lambda m: lambda e: e.activation(out=tt[m][:, :], in_=bank(4 + m), func=AF.Copy))(m),
                              reads=[PB[4 + m]], writes=[b_tt[m]])
                    for m in (1, 0):
                        P.add("dve", (lambda m: lambda e: e.reciprocal(out=rr[m][:, :], in_=bank(7 - m)))(m),
                              reads=[PB[7 - m]], writes=[b_rr[m]])

                def finalize2(g):
                    i = g % 2
                    for m in range(2):
                        P.add("dve", (lambda m: lambda e: e.tensor_tensor(out=tt[m][:, :], in0=tt[m][:, :], in1=rr[m][:, :],
                                                                         op=ALU.mult))(m),
                              reads=[b_tt[m], b_rr[m]], writes=[b_tt[m]])
                    P.add("dve", lambda e: e.scalar_tensor_tensor(out=dd[:, :], in0=tt[1][:, :], scalar=neglam[:, 0:1],
                                                                 in1=tt[0][:, :], op0=ALU.mult, op1=ALU.add),
                          reads=[b_tt[0], b_tt[1]], writes=[b_dd])
                    P.add("act", lambda e: e.activation(out=dsq[:, :], in_=dd[:, :], func=AF.Square), reads=[b_dd], writes=[b_dsq])
                    rstd_from_sq(P, lambda c: dsq[:, :], 1, ones_v, 7, b_dsq, r2t, r2, b_r2t, b_r2, NORM_EPS)
                    P.add("dve", lambda e: e.scalar_tensor_tensor(out=oT[i][:, :], in0=dd[:, :], scalar=subg[:, 0:1],
                                                                 in1=r2[:, :], op0=ALU.mult, op1=ALU.mult),
                          reads=[b_dd, b_r2], writes=[b_oT[i]])
                    row0 = (slot * NT + g) * 128
                    P.add("sp", lambda e: e.dma_start(out=ag_in.ap()[row0:row0 + 128, :], in_=oT[i][:, :]),
                          reads=[b_oT[i]], dma=True, key=f"st_oT{i}")

                prep(0)
                for part in range(5):
                    qkv_part(0, part)
                for g in range(NT):
                    hooks = []
                    if g >= 1:
                        hooks.append((lambda g: lambda: finalize2(g - 1))(g))
                    if g + 1 < NT:
                        hooks.append((lambda g: lambda: prep(g + 1))(g))
                        for part in range(5):
                            hooks.append((lambda g, part: lambda: qkv_part(g + 1, part))(g, part))
                    attention(g, hooks)
                    finalize1(g)
                finalize2(NT - 1)
                P.emit()

        if _STOP < 3:
            return nc
        if True:
            csem = top.enter_context(nc.semaphore("csem"))
            with nc.Block() as block:
                @block.gpsimd
                def _(g):
                    RP = 512
                    for k in range(2 * NT * 128 // RP):
                        g.collective_compute("AllGather", ALU.bypass, replica_groups=[[0, 1, 2, 3], [4, 5, 6, 7]],
                                             ins=[ag_in.ap()[k * RP:(k + 1) * RP, :].opt()],
                                             outs=[ag_out.ap()[k * 4 * RP:(k + 1) * 4 * RP, :].opt()]).then_inc(csem)
                        g.wait_ge(csem, k + 1)

        if _STOP < 4:
            return nc
        def postnorm_residual(P, msb, b_msb, msq, b_msq, rt, rs, b_rt, b_rs, gvec, resid_fn, b_resid, out_fn, b_out, tmp, b_tmp2):
            rstd_from_sq(P, lambda c: msq[:, c, :], 8, ones_m, 7, b_msq, rt, rs, b_rt, b_rs, NORM_EPS)
            for dc in range(8):
                P.add("dve", (lambda dc: lambda e: e.scalar_tensor_tensor(
                    out=tmp[:, :], in0=msb[:, dc, :], scalar=gn[:, gvec, dc:dc + 1], in1=rs[:, :],
                    op0=ALU.mult, op1=ALU.mult))(dc), reads=[b_msb, b_rs], writes=[b_tmp2])
                P.add("dve", (lambda dc: lambda e: e.tensor_tensor(out=out_fn(dc), in0=tmp[:, :], in1=resid_fn(dc), op=ALU.add))(dc),
                      reads=[b_tmp2, b_resid], writes=[b_out])

        with contextlib.ExitStack() as st:
            wo = sb(st, "wo", [128, 8, D], BF16)
            b_wo = Buf("wo")
            with contextlib.ExitStack() as s2:
                stg = sb(s2, "stgo", [128, 8, D], F32)
                b_stg = Buf("stg")
                P = newP("wo0")
                P.add("sp", lambda e: e.dma_start(out=stg[:, :, :], in_=w_o.ap()), writes=[b_stg], dma=True)
                for c in range(8):
                    P.add("dve", (lambda c: lambda e: e.tensor_copy(out=wo[:, c, :], in_=stg[:, c, :]))(c),
                          reads=[b_stg], writes=[b_wo])
                P.emit()
            idx = sb(st, "idx", [128, NL * 8], I32)
            og = sb(st, "og", [128, 8, 512], BF16)
            xt = sb(st, "xtw", [128, 8, 512], F32)
            msb = sb(st, "msb", [128, 8, 512], F32)
            msq = sb(st, "msq", [128, 8, 512], BF16)
            rt = sb(st, "rtw", [128, 512], F32)
            rs = sb(st, "rsw", [128, 512], F32)
            tmp = sb(st, "tmpw", [128, 512], F32)
            b_idx, b_og, b_xt, b_msb, b_msq, b_rt, b_rs, b_tmp = (Buf(n) for n in ("idx", "og", "xt", "msb", "msq", "rt", "rs", "tmp"))
            P = newP("wo1")
            P.add("sp", lambda e: e.dma_start(out=idx[:, :], in_=idxtab.ap()), writes=[b_idx], dma=True, key="ld")
            for n in range(NL):
                for hd in range(8):
                    P.add("pool", (lambda n, hd: lambda e: e.indirect_dma_start(
                        out=og[:, hd, :], out_offset=None, in_=ag_out.ap(),
                        in_offset=bass.IndirectOffsetOnAxis(ap=idx[:, n * 8 + hd:n * 8 + hd + 1], axis=0)))(n, hd),
                        reads=[b_idx], writes=[b_og], dma=True, key="og")
                P.add("sp", (lambda n: lambda e: e.dma_start(out=xt[:, :, :], in_=fm(xT_loc)[:, :, n * 512:(n + 1) * 512]))(n),
                      writes=[b_xt], dma=True)
                for dc in range(8):
                    b = dc % 2
                    for hd in range(8):
                        P.add("pe", (lambda b, dc, hd: lambda e: e.matmul(bank(b), lhsT=wo[:, hd, dc * 128:(dc + 1) * 128],
                                                                          rhs=og[:, hd, :], start=(hd == 0), stop=(hd == 7)))(b, dc, hd),
                              reads=[b_og], writes=[PB[b]])
                    P.add("act", (lambda b, dc: lambda e: e.activation(out=msb[:, dc, :], in_=bank(b), func=AF.Copy))(b, dc),
                          reads=[PB[b]], writes=[b_msb])
                    P.add("act", (lambda b, dc: lambda e: e.activation(out=msq[:, dc, :], in_=bank(b), func=AF.Square))(b, dc),
                          reads=[PB[b]], writes=[b_msq])
                postnorm_residual(P, msb, b_msb, msq, b_msq, rt, rs, b_rt, b_rs, 1, lambda dc: xt[:, dc, :], b_xt,
                                  lambda dc: msb[:, dc, :], b_msb, tmp, b_tmp)
                P.add("sp", (lambda n: lambda e: e.dma_start(out=fm(xmid0)[:, :, n * 512:(n + 1) * 512], in_=msb[:, :, :]))(n),
                      reads=[b_msb], dma=True, key="st_msb")
            P.emit()

        if _STOP < 5:
            return nc
        def ffn_phase(layer, src, dst, dst_is_out):
            gpre, gpost = (2, 3) if layer == 0 else (6, 7)
            with contextlib.ExitStack() as st:
                wup = sb(st, "wup", [128, 8, 2 * FF], BF16)
                wdn = sb(st, "wdn", [128, NFC, D], BF16)
                cw = sb(st, "cw", [128, NFC, 3], F32)
                cb = sb(st, "cb", [128, NFC], F32)
                vl0 = sb(st, "vl0", [128, 512], F32)
                b_w = Buf("w")
                with contextlib.ExitStack() as s2:
                    stg = [sb(s2, f"stgf{i}", [128, 2 * FF], F32) for i in range(2)]
                    b_stg = [Buf(f"stg{i}") for i in range(2)]
                    P = newP(f"fw{layer}")
                    P.add("sp", lambda e: e.dma_start(out=cw[:, :, :], in_=convwL[layer].ap()), writes=[b_w], dma=True, key="misc")
                    P.add("sp", lambda e: e.dma_start(out=cb[:, :], in_=convbL[layer].ap()), writes=[b_w], dma=True, key="misc")
                    P.add("sp", lambda e: e.dma_start(out=vl0[:, :], in_=valid0.ap()), writes=[b_w], dma=True, key="misc")
                    for c in range(8):
                        i = c % 2
                        P.add("sp", (lambda c, i: lambda e: e.dma_start(out=stg[i][:, :], in_=w_upL[layer].ap()[:, c, :]))(c, i),
                              writes=[b_stg[i]], dma=True)
                        P.add("dve", (lambda c, i: lambda e: e.tensor_scalar(out=wup[:, c, :], in0=stg[i][:, :],
                                                                            scalar1=gn[:, gpre, c:c + 1], scalar2=None, op0=ALU.mult))(c, i),
                              reads=[b_stg[i]], writes=[b_w])
                    for q in range(5):
                        f0, f1 = q * 5, min(NFC, q * 5 + 5)
                        nf = f1 - f0
                        i = q % 2
                        P.add("sp", (lambda f0, f1, nf, i: lambda e: e.dma_start(
                            out=stg[i][:, 0:nf * D].rearrange("p (f n) -> p f n", n=D), in_=w_dnL[layer].ap()[:, f0:f1, :]))(f0, f1, nf, i),
                            writes=[b_stg[i]], dma=True)
                        P.add("pool", (lambda f0, f1, nf, i: lambda e: e.tensor_copy(
                            out=wdn[:, f0:f1, :], in_=stg[i][:, 0:nf * D].rearrange("p (f n) -> p f n", n=D)))(f0, f1, nf, i),
                            reads=[b_stg[i]], writes=[b_w])
                    P.emit()
                xf = sb(st, "xf", [128, 8, 512], F32)
                xn = sb(st, "xnf", [128, 8, 512], BF16)
                hh = sb(st, "hh", [128, NFC, 512], BF16)
                abufL = [sb(st, f"abuf{i_}", [128, 514], F32) for i_ in range(2)]
                acL = [sb(st, f"ac{i_}", [128, 512], F32) for i_ in range(2)]
                geL = [sb(st, f"ge{i_}", [128, 512], F32) for i_ in range(2)]
                xr = [sb(st, f"xr{i}", [128, 512], F32) for i in range(2)]
                rt = sb(st, "rtf", [128, 512], F32)
                rs = sb(st, "rsf", [128, 512], F32)
                tmp = sb(st, "tmpf", [128, 512], F32)
                ahalo = sb(st, "ahalo", [128, NFC, 2], F32)
                b_xf, b_xn, b_hh = (Buf(n) for n in ("xf", "xn", "hh"))
                b_abufL = [Buf(f"abuf{i_}") for i_ in range(2)]
                b_acL = [Buf(f"ac{i_}") for i_ in range(2)]
                b_geL = [Buf(f"ge{i_}") for i_ in range(2)]
                b_xr = [Buf(f"xr{i}") for i in range(2)]
                b_rt, b_rs, b_tmp, b_ah = Buf("rt"), Buf("rs"), Buf("tmp"), Buf("ahalo")
                P = newP(f"ff{layer}")
                P.add("pool", lambda e: e.memset(ahalo[:, :, :], 0.0), writes=[b_ah])
                for n in range(NL):
                    P.add("sp", (lambda n: lambda e: e.dma_start(out=xf[:, :, :], in_=fm(src)[:, :, n * 512:(n + 1) * 512]))(n),
                          writes=[b_xf], dma=True)
                    P.add("act", lambda e: e.activation(out=xn[:, :, :], in_=xf[:, :, :], func=AF.Square), reads=[b_xf], writes=[b_xn])
                    rstd_from_sq(P, lambda c: xn[:, c, :], 8, ones_m, 7, b_xn, rt, rs, b_rt, b_rs, NORM_EPS)
                    if n == 0:
                        P.add("dve", lambda e: e.tensor_tensor(out=rs[:, :], in0=rs[:, :], in1=vl0[:, :], op=ALU.mult),
                              reads=[b_rs], writes=[b_rs])
                    for c in range(8):
                        P.add("dve", (lambda c: lambda e: e.tensor_tensor(out=xn[:, c, :], in0=xf[:, c, :], in1=rs[:, :], op=ALU.mult))(c),
                              reads=[b_xf, b_rs], writes=[b_xn])
                    for fc in range(NFC):
                        ba = (2 * fc) % 4
                        bg = ba + 1
                        abuf, ac, ge = abufL[fc % 2], acL[fc % 2], geL[fc % 2]
                        b_abuf, b_ac, b_ge = b_abufL[fc % 2], b_acL[fc % 2], b_geL[fc % 2]
                        for (b, col0) in ((ba, fc * 128), (bg, FF + fc * 128)):
                            for c in range(8):
                                P.add("pe", (lambda b, col0, c: lambda e: e.matmul(bank(b), lhsT=wup[:, c, col0:col0 + 128],
                                                                                   rhs=xn[:, c, :], start=(c == 0), stop=(c == 7)))(b, col0, c),
                                      reads=[b_xn], writes=[PB[b]])
                        P.add("pool", (lambda fc, abuf: lambda e: e.tensor_copy(out=abuf[:, 0:2], in_=ahalo[:, fc, :]))(fc, abuf),
                              reads=[b_ah], writes=[b_abuf])
                        P.add("act", (lambda ba, abuf: lambda e: e.activation(out=abuf[:, 2:514], in_=bank(ba), func=AF.Copy))(ba, abuf),
                              reads=[PB[ba]], writes=[b_abuf])
                        P.add("pool", (lambda fc, abuf: lambda e: e.tensor_copy(out=ahalo[:, fc, :], in_=abuf[:, 512:514]))(fc, abuf),
                              reads=[b_abuf], writes=[b_ah])
                        P.add("dve", (lambda fc, abuf, ac: lambda e: e.tensor_scalar(out=ac[:, :], in0=abuf[:, 2:514], scalar1=cw[:, fc, 2:3],
                                                                          scalar2=cb[:, fc:fc + 1], op0=ALU.mult, op1=ALU.add))(fc, abuf, ac),
                              reads=[b_abuf], writes=[b_ac])
                        P.add("dve", (lambda fc, abuf, ac: lambda e: e.scalar_tensor_tensor(out=ac[:, :], in0=abuf[:, 1:513], scalar=cw[:, fc, 1:2],
                                                                                 in1=ac[:, :], op0=ALU.mult, op1=ALU.add))(fc, abuf, ac),
                              reads=[b_abuf, b_ac], writes=[b_ac])
                        P.add("dve", (lambda fc, abuf, ac: lambda e: e.scalar_tensor_tensor(out=ac[:, :], in0=abuf[:, 0:512], scalar=cw[:, fc, 0:1],
                                                                                 in1=ac[:, :], op0=ALU.mult, op1=ALU.add))(fc, abuf, ac),
                              reads=[b_abuf, b_ac], writes=[b_ac])
                        P.add("act", (lambda ac, ge: lambda e: e.activation(out=ge[:, :], in_=ac[:, :], func=AF.Gelu_apprx_tanh))(ac, ge), reads=[b_ac], writes=[b_ge])
                        P.add("dve", (lambda fc, bg, ge: lambda e: e.tensor_tensor(out=hh[:, fc, :], in0=bank(bg), in1=ge[:, :], op=ALU.mult))(fc, bg, ge),
                              reads=[PB[bg], b_ge], writes=[b_hh])
                    if layer == 1 and n == 0 and _SKIP0:
                        continue
                    for dc in range(8):
                        b = 4 + dc % 2
                        for fc in range(NFC):
                            P.add("pe", (lambda b, dc, fc: lambda e: e.matmul(bank(b), lhsT=wdn[:, fc, dc * 128:(dc + 1) * 128],
                                                                              rhs=hh[:, fc, :], start=(fc == 0), stop=(fc == NFC - 1)))(b, dc, fc),
                                  reads=[b_hh], writes=[PB[b]])
                        P.add("act", (lambda b, dc: lambda e: e.activation(out=xf[:, dc, :], in_=bank(b), func=AF.Copy))(b, dc),
                              reads=[PB[b]], writes=[b_xf])
                        P.add("act", (lambda b, dc: lambda e: e.activation(out=xn[:, dc, :], in_=bank(b), func=AF.Square))(b, dc),
                              reads=[PB[b]], writes=[b_xn])
                    rstd_from_sq(P, lambda c: xn[:, c, :], 8, ones_m, 7, b_xn, rt, rs, b_rt, b_rs, NORM_EPS)
                    for dc in range(8):
                        i = dc % 2
                        P.add("sp", (lambda n, dc, i: lambda e: e.dma_start(
                            out=xr[i][:, :], in_=src.ap()[dc * 128:(dc + 1) * 128, n * 512:(n + 1) * 512]))(n, dc, i),
                            writes=[b_xr[i]], dma=True)
                        P.add("dve", (lambda dc: lambda e: e.scalar_tensor_tensor(
                            out=tmp[:, :], in0=xf[:, dc, :], scalar=gn[:, gpost, dc:dc + 1], in1=rs[:, :],
                            op0=ALU.mult, op1=ALU.mult))(dc), reads=[b_xf, b_rs], writes=[b_tmp])
                        P.add("dve", (lambda dc, i: lambda e: e.tensor_tensor(out=xf[:, dc, :], in0=tmp[:, :], in1=xr[i][:, :], op=ALU.add))(dc, i),
                              reads=[b_tmp, b_xr[i]], writes=[b_xf])
                    if dst_is_out and _OUTINT:
                        pass
                    elif dst_is_out:
                        P.add("sp", (lambda n: lambda e: e.dma_start(out=fm(dst)[:, :, (n - 1) * 512:n * 512], in_=xf[:, :, :]))(n),
                              reads=[b_xf], dma=True, key="st_xf")
                    else:
                        P.add("sp", (lambda n: lambda e: e.dma_start(out=fm(dst)[:, :, n * 512:(n + 1) * 512], in_=xf[:, :, :]))(n),
                              reads=[b_xf], dma=True, key="st_xf")
                P.emit()

        ffn_phase(0, xmid0, x1s, False)
        if _TESTF == 1:
            ffn_phase(0, xmid0, x1s, False)
            return nc
        if _TESTF == 2:
            ffn_phase(1, xmid0, x1s, False)
            return nc
        if _STOP < 6:
            return nc

        with contextlib.ExitStack() as st:
            win = sb(st, "win", [128, 8, 2 * E], BF16)
            wout = sb(st, "wout", [128, NEC, D], BF16)
            wsm = sb(st, "wsm", [128, 8, 128], BF16)
            bs = sb(st, "bs", [128, 8, 128], F32)
            lg = sb(st, "lg", [128, E], F32)
            lb = sb(st, "lb", [128, E], F32)
            b_w = Buf("w")
            with contextlib.ExitStack() as s2:
                stg = [sb(s2, f"stgs{i}", [128, 2 * E], F32) for i in range(2)]
                trl = sb(s2, "trl", [128, 128], F32)
                b_stg = [Buf(f"stg{i}") for i in range(2)]
                b_trl = Buf("trl")
                P = newP("sw")
                P.add("sp", lambda e: e.dma_start(out=bs[:, :, :], in_=bsb.ap()), writes=[b_w], dma=True, key="misc")
                P.add("sp", lambda e: e.dma_start(out=lg[:, :], in_=lng.ap()), writes=[b_w], dma=True, key="misc")
                P.add("sp", lambda e: e.dma_start(out=lb[:, :], in_=lnb.ap()), writes=[b_w], dma=True, key="misc")
                P.add("sp", lambda e: e.dma_start(out=trl[:, :], in_=tril.ap()), writes=[b_trl], dma=True, key="misc")
                P.add("sp", lambda e: e.dma_start(out=stg[0][:, 0:1024].rearrange("p (g t) -> p g t", t=128), in_=wsT.ap()),
                      writes=[b_stg[0]], dma=True)
                for g8 in range(8):
                    P.add("dve", (lambda g8: lambda e: e.tensor_tensor(out=wsm[:, g8, :], in0=stg[0][:, g8 * 128:(g8 + 1) * 128],
                                                                      in1=trl[:, :], op=ALU.mult))(g8),
                          reads=[b_stg[0], b_trl], writes=[b_w])
                for c in range(8):
                    i = (c + 1) % 2
                    P.add("sp", (lambda c, i: lambda e: e.dma_start(out=stg[i][:, :], in_=w_in.ap()[:, c, :]))(c, i),
                          writes=[b_stg[i]], dma=True)
                    P.add("dve", (lambda c, i: lambda e: e.tensor_scalar(out=win[:, c, :], in0=stg[i][:, :],
                                                                        scalar1=gn[:, 4, c:c + 1], scalar2=None, op0=ALU.mult))(c, i),
                          reads=[b_stg[i]], writes=[b_w])
                for q in range(4):
                    i = (q + 1) % 2
                    P.add("sp", (lambda q, i: lambda e: e.dma_start(
                        out=stg[i][:, 0:4 * D].rearrange("p (f n) -> p f n", n=D), in_=w_out.ap()[:, q * 4:(q + 1) * 4, :]))(q, i),
                        writes=[b_stg[i]], dma=True)
                    P.add("pool", (lambda q, i: lambda e: e.tensor_copy(
                        out=wout[:, q * 4:(q + 1) * 4, :], in_=stg[i][:, 0:4 * D].rearrange("p (f n) -> p f n", n=D)))(q, i),
                        reads=[b_stg[i]], writes=[b_w])
                P.emit()
            xt = sb(st, "xts", [128, 8, 512], F32)
            xn = sb(st, "xns", [128, 8, 512], BF16)
            uu = sb(st, "uu", [128, NEC, 512], BF16)
            vt = sb(st, "vt", [128, E], F32)
            junk = sb(st, "junk", [128, E], BF16)
            vnb = sb(st, "vnb", [128, E], BF16)
            vsum = sb(st, "vsum", [128, 4], F32)
            st1 = sb(st, "st1", [128, 4], F32)
            msb = sb(st, "msbs", [128, 8, 512], F32)
            rt = sb(st, "rts", [128, 512], F32)
            rs = sb(st, "rss", [128, 512], F32)
            tmp = sb(st, "tmps", [128, 512], F32)
            ts = sb(st, "ts", [128, 128], F32)
            b_xt, b_xn, b_uu, b_vt, b_junk, b_vnb, b_vsum, b_st1, b_msb, b_rt, b_rs, b_tmp, b_ts = (
                Buf(n) for n in ("xt", "xn", "uu", "vt", "junk", "vnb", "vsum", "st1", "msb", "rt", "rs", "tmp", "ts"))
            P = newP("sg")
            for n in range(NL):
                P.add("sp", (lambda n: lambda e: e.dma_start(out=xt[:, :, :], in_=fm(x1s)[:, :, n * 512:(n + 1) * 512]))(n),
                      writes=[b_xt], dma=True)
                P.add("act", lambda e: e.activation(out=xn[:, :, :], in_=xt[:, :, :], func=AF.Square), reads=[b_xt], writes=[b_xn])
                rstd_from_sq(P, lambda c: xn[:, c, :], 8, ones_m, 7, b_xn, rt, rs, b_rt, b_rs, NORM_EPS)
                for c in range(8):
                    P.add("dve", (lambda c: lambda e: e.tensor_tensor(out=xn[:, c, :], in0=xt[:, c, :], in1=rs[:, :], op=ALU.mult))(c),
                          reads=[b_xt, b_rs], writes=[b_xn])
                for uc in range(NEC):
                    b = uc % 2
                    for c in range(8):
                        P.add("pe", (lambda b, uc, c: lambda e: e.matmul(bank(b), lhsT=win[:, c, uc * 128:(uc + 1) * 128], rhs=xn[:, c, :],
                                                                         start=(c == 0), stop=(c == 7)))(b, uc, c),
                              reads=[b_xn], writes=[PB[b]])
                    P.add("act", (lambda b, uc: lambda e: e.activation(out=uu[:, uc, :], in_=bank(b), func=AF.Gelu_apprx_tanh))(b, uc),
                          reads=[PB[b]], writes=[b_uu])
                for j in range(4):
                    for q in range(4):
                        b = 2 + q % 2
                        for c in range(8):
                            P.add("pe", (lambda b, q, c, j: lambda e: e.matmul(
                                bank(b), lhsT=xn[:, c, j * 128:(j + 1) * 128], rhs=win[:, c, E + q * 512:E + (q + 1) * 512],
                                start=(c == 0), stop=(c == 7)))(b, q, c, j),
                                reads=[b_xn], writes=[PB[b]])
                        P.add("act", (lambda b, q: lambda e: e.activation(out=vt[:, q * 512:(q + 1) * 512], in_=bank(b),
                                                                          func=AF.Gelu_apprx_tanh, accum_out=vsum[:, q:q + 1]))(b, q),
                              reads=[PB[b]], writes=[b_vt, b_vsum])
                    P.add("dve", lambda e: e.reduce_sum(out=st1[:, 0:1], in_=vsum[:, :], axis=AX.X), reads=[b_vsum], writes=[b_st1])
                    P.add("dve", lambda e: e.tensor_scalar(out=st1[:, 0:1], in0=st1[:, 0:1], scalar1=-1.0 / E, scalar2=None, op0=ALU.mult),
                          reads=[b_st1], writes=[b_st1])
                    P.add("dve", lambda e: e.tensor_scalar(out=vt[:, :], in0=vt[:, :], scalar1=st1[:, 0:1], scalar2=None, op0=ALU.add),
                          reads=[b_vt, b_st1], writes=[b_vt])
                    P.add("act", lambda e: e.activation(out=junk[:, :], in_=vt[:, :], func=AF.Square, accum_out=st1[:, 1:2]),
                          reads=[b_vt], writes=[b_junk, b_st1])
                    P.add("dve", lambda e: e.tensor_scalar(out=st1[:, 2:3], in0=st1[:, 1:2], scalar1=1.0 / E, scalar2=LN_EPS,
                                                           op0=ALU.mult, op1=ALU.add), reads=[b_st1], writes=[b_st1])
                    P.add("act", lambda e: e.activation(out=st1[:, 3:4], in_=st1[:, 2:3], func=AF.Sqrt), reads=[b_st1], writes=[b_st1])
                    P.add("dve", lambda e: e.reciprocal(out=st1[:, 2:3], in_=st1[:, 3:4]), reads=[b_st1], writes=[b_st1])
                    P.add("dve", lambda e: e.scalar_tensor_tensor(out=vt[:, :], in0=vt[:, :], scalar=st1[:, 2:3], in1=lg[:, :],
                                                                 op0=ALU.mult, op1=ALU.mult), reads=[b_vt, b_st1], writes=[b_vt])
                    P.add("dve", lambda e: e.tensor_tensor(out=vnb[:, :], in0=vt[:, :], in1=lb[:, :], op=ALU.add),
                          reads=[b_vt], writes=[b_vnb])
                    for fc in range(NEC):
                        b = 4 + fc % 2
                        g8 = fc // 2
                        P.add("pe", (lambda b, fc, g8: lambda e: e.matmul(bank(b, 128, 128), lhsT=vnb[:, fc * 128:(fc + 1) * 128],
                                                                          rhs=wsm[:, g8, :], start=True, stop=True))(b, fc, g8),
                              reads=[b_vnb], writes=[PB[b]])
                        P.add("dve", (lambda b, g8: lambda e: e.tensor_tensor(out=ts[:, :], in0=bank(b, 128, 128), in1=bs[:, g8, :], op=ALU.add))(b, g8),
                              reads=[PB[b]], writes=[b_ts])
                        P.add("dve", (lambda fc, j: lambda e: e.tensor_tensor(out=uu[:, fc, j * 128:(j + 1) * 128], in0=ts[:, :],
                                                                             in1=uu[:, fc, j * 128:(j + 1) * 128], op=ALU.mult))(fc, j),
                              reads=[b_ts, b_uu], writes=[b_uu])
                for dc in range(8):
                    b = 6 + dc % 2 if False else dc % 2
                    for fc in range(NEC):
                        P.add("pe", (lambda b, dc, fc: lambda e: e.matmul(bank(b), lhsT=wout[:, fc, dc * 128:(dc + 1) * 128], rhs=uu[:, fc, :],
                                                                          start=(fc == 0), stop=(fc == NEC - 1)))(b, dc, fc),
                              reads=[b_uu], writes=[PB[b]])
                    P.add("act", (lambda b, dc: lambda e: e.activation(out=msb[:, dc, :], in_=bank(b), func=AF.Copy))(b, dc),
                          reads=[PB[b]], writes=[b_msb])
                    P.add("act", (lambda b, dc: lambda e: e.activation(out=xn[:, dc, :], in_=bank(b), func=AF.Square))(b, dc),
                          reads=[PB[b]], writes=[b_xn])
                postnorm_residual(P, msb, b_msb, xn, b_xn, rt, rs, b_rt, b_rs, 5, lambda dc: xt[:, dc, :], b_xt,
                                  lambda dc: msb[:, dc, :], b_msb, tmp, b_tmp)
                P.add("sp", (lambda n: lambda e: e.dma_start(out=fm(xmid1)[:, :, n * 512:(n + 1) * 512], in_=msb[:, :, :]))(n),
                      reads=[b_msb], dma=True, key="st_msb")
            P.emit()

        if _STOP < 7:
            return nc
        if _TESTF == 3:
            ffn_phase(1, xmid0, x1s, False)
            return nc
        if _TESTF == 4:
            ffn_phase(0, xmid1, x1s, False)
            return nc
        ffn_phase(1, xmid1, yT, True)
    return nc


_CACHE = {}


def _prep_inputs(S, inp):
    f32 = np.float32
    T = S // 4
    TL = T + HALO
    NL = TL // 512
    NT = S // 512
    x = np.asarray(inp["x"], f32)
    wqkv_full = np.asarray(inp["attn_w_qkv"], f32)[0]
    pos = np.arange(S)
    c7, a3, b4 = pos // 128, (pos % 128) // 16, pos % 16
    tril = (np.arange(128)[:, None] <= np.arange(128)[None, :]).astype(f32)
    ii = np.arange(128)[:, None, None]
    vv = np.arange(4)[None, :, None]
    jj = np.arange(512)[None, None, :]
    maskd = np.where(jj >= 128 * vv + ii, 0.0, NEG).astype(f32)
    gl = [inp["norm_mix_pre"][0], inp["norm_mix_post"][0], inp["norm_ffn_pre"][0], inp["norm_ffn_post"][0],
          inp["norm_mix_pre"][1], inp["norm_mix_post"][1], inp["norm_ffn_pre"][1], inp["norm_ffn_post"][1]]
    gains = np.ascontiguousarray(np.stack([np.asarray(g, f32).reshape(8, 128).T for g in gl], axis=1))
    lamv = np.stack([inp["attn_lambda_q1"][0], inp["attn_lambda_k1"][0], inp["attn_lambda_q2"][0], inp["attn_lambda_k2"][0]])[None].astype(f32)
    subln = np.asarray(inp["attn_subln"], f32)[0].reshape(128, 1)
    w_o = np.ascontiguousarray(np.asarray(inp["attn_w_o"], f32)[0].reshape(8, 128, D).transpose(1, 0, 2))
    w_up = np.ascontiguousarray(np.asarray(inp["ffn_w_up"], f32).reshape(2, 8, 128, 2 * FF).transpose(0, 2, 1, 3))
    w_dn = np.ascontiguousarray(np.asarray(inp["ffn_w_down"], f32).reshape(2, NFC, 128, D).transpose(0, 2, 1, 3))
    convw = np.ascontiguousarray(np.asarray(inp["ffn_conv_w"], f32).reshape(2, 3, NFC, 128).transpose(0, 3, 2, 1))
    convb = np.ascontiguousarray(np.asarray(inp["ffn_conv_b"], f32).reshape(2, NFC, 128).transpose(0, 2, 1))
    w_in = np.ascontiguousarray(np.asarray(inp["sgu_w_in"], f32)[0].reshape(8, 128, 2 * E).transpose(1, 0, 2))
    lng = np.ascontiguousarray(np.broadcast_to(np.asarray(inp["sgu_ln_g"], f32)[0][None, :], (128, E)))
    lnb = np.ascontiguousarray(np.broadcast_to(np.asarray(inp["sgu_ln_b"], f32)[0][None, :], (128, E)))
    wsT = np.ascontiguousarray(np.asarray(inp["sgu_w_s"], f32)[0].transpose(2, 0, 1))
    bsb = np.ascontiguousarray(np.broadcast_to(np.asarray(inp["sgu_b_s"], f32)[0][None], (128, 8, 128)))
    w_out = np.ascontiguousarray(np.asarray(inp["sgu_w_out"], f32)[0].reshape(NEC, 128, D).transpose(1, 0, 2))
    xT = [np.ascontiguousarray(x[b].T) for b in range(2)]
    maps = []
    for c in range(8):
        b, r = c // 4, c % 4
        t0 = r * T - HALO
        xl = np.zeros((D, TL), f32)
        lo = max(t0, 0)
        xl[:, lo - t0:] = xT[b][:, lo:t0 + TL]
        valid0 = np.ones((128, 512), f32) if r > 0 else np.zeros((128, 512), f32)
        wq = np.zeros((2, 5, D, 128), f32)
        ka = np.zeros((2, 6, S), ml_dtypes.bfloat16)
        qa = np.zeros((2, 6, S), ml_dtypes.bfloat16)
        for s in range(2):
            hd = r if s == 0 else 4 + r
            slope = 2.0 ** (-(hd + 1))
            for part, base in ((0, 0), (1, D)):
                A = wqkv_full[:, base + hd * 128: base + (hd + 1) * 128]
                wq[s, 2 * part + 0] = A
                wq[s, 2 * part + 1] = np.concatenate([A[:, 64:], A[:, :64]], axis=1)
            wq[s, 4] = wqkv_full[:, 2 * D + hd * 128: 2 * D + (hd + 1) * 128]
            ka[s, 0:3] = 1.0
            ka[s, 3] = (slope * 128.0 * c7).astype(ml_dtypes.bfloat16)
            ka[s, 4] = (slope * 16.0 * a3).astype(ml_dtypes.bfloat16)
            ka[s, 5] = (slope * b4).astype(ml_dtypes.bfloat16)
            qa[s, 0] = (-slope * 128.0 * c7).astype(ml_dtypes.bfloat16)
            qa[s, 1] = (-slope * 16.0 * a3).astype(ml_dtypes.bfloat16)
            qa[s, 2] = (-slope * b4).astype(ml_dtypes.bfloat16)
            qa[s, 3:6] = 1.0
        idx = np.zeros((128, NL * 8), np.int32)
        for n in range(NL):
            tile = max(r * (T // 512) + n - 1, 0)
            for hd in range(8):
                rank, s = hd % 4, hd // 4
                unit = s * NT + tile
                idx[:, n * 8 + hd] = ((unit // 4) * 4 + rank) * 512 + (unit % 4) * 128 + np.arange(128)
        maps.append({
            "xT_full": xT[b], "xT_loc": xl, "valid0": valid0, "kaug0": ka[0], "kaug1": ka[1], "qaug0": qa[0], "qaug1": qa[1], "maskd": maskd,
            **{f"wqkv{s_}_{w_}": np.ascontiguousarray(wq[s_, w_]) for s_ in range(2) for w_ in range(5)},
            "lamv": lamv, "subln": subln, "gains": gains, "idxtab": idx, "w_o": w_o, "w_up0": w_up[0], "w_up1": w_up[1], "w_dn0": w_dn[0], "w_dn1": w_dn[1],
            "convw0": convw[0], "convw1": convw[1], "convb0": convb[0], "convb1": convb[1], "w_in": w_in, "lng": lng, "lnb": lnb, "wsT": wsT, "tril": tril,
            "bsb": bsb, "w_out": w_out,
        })
    return maps


def kernel(**inputs):
    x = inputs["x"]
    S = x.shape[1]
    if S not in _CACHE:
        _CACHE[S] = build_program(S)
    nc = _CACHE[S]
    maps = _prep_inputs(S, inputs)
    res = run_bass_kernel_spmd(nc, maps, core_ids=list(range(8)))
    T = S // 4
    out = np.zeros((2, S, D), np.float32)
    for c in range(8):
        b, r = c // 4, c % 4
        out[b, r * T:(r + 1) * T, :] = np.asarray(res.results[c]["yT"]).T
    return out
```

```python
import contextlib
import numpy as np
import ml_dtypes
import concourse.bass as bass
import concourse.mybir as mybir
from concourse.bass_utils import run_bass_kernel_spmd

F32 = mybir.dt.float32
BF16 = mybir.dt.bfloat16
I32 = mybir.dt.int32
AF = mybir.ActivationFunctionType
ALU = mybir.AluOpType
AX = mybir.AxisListType

D = 1024
H = 8
DH = 64
FF = 2816
NFC = FF // 128
E = 2048
NEC = E // 128
NORM_EPS = 1e-6
LN_EPS = 1e-5
LAM_INIT = 0.2
WIN_A = 8
HALO = 512
NEG = -30000.0
_STOP = 99
_SKIP0 = True
_TESTF = 0
_ABL = 0
_OUTINT = False


class Buf:
    __slots__ = ("name", "lw", "rs")

    def __init__(self, name):
        self.name = name
        self.lw = None
        self.rs = []


class Op:
    __slots__ = ("eng", "fn", "dma", "key", "deps", "sig", "seq", "inc")

    def __init__(self, eng, fn, dma, key, inc):
        self.eng, self.fn, self.dma, self.key, self.inc = eng, fn, dma, key, inc
        self.deps = []
        self.sig = dma
        self.seq = 0


class Prog:
    ENGS = ("sp", "act", "dve", "pool", "pe")

    def __init__(self, nc, tag, semstack):
        self.nc = nc
        self.tag = tag
        self.semstack = semstack
        self.ops = []
        self.lastkey = {}

    def add(self, eng, fn, reads=(), writes=(), dma=False, key=None, inc=16):
        if dma and key is None:
            key = writes[0].name
        op = Op(eng, fn, dma, key, inc if dma else 1)
        hard, war = [], []
        for b in reads:
            if b.lw is not None:
                hard.append(b.lw)
        for b in writes:
            if b.lw is not None:
                hard.append(b.lw)
            war.extend(b.rs)
        deps = {}
        for d in hard:
            same = (d.eng == eng) and not d.dma and not dma
            if same and eng == "pe":
                continue
            deps[id(d)] = d
        for d in war:
            same = (d.eng == eng) and not d.dma and not dma
            if same:
                continue
            deps[id(d)] = d
        if dma and key in self.lastkey:
            d = self.lastkey[key]
            deps[id(d)] = d
        if dma:
            self.lastkey[key] = op
        for d in deps.values():
            d.sig = True
            op.deps.append(d)
        for b in reads:
            b.rs.append(op)
        for b in writes:
            b.lw = op
            b.rs = []
        self.ops.append(op)
        return op

    def emit(self):
        nc = self.nc
        st = self.semstack
        if True:
            esem = {e: st.enter_context(nc.semaphore(f"{self.tag}_{e}")) for e in self.ENGS}
            keys = []
            for op in self.ops:
                if op.dma and op.key not in keys:
                    keys.append(op.key)
            ksem = {k: st.enter_context(nc.semaphore(f"{self.tag}_k{i}")) for i, k in enumerate(keys)}
            cnt = {e: 0 for e in self.ENGS}
            kcnt = {k: 0 for k in keys}
            for op in self.ops:
                if op.dma:
                    kcnt[op.key] += op.inc
                    op.seq = kcnt[op.key]
                elif op.sig:
                    cnt[op.eng] += 1
                    op.seq = cnt[op.eng]
            per = {e: [o for o in self.ops if o.eng == e] for e in self.ENGS}

            def run(ename, e):
                waited = {}
                for op in per[ename]:
                    for d in op.deps:
                        sem = ksem[d.key] if d.dma else esem[d.eng]
                        sid = id(sem)
                        if waited.get(sid, 0) >= d.seq:
                            continue
                        e.wait_ge(sem, d.seq)
                        waited[sid] = d.seq
                    ins = op.fn(e)
                    if op.dma:
                        ins.then_inc(ksem[op.key], op.inc)
                    elif op.sig:
                        ins.then_inc(esem[op.eng], 1)
                if ename == "sp":
                    for k in keys:
                        if kcnt[k] > 0:
                            e.wait_ge(ksem[k], kcnt[k])

            with nc.Block() as block:
                @block.sync
                def _(e):
                    run("sp", e)

                @block.scalar
                def _(e):
                    run("act", e)

                @block.vector
                def _(e):
                    run("dve", e)

                @block.gpsimd
                def _(e):
                    run("pool", e)

                @block.tensor
                def _(e):
                    run("pe", e)


def build_program(S):
    NT = S // 512
    NKB = S // 128
    T = S // 4
    TL = T + HALO
    NL = TL // 512

    nc = bass.Bass("TRN2", target_bir_lowering=False)
    din = lambda n, s, d: nc.dram_tensor(n, s, d, kind="ExternalInput")
    xT_full = din("xT_full", [D, S], F32)
    xT_loc = din("xT_loc", [D, TL], F32)
    valid0 = din("valid0", [128, 512], F32)
    wqkvL = [[din(f"wqkv{s_}_{w_}", [D, 128], F32) for w_ in range(5)] for s_ in range(2)]
    kaugL = [din(f"kaug{s_}", [6, S], BF16) for s_ in range(2)]
    qaugL = [din(f"qaug{s_}", [6, S], BF16) for s_ in range(2)]
    maskd = din("maskd", [128, 4, 512], F32)
    lamv = din("lamv", [1, 4, 64], F32)
    subln = din("subln", [128, 1], F32)
    gains = din("gains", [128, 8, 8], F32)
    idxtab = din("idxtab", [128, NL * 8], I32)
    w_o = din("w_o", [128, 8, D], F32)
    w_upL = [din(f"w_up{l}", [128, 8, 2 * FF], F32) for l in range(2)]
    w_dnL = [din(f"w_dn{l}", [128, NFC, D], F32) for l in range(2)]
    convwL = [din(f"convw{l}", [128, NFC, 3], F32) for l in range(2)]
    convbL = [din(f"convb{l}", [128, NFC], F32) for l in range(2)]
    w_in = din("w_in", [128, 8, 2 * E], F32)
    lng = din("lng", [128, E], F32)
    lnb = din("lnb", [128, E], F32)
    wsT = din("wsT", [128, 8, 128], F32)
    tril = din("tril", [128, 128], F32)
    bsb = din("bsb", [128, 8, 128], F32)
    w_out = din("w_out", [128, NEC, D], F32)
    yT = nc.dram_tensor("yT", [D, T], F32, kind="ExternalOutput")

    ag_in = nc.dram_tensor("ag_in", [2 * NT * 128, 512], BF16)
    ag_out = nc.dram_tensor("ag_out", [4 * 2 * NT * 128, 512], BF16)
    xmid0 = nc.dram_tensor("xmid0", [D, TL], F32)
    x1s = nc.dram_tensor("x1s", [D, TL], F32)
    xmid1 = nc.dram_tensor("xmid1", [D, TL], F32)

    def fm(t):
        return t.ap().rearrange("(c p) t -> p c t", p=128)

    with contextlib.ExitStack() as top:
        _uid = iter(range(1 << 30))
        sb = lambda st, n, s, d: st.enter_context(nc.sbuf_tensor(f"{n}_u{next(_uid)}", s, d))
        ps = top.enter_context(nc.psum_tensor("ps", [128, 8 * 512], F32))
        PB = [Buf(f"ps{i}") for i in range(8)]

        def newP(tag):
            for b_ in PB:
                b_.lw = None
                b_.rs = []
            return Prog(nc, tag, top)

        def bank(i, rows=128, cols=512, c0=0):
            return ps[0:rows, i * 512 + c0:i * 512 + c0 + cols]

        ones_m = sb(top, "ones_m", [128, 128], BF16)
        ones_v = sb(top, "ones_v", [128, 128], BF16)
        ones_f = sb(top, "ones_f", [128, 128], F32)
        ones_b = sb(top, "ones_b", [128, 128], BF16)
        gn = sb(top, "gn", [128, 8, 8], F32)
        neglam = sb(top, "neglam", [128, 1], F32)
        subg = sb(top, "subg", [128, 1], F32)
        B_const = Buf("const")

        with contextlib.ExitStack() as st:
            P = newP("c0")
            lv = sb(st, "lv", [1, 4, 64], F32)
            pr = sb(st, "pr", [1, 2, 64], F32)
            sm = sb(st, "sm", [1, 2], F32)
            ex = sb(st, "ex", [1, 2], F32)
            nl = sb(st, "nl", [1, 1], F32)
            b_lv, b_pr, b_sm, b_ex, b_nl = (Buf(n) for n in ("lv", "pr", "sm", "ex", "nl"))
            b_gn, b_sub, b_ones = Buf("gn"), Buf("sub"), Buf("ones")
            P.add("sp", lambda e: e.dma_start(out=lv[:, :, :], in_=lamv.ap()), writes=[b_lv], dma=True, key="c")
            P.add("sp", lambda e: e.dma_start(out=gn[:, :, :], in_=gains.ap()), writes=[b_gn], dma=True, key="c")
            P.add("sp", lambda e: e.dma_start(out=subg[:, :], in_=subln.ap()), writes=[b_sub], dma=True, key="c")
            P.add("pool", lambda e: e.memset(ones_m[:, :], 1.0 / 1024.0), writes=[b_ones])
            P.add("pool", lambda e: e.memset(ones_v[:, :], 1.0 / 128.0), writes=[b_ones])
            P.add("pool", lambda e: e.memset(ones_f[:, :], 1.0), writes=[b_ones])
            P.add("pool", lambda e: e.memset(ones_b[:, :], 1.0), writes=[b_ones])
            P.add("dve", lambda e: e.tensor_tensor(out=pr[:, 0, :], in0=lv[:, 0, :], in1=lv[:, 1, :], op=ALU.mult),
                  reads=[b_lv], writes=[b_pr])
            P.add("dve", lambda e: e.tensor_tensor(out=pr[:, 1, :], in0=lv[:, 2, :], in1=lv[:, 3, :], op=ALU.mult),
                  reads=[b_lv], writes=[b_pr])
            P.add("dve", lambda e: e.reduce_sum(out=sm[:, 0:1], in_=pr[:, 0, :], axis=AX.X), reads=[b_pr], writes=[b_sm])
            P.add("dve", lambda e: e.reduce_sum(out=sm[:, 1:2], in_=pr[:, 1, :], axis=AX.X), reads=[b_pr], writes=[b_sm])
            P.add("act", lambda e: e.activation(out=ex[:, :], in_=sm[:, :], func=AF.Exp), reads=[b_sm], writes=[b_ex])
            P.add("dve", lambda e: e.scalar_tensor_tensor(out=nl[:, :], in0=ex[:, 1:2], scalar=-LAM_INIT, in1=ex[:, 0:1],
                                                         op0=ALU.add, op1=ALU.subtract), reads=[b_ex], writes=[b_nl])
            P.add("pe", lambda e: e.matmul(bank(0, 128, 1), lhsT=ones_f[0:1, :], rhs=nl[0:1, 0:1], start=True, stop=True),
                  reads=[b_nl, b_ones], writes=[PB[0]])
            P.add("dve", lambda e: e.tensor_copy(out=neglam[:, :], in_=bank(0, 128, 1)), reads=[PB[0]], writes=[B_const])
            P.add("dve", lambda e: e.tensor_scalar(out=subg[:, :], in0=subg[:, :], scalar1=1.0 - LAM_INIT, scalar2=None,
                                                   op0=ALU.mult), reads=[b_sub], writes=[b_sub])
            P.emit()

        def rstd_from_sq(P, sq_ap_fn, nch, ones, bankid, sq_buf, tmp, rstd, b_tmp, b_rstd, eps):
            for c in range(nch):
                P.add("pe", (lambda c: lambda e: e.matmul(bank(bankid), lhsT=ones[:, :], rhs=sq_ap_fn(c),
                                                            start=(c == 0), stop=(c == nch - 1)))(c),
                      reads=[sq_buf], writes=[PB[bankid]])
            P.add("act", lambda e: e.activation(out=tmp[:, :], in_=bank(bankid), func=AF.Sqrt, bias=eps, scale=1.0),
                  reads=[PB[bankid]], writes=[b_tmp])
            P.add("dve", lambda e: e.reciprocal(out=rstd[:, :], in_=tmp[:, :]), reads=[b_tmp], writes=[b_rstd])

        for slot in range(2):
            if _STOP < 1 + slot:
                return nc
            with contextlib.ExitStack() as st:
                KT = [sb(st, f"KT{m}", [70, S], BF16) for m in range(2)]
                Vt = sb(st, "Vt", [128, S], BF16)
                wq = [sb(st, f"wq{m}", [128, 8, 128], BF16) for m in range(2)]
                wk = [sb(st, f"wk{m}", [128, 8, 128], BF16) for m in range(2)]
                wv = sb(st, "wv", [128, 8, 128], BF16)
                stg = sb(st, "stg", [128, 8, 128], F32)
                xt = [sb(st, "xt0", [128, 8, 512], F32)] * 2
                xsq = sb(st, "xsq", [128, 8, 512], BF16)
                xn = [sb(st, f"xn{i}", [128, 8, 512], BF16) for i in range(2)]
                rtmp = sb(st, "rtmp", [128, 512], F32)
                rstd = sb(st, "rstd", [128, 512], F32)
                QT = [[sb(st, f"QT{m}_{i}", [70, 512], BF16) for i in range(2)] for m in range(2)]
                PT = [sb(st, f"PT{i}", [128, 1024], BF16) for i in range(3)]
                tmpS = [sb(st, f"tmpS{i}", [128, 1024], F32) for i in range(2)]
                acc = [sb(st, f"acc{m}", [128, 512], F32) for m in range(2)]
                mk = sb(st, "mk", [128, 4, 512], F32)
                rr = [sb(st, f"rr{m}", [128, 512], F32) for m in range(2)]
                tt = [sb(st, f"tt{m}", [128, 512], F32) for m in range(2)]
                dd = sb(st, "dd", [128, 512], F32)
                dsq = sb(st, "dsq", [128, 512], BF16)
                r2t = sb(st, "r2t", [128, 512], F32)
                r2 = sb(st, "r2", [128, 512], F32)
                oT = [sb(st, f"oT{i}", [128, 512], BF16) for i in range(2)]

                P = newP(f"a{slot}")
                b_KT = [Buf(f"KT{m}") for m in range(2)]
                b_KTa = Buf("KTaug")
                b_V = Buf("V")
                b_w = Buf("w")
                b_stg = Buf("stg")
                b_xt = [Buf("xt0")] * 2
                b_xsq = Buf("xsq")
                b_xn = [Buf(f"xn{i}") for i in range(2)]
                b_rtmp, b_rstd = Buf("rtmp"), Buf("rstd")
                b_QT = [[Buf(f"QT{m}_{i}") for i in range(2)] for m in range(2)]
                b_QTa = [[Buf(f"QTa{m}_{i}") for i in range(2)] for m in range(2)]
                b_PT = [Buf(f"PT{i}") for i in range(3)]
                b_tmpS = [Buf(f"tmpS{i}") for i in range(2)]
                b_acc = [Buf(f"acc{m}") for m in range(2)]
                b_mk = Buf("mk")
                b_rr = [Buf(f"rr{m}") for m in range(2)]
                b_tt = [Buf(f"tt{m}") for m in range(2)]
                b_dd, b_dsq, b_r2t, b_r2 = Buf("dd"), Buf("dsq"), Buf("r2t"), Buf("r2")
                b_oT = [Buf(f"oT{i}") for i in range(2)]

                dests = [wq[0], wq[1], wk[0], wk[1], wv]
                for wi in range(5):
                    P.add("sp", (lambda wi: lambda e: e.dma_start(
                        out=stg[:, :, :], in_=wqkvL[slot][wi].ap().rearrange("(c p) n -> p c n", p=128)))(wi),
                        writes=[b_stg], dma=True, key="ld")
                    for c in range(8):
                        sc2 = 0.125 if wi < 2 else 1.0
                        P.add("dve", (lambda wi, c, sc2: lambda e: e.tensor_scalar(
                            out=dests[wi][:, c, :], in0=stg[:, c, :], scalar1=gn[:, 0, c:c + 1], scalar2=sc2,
                            op0=ALU.mult, op1=ALU.mult))(wi, c, sc2), reads=[b_stg], writes=[b_w])
                for m in range(2):
                    P.add("sp", (lambda m: lambda e: e.dma_start(out=KT[m][64:70, :], in_=kaugL[slot].ap()))(m),
                          writes=[b_KTa], dma=True, key="ld")
                P.add("sp", lambda e: e.dma_start(out=mk[:, :, :], in_=maskd.ap()), writes=[b_mk], dma=True, key="ld")

                def prep(g):
                    i = g % 2
                    P.add("sp", lambda e: e.dma_start(out=xt[i][:, :, :], in_=fm(xT_full)[:, :, g * 512:(g + 1) * 512]),
                          writes=[b_xt[i]], dma=True)
                    P.add("act", lambda e: e.activation(out=xsq[:, :, :], in_=xt[i][:, :, :], func=AF.Square),
                          reads=[b_xt[i]], writes=[b_xsq])
                    rstd_from_sq(P, lambda c: xsq[:, c, :], 8, ones_m, 7, b_xsq, rtmp, rstd, b_rtmp, b_rstd, NORM_EPS)
                    for c in range(8):
                        P.add("dve", (lambda c: lambda e: e.tensor_tensor(out=xn[i][:, c, :], in0=xt[i][:, c, :],
                                                                         in1=rstd[:, :], op=ALU.mult))(c),
                              reads=[b_xt[i], b_rstd], writes=[b_xn[i]])
                    for m in range(2):
                        P.add("sp", (lambda m: lambda e: e.dma_start(out=QT[m][i][64:70, :],
                                                                      in_=qaugL[slot].ap()[:, g * 512:(g + 1) * 512]))(m),
                              writes=[b_QTa[m][i]], dma=True, key=f"qa{i}")

                def qkv(g):
                    i = g % 2
                    bk = [6, 7]
                    n = 0
                    for m in range(2):
                        for kind in range(2):
                            w = wq[m] if kind == 0 else wk[m]
                            b = bk[n % 2]
                            n += 1
                            for c in range(8):
                                P.add("pe", (lambda w, b, c: lambda e: e.matmul(bank(b), lhsT=w[:, c, :], rhs=xn[i][:, c, :],
                                                                                 start=(c == 0), stop=(c == 7)))(w, b, c),
                                      reads=[b_w, b_xn[i]], writes=[PB[b]])
                            if kind == 0:
                                P.add("act", (lambda m, b: lambda e: e.activation(out=QT[m][i][0:64, :], in_=bank(b, 64),
                                                                                   func=AF.Copy))(m, b),
                                      reads=[PB[b]], writes=[b_QT[m][i]])
                            else:
                                P.add("act", (lambda m, b: lambda e: e.activation(
                                    out=KT[m][0:64, g * 512:(g + 1) * 512], in_=bank(b, 64), func=AF.Copy))(m, b),
                                    reads=[PB[b]], writes=[b_KT[m]])
                    b = bk[n % 2]
                    for j in range(4):
                        for c in range(8):
                            P.add("pe", (lambda b, j, c: lambda e: e.matmul(
                                bank(b, 128, 128, j * 128), lhsT=xn[i][:, c, j * 128:(j + 1) * 128], rhs=wv[:, c, :],
                                start=(c == 0), stop=(c == 7)))(b, j, c),
                                reads=[b_w, b_xn[i]], writes=[PB[b]])
                    P.add("act", (lambda b: lambda e: e.activation(out=Vt[:, g * 512:(g + 1) * 512], in_=bank(b), func=AF.Copy))(b),
                          reads=[PB[b]], writes=[b_V])

                ucount = [0]

                def attention(g):
                    i = g % 2
                    kb0 = max(0, 4 * g - WIN_A) if slot == 0 else 0
                    kbl = 4 * g + 3
                    kbs = list(range(kb0, kbl + 1))
                    us = []
                    for _ in kbs:
                        us.append(ucount[0])
                        ucount[0] += 1

                    def emit_qk(kb, u):
                        pb = u % 2
                        for m in range(2):
                            P.add("pe", (lambda m, kb, pb: lambda e: e.matmul(
                                bank(2 * pb + m), lhsT=KT[m][0:70, kb * 128:(kb + 1) * 128], rhs=QT[m][i][0:70, :],
                                start=True, stop=True))(m, kb, pb),
                                reads=[b_KT[m], b_KTa, b_QT[m][i], b_QTa[m][i]], writes=[PB[2 * pb + m]])

                    def emit_exp(kb, u):
                        pb = u % 2
                        pt = u % 3
                        if kb >= 4 * g:
                            v = kb - 4 * g
                            for m in range(2):
                                P.add("dve", (lambda m, pb, v: lambda e: e.tensor_tensor(
                                    out=tmpS[pb][:, m * 512:(m + 1) * 512], in0=bank(2 * pb + m), in1=mk[:, v, :],
                                    op=ALU.add))(m, pb, v),
                                    reads=[PB[2 * pb + m], b_mk], writes=[b_tmpS[pb]])
                            P.add("act", (lambda pb, pt: lambda e: e.activation(out=PT[pt][:, :], in_=tmpS[pb][:, :], func=AF.Exp))(pb, pt),
                                  reads=[b_tmpS[pb]], writes=[b_PT[pt]])
                        else:
                            W_ = 512 if _ABL == 4 else 1024
                            P.add("act", (lambda pb, pt, W_: lambda e: e.activation(
                                out=PT[pt][:, 0:W_], in_=ps[:, 2 * pb * 512:2 * pb * 512 + W_], func=AF.Exp))(pb, pt, W_),
                                reads=[PB[2 * pb], PB[2 * pb + 1]], writes=[b_PT[pt]])

                    def emit_pv(kb, u):
                        pt = u % 3
                        for m in range(2):
                            if _ABL == 5 and kb != kb0 and kb != kbl:
                                continue
                            P.add("pe", (lambda m, kb, pt: lambda e: e.matmul(
                                bank(4 + m), lhsT=Vt[:, kb * 128:(kb + 1) * 128], rhs=PT[pt][:, m * 512:(m + 1) * 512],
                                start=(kb == kb0), stop=(kb == kbl)))(m, kb, pt),
                                reads=[b_V, b_PT[pt]], writes=[PB[4 + m]])
                        if _ABL in (1, 3) and kb != kb0:
                            pass
                        elif kb == kb0:
                            P.add("dve", (lambda pt: lambda e: e.tensor_copy(out=acc[0][:, :], in_=PT[pt][:, 0:512]))(pt),
                                  reads=[b_PT[pt]], writes=[b_acc[0]])
                        else:
                            P.add("dve", (lambda pt: lambda e: e.tensor_tensor(
                                out=acc[0][:, :], in0=acc[0][:, :], in1=PT[pt][:, 0:512], op=ALU.add))(pt),
                                reads=[b_PT[pt], b_acc[0]], writes=[b_acc[0]])
                        if not (_ABL in (2, 3) and kb != kb0):
                            P.add("pe", (lambda kb, pt: lambda e: e.matmul(
                                bank(6), lhsT=ones_b[:, :], rhs=PT[pt][:, 512:1024], start=(kb == kb0), stop=(kb == kbl or _ABL in (2, 3))))(kb, pt),
                                reads=[b_PT[pt]], writes=[PB[6]])

                    emit_qk(kbs[0], us[0])
                    for n_, kb in enumerate(kbs):
                        if n_ + 1 < len(kbs):
                            emit_qk(kbs[n_ + 1], us[n_ + 1])
                        emit_exp(kb, us[n_])
                        emit_pv(kb, us[n_])

                def finalize(g):
                    i = g % 2
                    P.add("pe", lambda e: e.matmul(bank(7), lhsT=ones_f[:, :], rhs=acc[0][:, :], start=True, stop=True),
                          reads=[b_acc[0]], writes=[PB[7]])
                    for m in range(2):
                        P.add("dve", (lambda m: lambda e: e.reciprocal(out=rr[m][:, :], in_=bank(7 - m)))(m),
                              reads=[PB[7 - m]], writes=[b_rr[m]])
                        P.add("dve", (lambda m: lambda e: e.tensor_tensor(out=tt[m][:, :], in0=bank(4 + m), in1=rr[m][:, :],
                                                                         op=ALU.mult))(m),
                              reads=[PB[4 + m], b_rr[m]], writes=[b_tt[m]])
                    P.add("dve", lambda e: e.scalar_tensor_tensor(out=dd[:, :], in0=tt[1][:, :], scalar=neglam[:, 0:1],
                                                                 in1=tt[0][:, :], op0=ALU.mult, op1=ALU.add),
                          reads=[b_tt[0], b_tt[1]], writes=[b_dd])
                    P.add("act", lambda e: e.activation(out=dsq[:, :], in_=dd[:, :], func=AF.Square), reads=[b_dd], writes=[b_dsq])
                    rstd_from_sq(P, lambda c: dsq[:, :], 1, ones_v, 6, b_dsq, r2t, r2, b_r2t, b_r2, NORM_EPS)
                    P.add("dve", lambda e: e.scalar_tensor_tensor(out=oT[i][:, :], in0=dd[:, :], scalar=subg[:, 0:1],
                                                                 in1=r2[:, :], op0=ALU.mult, op1=ALU.mult),
                          reads=[b_dd, b_r2], writes=[b_oT[i]])
                    row0 = (slot * NT + g) * 128
                    P.add("sp", lambda e: e.dma_start(out=ag_in.ap()[row0:row0 + 128, :], in_=oT[i][:, :]),
                          reads=[b_oT[i]], dma=True, key=f"st_oT{i}")

                prep(0)
                for g in range(NT):
                    qkv(g)
                    if g + 1 < NT:
                        prep(g + 1)
                    attention(g)
                    finalize(g)
                P.emit()

        if _STOP < 3:
            return nc
        if True:
            csem = top.enter_context(nc.semaphore("csem"))
            with nc.Block() as block:
                @block.gpsimd
                def _(g):
                    RP = 512
                    for k in range(2 * NT * 128 // RP):
                        g.collective_compute("AllGather", ALU.bypass, replica_groups=[[0, 1, 2, 3], [4, 5, 6, 7]],
                                             ins=[ag_in.ap()[k * RP:(k + 1) * RP, :].opt()],
                                             outs=[ag_out.ap()[k * 4 * RP:(k + 1) * 4 * RP, :].opt()]).then_inc(csem)
                        g.wait_ge(csem, k + 1)

        if _STOP < 4:
            return nc
        def postnorm_residual(P, msb, b_msb, msq, b_msq, rt, rs, b_rt, b_rs, gvec, resid_fn, b_resid, out_fn, b_out, tmp, b_tmp2):
            rstd_from_sq(P, lambda c: msq[:, c, :], 8, ones_m, 7, b_msq, rt, rs, b_rt, b_rs, NORM_EPS)
            for dc in range(8):
                P.add("dve", (lambda dc: lambda e: e.scalar_tensor_tensor(
                    out=tmp[:, :], in0=msb[:, dc, :], scalar=gn[:, gvec, dc:dc + 1], in1=rs[:, :],
                    op0=ALU.mult, op1=ALU.mult))(dc), reads=[b_msb, b_rs], writes=[b_tmp2])
                P.add("dve", (lambda dc: lambda e: e.tensor_tensor(out=out_fn(dc), in0=tmp[:, :], in1=resid_fn(dc), op=ALU.add))(dc),
                      reads=[b_tmp2, b_resid], writes=[b_out])

        with contextlib.ExitStack() as st:
            wo = sb(st, "wo", [128, 8, D], BF16)
            b_wo = Buf("wo")
            with contextlib.ExitStack() as s2:
                stg = sb(s2, "stgo", [128, 8, D], F32)
                b_stg = Buf("stg")
                P = newP("wo0")
                P.add("sp", lambda e: e.dma_start(out=stg[:, :, :], in_=w_o.ap()), writes=[b_stg], dma=True)
                for c in range(8):
                    P.add("dve", (lambda c: lambda e: e.tensor_copy(out=wo[:, c, :], in_=stg[:, c, :]))(c),
                          reads=[b_stg], writes=[b_wo])
                P.emit()
            idx = sb(st, "idx", [128, NL * 8], I32)
            ogL = [sb(st, f"og{i_}", [128, 8, 512], BF16) for i_ in range(2)]
            xtL = [sb(st, f"xtw{i_}", [128, 8, 512], F32) for i_ in range(2)]
            msb = sb(st, "msb", [128, 8, 512], F32)
            msq = sb(st, "msq", [128, 8, 512], BF16)
            rt = sb(st, "rtw", [128, 512], F32)
            rs = sb(st, "rsw", [128, 512], F32)
            tmp = sb(st, "tmpw", [128, 512], F32)
            b_idx, b_msb, b_msq, b_rt, b_rs, b_tmp = (Buf(n) for n in ("idx", "msb", "msq", "rt", "rs", "tmp"))
            b_ogL = [[Buf(f"og{i_}_{h_}") for h_ in range(8)] for i_ in range(2)]
            b_xtL = [Buf(f"xt{i_}") for i_ in range(2)]
            P = newP("wo1")
            P.add("sp", lambda e: e.dma_start(out=idx[:, :], in_=idxtab.ap()), writes=[b_idx], dma=True, key="ld")

            def wo_loads(n):
                i_ = n % 2
                for hd in range(8):
                    P.add("pool", (lambda n, hd, i_: lambda e: e.indirect_dma_start(
                        out=ogL[i_][:, hd, :], out_offset=None, in_=ag_out.ap(),
                        in_offset=bass.IndirectOffsetOnAxis(ap=idx[:, n * 8 + hd:n * 8 + hd + 1], axis=0)))(n, hd, i_),
                        reads=[b_idx], writes=[b_ogL[i_][hd]], dma=True, key=f"og{i_}")
                P.add("sp", (lambda n, i_: lambda e: e.dma_start(out=xtL[i_][:, :, :], in_=fm(xT_loc)[:, :, n * 512:(n + 1) * 512]))(n, i_),
                      writes=[b_xtL[i_]], dma=True)

            wo_loads(0)
            for n in range(NL):
                if n + 1 < NL:
                    wo_loads(n + 1)
                og, xt = ogL[n % 2], xtL[n % 2]
                b_og, b_xt = b_ogL[n % 2], b_xtL[n % 2]
                for dc in range(8):
                    b = dc % 2
                    for hd in range(8):
                        P.add("pe", (lambda b, dc, hd, og: lambda e: e.matmul(bank(b), lhsT=wo[:, hd, dc * 128:(dc + 1) * 128],
                                                                              rhs=og[:, hd, :], start=(hd == 0), stop=(hd == 7)))(b, dc, hd, og),
                              reads=[b_og[hd]], writes=[PB[b]])
                    P.add("act", (lambda b, dc: lambda e: e.activation(out=msb[:, dc, :], in_=bank(b), func=AF.Copy))(b, dc),
                          reads=[PB[b]], writes=[b_msb])
                    P.add("act", (lambda b, dc: lambda e: e.activation(out=msq[:, dc, :], in_=bank(b), func=AF.Square))(b, dc),
                          reads=[PB[b]], writes=[b_msq])
                postnorm_residual(P, msb, b_msb, msq, b_msq, rt, rs, b_rt, b_rs, 1, (lambda xt: lambda dc: xt[:, dc, :])(xt), b_xt,
                                  lambda dc: msb[:, dc, :], b_msb, tmp, b_tmp)
                P.add("sp", (lambda n: lambda e: e.dma_start(out=fm(xmid0)[:, :, n * 512:(n + 1) * 512], in_=msb[:, :, :]))(n),
                      reads=[b_msb], dma=True, key="st_msb")
            P.emit()

        if _STOP < 5:
            return nc
        def ffn_phase(layer, src, dst, dst_is_out):
            gpre, gpost = (2, 3) if layer == 0 else (6, 7)
            with contextlib.ExitStack() as st:
                wup = sb(st, "wup", [128, 8, 2 * FF], BF16)
                wdn = sb(st, "wdn", [128, NFC, D], BF16)
                cw = sb(st, "cw", [128, NFC, 3], F32)
                cb = sb(st, "cb", [128, NFC], F32)
                vl0 = sb(st, "vl0", [128, 512], F32)
                b_w = Buf("w")
                with contextlib.ExitStack() as s2:
                    stg = [sb(s2, f"stgf{i}", [128, 2 * FF], F32) for i in range(2)]
                    b_stg = [Buf(f"stg{i}") for i in range(2)]
                    P = newP(f"fw{layer}")
                    P.add("sp", lambda e: e.dma_start(out=cw[:, :, :], in_=convwL[layer].ap()), writes=[b_w], dma=True, key="misc")
                    P.add("sp", lambda e: e.dma_start(out=cb[:, :], in_=convbL[layer].ap()), writes=[b_w], dma=True, key="misc")
                    P.add("sp", lambda e: e.dma_start(out=vl0[:, :], in_=valid0.ap()), writes=[b_w], dma=True, key="misc")
                    for c in range(8):
                        i = c % 2
                        P.add("sp", (lambda c, i: lambda e: e.dma_start(out=stg[i][:, :], in_=w_upL[layer].ap()[:, c, :]))(c, i),
                              writes=[b_stg[i]], dma=True)
                        P.add("dve", (lambda c, i: lambda e: e.tensor_scalar(out=wup[:, c, :], in0=stg[i][:, :],
                                                                            scalar1=gn[:, gpre, c:c + 1], scalar2=None, op0=ALU.mult))(c, i),
                              reads=[b_stg[i]], writes=[b_w])
                    for q in range(5):
                        f0, f1 = q * 5, min(NFC, q * 5 + 5)
                        nf = f1 - f0
                        i = q % 2
                        P.add("sp", (lambda f0, f1, nf, i: lambda e: e.dma_start(
                            out=stg[i][:, 0:nf * D].rearrange("p (f n) -> p f n", n=D), in_=w_dnL[layer].ap()[:, f0:f1, :]))(f0, f1, nf, i),
                            writes=[b_stg[i]], dma=True)
                        P.add("pool", (lambda f0, f1, nf, i: lambda e: e.tensor_copy(
                            out=wdn[:, f0:f1, :], in_=stg[i][:, 0:nf * D].rearrange("p (f n) -> p f n", n=D)))(f0, f1, nf, i),
                            reads=[b_stg[i]], writes=[b_w])
                    P.emit()
                xf = sb(st, "xf", [128, 8, 512], F32)
                xn = sb(st, "xnf", [128, 8, 512], BF16)
                hh = sb(st, "hh", [128, NFC, 512], BF16)
                abufL = [sb(st, f"abuf{i_}", [128, 514], F32) for i_ in range(2)]
                acL = [sb(st, f"ac{i_}", [128, 512], F32) for i_ in range(2)]
                geL = [sb(st, f"ge{i_}", [128, 512], F32) for i_ in range(2)]
                xr = [sb(st, f"xr{i}", [128, 512], F32) for i in range(2)]
                rt = sb(st, "rtf", [128, 512], F32)
                rs = sb(st, "rsf", [128, 512], F32)
                tmp = sb(st, "tmpf", [128, 512], F32)
                ahalo = sb(st, "ahalo", [128, NFC, 2], F32)
                b_xf, b_xn, b_hh = (Buf(n) for n in ("xf", "xn", "hh"))
                b_abufL = [Buf(f"abuf{i_}") for i_ in range(2)]
                b_acL = [Buf(f"ac{i_}") for i_ in range(2)]
                b_geL = [Buf(f"ge{i_}") for i_ in range(2)]
                b_xr = [Buf(f"xr{i}") for i in range(2)]
                b_rt, b_rs, b_tmp, b_ah = Buf("rt"), Buf("rs"), Buf("tmp"), Buf("ahalo")
                P = newP(f"ff{layer}")
                P.add("pool", lambda e: e.memset(ahalo[:, :, :], 0.0), writes=[b_ah])
                for n in range(NL):
                    P.add("sp", (lambda n: lambda e: e.dma_start(out=xf[:, :, :], in_=fm(src)[:, :, n * 512:(n + 1) * 512]))(n),
                          writes=[b_xf], dma=True)
                    P.add("act", lambda e: e.activation(out=xn[:, :, :], in_=xf[:, :, :], func=AF.Square), reads=[b_xf], writes=[b_xn])
                    rstd_from_sq(P, lambda c: xn[:, c, :], 8, ones_m, 7, b_xn, rt, rs, b_rt, b_rs, NORM_EPS)
                    if n == 0:
                        P.add("dve", lambda e: e.tensor_tensor(out=rs[:, :], in0=rs[:, :], in1=vl0[:, :], op=ALU.mult),
                              reads=[b_rs], writes=[b_rs])
                    for c in range(8):
                        P.add("dve", (lambda c: lambda e: e.tensor_tensor(out=xn[:, c, :], in0=xf[:, c, :], in1=rs[:, :], op=ALU.mult))(c),
                              reads=[b_xf, b_rs], writes=[b_xn])
                    for fc in range(NFC):
                        ba = (2 * fc) % 4
                        bg = ba + 1
                        abuf, ac, ge = abufL[fc % 2], acL[fc % 2], geL[fc % 2]
                        b_abuf, b_ac, b_ge = b_abufL[fc % 2], b_acL[fc % 2], b_geL[fc % 2]
                        for (b, col0) in ((ba, fc * 128), (bg, FF + fc * 128)):
                            for c in range(8):
                                P.add("pe", (lambda b, col0, c: lambda e: e.matmul(bank(b), lhsT=wup[:, c, col0:col0 + 128],
                                                                                   rhs=xn[:, c, :], start=(c == 0), stop=(c == 7)))(b, col0, c),
                                      reads=[b_xn], writes=[PB[b]])
                        P.add("pool", (lambda fc, abuf: lambda e: e.tensor_copy(out=abuf[:, 0:2], in_=ahalo[:, fc, :]))(fc, abuf),
                              reads=[b_ah], writes=[b_abuf])
                        P.add("act", (lambda ba, abuf: lambda e: e.activation(out=abuf[:, 2:514], in_=bank(ba), func=AF.Copy))(ba, abuf),
                              reads=[PB[ba]], writes=[b_abuf])
                        P.add("pool", (lambda fc, abuf: lambda e: e.tensor_copy(out=ahalo[:, fc, :], in_=abuf[:, 512:514]))(fc, abuf),
                              reads=[b_abuf], writes=[b_ah])
                        P.add("dve", (lambda fc, abuf, ac: lambda e: e.tensor_scalar(out=ac[:, :], in0=abuf[:, 2:514], scalar1=cw[:, fc, 2:3],
                                                                          scalar2=cb[:, fc:fc + 1], op0=ALU.mult, op1=ALU.add))(fc, abuf, ac),
                              reads=[b_abuf], writes=[b_ac])
                        P.add("dve", (lambda fc, abuf, ac: lambda e: e.scalar_tensor_tensor(out=ac[:, :], in0=abuf[:, 1:513], scalar=cw[:, fc, 1:2],
                                                                                 in1=ac[:, :], op0=ALU.mult, op1=ALU.add))(fc, abuf, ac),
                              reads=[b_abuf, b_ac], writes=[b_ac])
                        P.add("dve", (lambda fc, abuf, ac: lambda e: e.scalar_tensor_tensor(out=ac[:, :], in0=abuf[:, 0:512], scalar=cw[:, fc, 0:1],
                                                                                 in1=ac[:, :], op0=ALU.mult, op1=ALU.add))(fc, abuf, ac),
                              reads=[b_abuf, b_ac], writes=[b_ac])
                        P.add("act", (lambda ac, ge: lambda e: e.activation(out=ge[:, :], in_=ac[:, :], func=AF.Gelu_apprx_tanh))(ac, ge), reads=[b_ac], writes=[b_ge])
                        P.add("dve", (lambda fc, bg, ge: lambda e: e.tensor_tensor(out=hh[:, fc, :], in0=bank(bg), in1=ge[:, :], op=ALU.mult))(fc, bg, ge),
                              reads=[PB[bg], b_ge], writes=[b_hh])
                    if layer == 1 and n == 0 and _SKIP0:
                        continue
                    for dc in range(8):
                        b = 4 + dc % 2
                        for fc in range(NFC):
                            P.add("pe", (lambda b, dc, fc: lambda e: e.matmul(bank(b), lhsT=wdn[:, fc, dc * 128:(dc + 1) * 128],
                                                                              rhs=hh[:, fc, :], start=(fc == 0), stop=(fc == NFC - 1)))(b, dc, fc),
                                  reads=[b_hh], writes=[PB[b]])
                        P.add("act", (lambda b, dc: lambda e: e.activation(out=xf[:, dc, :], in_=bank(b), func=AF.Copy))(b, dc),
                              reads=[PB[b]], writes=[b_xf])
                        P.add("act", (lambda b, dc: lambda e: e.activation(out=xn[:, dc, :], in_=bank(b), func=AF.Square))(b, dc),
                              reads=[PB[b]], writes=[b_xn])
                    rstd_from_sq(P, lambda c: xn[:, c, :], 8, ones_m, 7, b_xn, rt, rs, b_rt, b_rs, NORM_EPS)
                    for dc in range(8):
                        i = dc % 2
                        P.add("sp", (lambda n, dc, i: lambda e: e.dma_start(
                            out=xr[i][:, :], in_=src.ap()[dc * 128:(dc + 1) * 128, n * 512:(n + 1) * 512]))(n, dc, i),
                            writes=[b_xr[i]], dma=True)
                        P.add("dve", (lambda dc: lambda e: e.scalar_tensor_tensor(
                            out=tmp[:, :], in0=xf[:, dc, :], scalar=gn[:, gpost, dc:dc + 1], in1=rs[:, :],
                            op0=ALU.mult, op1=ALU.mult))(dc), reads=[b_xf, b_rs], writes=[b_tmp])
                        P.add("dve", (lambda dc, i: lambda e: e.tensor_tensor(out=xf[:, dc, :], in0=tmp[:, :], in1=xr[i][:, :], op=ALU.add))(dc, i),
                              reads=[b_tmp, b_xr[i]], writes=[b_xf])
                    if dst_is_out and _OUTINT:
                        pass
                    elif dst_is_out:
                        P.add("sp", (lambda n: lambda e: e.dma_start(out=fm(dst)[:, :, (n - 1) * 512:n * 512], in_=xf[:, :, :]))(n),
                              reads=[b_xf], dma=True, key="st_xf")
                    else:
                        P.add("sp", (lambda n: lambda e: e.dma_start(out=fm(dst)[:, :, n * 512:(n + 1) * 512], in_=xf[:, :, :]))(n),
                              reads=[b_xf], dma=True, key="st_xf")
                P.emit()

        ffn_phase(0, xmid0, x1s, False)
        if _TESTF == 1:
            ffn_phase(0, xmid0, x1s, False)
            return nc
        if _TESTF == 2:
            ffn_phase(1, xmid0, x1s, False)
            return nc
        if _STOP < 6:
            return nc

        with contextlib.ExitStack() as st:
            win = sb(st, "win", [128, 8, 2 * E], BF16)
            wout = sb(st, "wout", [128, NEC, D], BF16)
            wsm = sb(st, "wsm", [128, 8, 128], BF16)
            bs = sb(st, "bs", [128, 8, 128], F32)
            lg = sb(st, "lg", [128, E], F32)
            lb = sb(st, "lb", [128, E], F32)
            b_w = Buf("w")
            with contextlib.ExitStack() as s2:
                stg = [sb(s2, f"stgs{i}", [128, 2 * E], F32) for i in range(2)]
                trl = sb(s2, "trl", [128, 128], F32)
                b_stg = [Buf(f"stg{i}") for i in range(2)]
                b_trl = Buf("trl")
                P = newP("sw")
                P.add("sp", lambda e: e.dma_start(out=bs[:, :, :], in_=bsb.ap()), writes=[b_w], dma=True, key="misc")
                P.add("sp", lambda e: e.dma_start(out=lg[:, :], in_=lng.ap()), writes=[b_w], dma=True, key="misc")
                P.add("sp", lambda e: e.dma_start(out=lb[:, :], in_=lnb.ap()), writes=[b_w], dma=True, key="misc")
                P.add("sp", lambda e: e.dma_start(out=trl[:, :], in_=tril.ap()), writes=[b_trl], dma=True, key="misc")
                P.add("sp", lambda e: e.dma_start(out=stg[0][:, 0:1024].rearrange("p (g t) -> p g t", t=128), in_=wsT.ap()),
                      writes=[b_stg[0]], dma=True)
                for g8 in range(8):
                    P.add("dve", (lambda g8: lambda e: e.tensor_tensor(out=wsm[:, g8, :], in0=stg[0][:, g8 * 128:(g8 + 1) * 128],
                                                                      in1=trl[:, :], op=ALU.mult))(g8),
                          reads=[b_stg[0], b_trl], writes=[b_w])
                for c in range(8):
                    i = (c + 1) % 2
                    P.add("sp", (lambda c, i: lambda e: e.dma_start(out=stg[i][:, :], in_=w_in.ap()[:, c, :]))(c, i),
                          writes=[b_stg[i]], dma=True)
                    P.add("dve", (lambda c, i: lambda e: e.tensor_scalar(out=win[:, c, :], in0=stg[i][:, :],
                                                                        scalar1=gn[:, 4, c:c + 1], scalar2=None, op0=ALU.mult))(c, i),
                          reads=[b_stg[i]], writes=[b_w])
                for q in range(4):
                    i = (q + 1) % 2
                    P.add("sp", (lambda q, i: lambda e: e.dma_start(
                        out=stg[i][:, 0:4 * D].rearrange("p (f n) -> p f n", n=D), in_=w_out.ap()[:, q * 4:(q + 1) * 4, :]))(q, i),
                        writes=[b_stg[i]], dma=True)
                    P.add("pool", (lambda q, i: lambda e: e.tensor_copy(
                        out=wout[:, q * 4:(q + 1) * 4, :], in_=stg[i][:, 0:4 * D].rearrange("p (f n) -> p f n", n=D)))(q, i),
                        reads=[b_stg[i]], writes=[b_w])
                P.emit()
            xt = sb(st, "xts", [128, 8, 512], F32)
            xn = sb(st, "xns", [128, 8, 512], BF16)
            uu = sb(st, "uu", [128, NEC, 512], BF16)
            vt = sb(st, "vt", [128, E], F32)
            junk = sb(st, "junk", [128, E], BF16)
            vnb = sb(st, "vnb", [128, E], BF16)
            vsum = sb(st, "vsum", [128, 4], F32)
            st1 = sb(st, "st1", [128, 4], F32)
            msb = sb(st, "msbs", [128, 8, 512], F32)
            rt = sb(st, "rts", [128, 512], F32)
            rs = sb(st, "rss", [128, 512], F32)
            tmp = sb(st, "tmps", [128, 512], F32)
            ts = sb(st, "ts", [128, 128], F32)
            b_xt, b_xn, b_uu, b_vt, b_junk, b_vnb, b_vsum, b_st1, b_msb, b_rt, b_rs, b_tmp, b_ts = (
                Buf(n) for n in ("xt", "xn", "uu", "vt", "junk", "vnb", "vsum", "st1", "msb", "rt", "rs", "tmp", "ts"))
            P = newP("sg")
            for n in range(NL):
                P.add("sp", (lambda n: lambda e: e.dma_start(out=xt[:, :, :], in_=fm(x1s)[:, :, n * 512:(n + 1) * 512]))(n),
                      writes=[b_xt], dma=True)
                P.add("act", lambda e: e.activation(out=xn[:, :, :], in_=xt[:, :, :], func=AF.Square), reads=[b_xt], writes=[b_xn])
                rstd_from_sq(P, lambda c: xn[:, c, :], 8, ones_m, 7, b_xn, rt, rs, b_rt, b_rs, NORM_EPS)
                for c in range(8):
                    P.add("dve", (lambda c: lambda e: e.tensor_tensor(out=xn[:, c, :], in0=xt[:, c, :], in1=rs[:, :], op=ALU.mult))(c),
                          reads=[b_xt, b_rs], writes=[b_xn])
                for uc in range(NEC):
                    b = uc % 2
                    for c in range(8):
                        P.add("pe", (lambda b, uc, c: lambda e: e.matmul(bank(b), lhsT=win[:, c, uc * 128:(uc + 1) * 128], rhs=xn[:, c, :],
                                                                         start=(c == 0), stop=(c == 7)))(b, uc, c),
                              reads=[b_xn], writes=[PB[b]])
                    P.add("act", (lambda b, uc: lambda e: e.activation(out=uu[:, uc, :], in_=bank(b), func=AF.Gelu_apprx_tanh))(b, uc),
                          reads=[PB[b]], writes=[b_uu])
                for j in range(4):
                    for q in range(4):
                        b = 2 + q % 2
                        for c in range(8):
                            P.add("pe", (lambda b, q, c, j: lambda e: e.matmul(
                                bank(b), lhsT=xn[:, c, j * 128:(j + 1) * 128], rhs=win[:, c, E + q * 512:E + (q + 1) * 512],
                                start=(c == 0), stop=(c == 7)))(b, q, c, j),
                                reads=[b_xn], writes=[PB[b]])
                        P.add("act", (lambda b, q: lambda e: e.activation(out=vt[:, q * 512:(q + 1) * 512], in_=bank(b),
                                                                          func=AF.Gelu_apprx_tanh, accum_out=vsum[:, q:q + 1]))(b, q),
                              reads=[PB[b]], writes=[b_vt, b_vsum])
                    P.add("dve", lambda e: e.reduce_sum(out=st1[:, 0:1], in_=vsum[:, :], axis=AX.X), reads=[b_vsum], writes=[b_st1])
                    P.add("dve", lambda e: e.tensor_scalar(out=st1[:, 0:1], in0=st1[:, 0:1], scalar1=-1.0 / E, scalar2=None, op0=ALU.mult),
                          reads=[b_st1], writes=[b_st1])
                    P.add("dve", lambda e: e.tensor_scalar(out=vt[:, :], in0=vt[:, :], scalar1=st1[:, 0:1], scalar2=None, op0=ALU.add),
                          reads=[b_vt, b_st1], writes=[b_vt])
                    P.add("act", lambda e: e.activation(out=junk[:, :], in_=vt[:, :], func=AF.Square, accum_out=st1[:, 1:2]),
                          reads=[b_vt], writes=[b_junk, b_st1])
                    P.add("dve", lambda e: e.tensor_scalar(out=st1[:, 2:3], in0=st1[:, 1:2], scalar1=1.0 / E, scalar2=LN_EPS,
                                                           op0=ALU.mult, op1=ALU.add), reads=[b_st1], writes=[b_st1])
                    P.add("act", lambda e: e.activation(out=st1[:, 3:4], in_=st1[:, 2:3], func=AF.Sqrt), reads=[b_st1], writes=[b_st1])
                    P.add("dve", lambda e: e.reciprocal(out=st1[:, 2:3], in_=st1[:, 3:4]), reads=[b_st1], writes=[b_st1])
                    P.add("dve", lambda e: e.scalar_tensor_tensor(out=vt[:, :], in0=vt[:, :], scalar=st1[:, 2:3], in1=lg[:, :],
                                                                 op0=ALU.mult, op1=ALU.mult), reads=[b_vt, b_st1], writes=[b_vt])
                    P.add("dve", lambda e: e.tensor_tensor(out=vnb[:, :], in0=vt[:, :], in1=lb[:, :], op=ALU.add),
                          reads=[b_vt], writes=[b_vnb])
                    for fc in range(NEC):
                        b = 4 + fc % 2
                        g8 = fc // 2
                        P.add("pe", (lambda b, fc, g8: lambda e: e.matmul(bank(b, 128, 128), lhsT=vnb[:, fc * 128:(fc + 1) * 128],
                                                                          rhs=wsm[:, g8, :], start=True, stop=True))(b, fc, g8),
                              reads=[b_vnb], writes=[PB[b]])
                        P.add("dve", (lambda b, g8: lambda e: e.tensor_tensor(out=ts[:, :], in0=bank(b, 128, 128), in1=bs[:, g8, :], op=ALU.add))(b, g8),
                              reads=[PB[b]], writes=[b_ts])
                        P.add("dve", (lambda fc, j: lambda e: e.tensor_tensor(out=uu[:, fc, j * 128:(j + 1) * 128], in0=ts[:, :],
                                                                             in1=uu[:, fc, j * 128:(j + 1) * 128], op=ALU.mult))(fc, j),
                              reads=[b_ts, b_uu], writes=[b_uu])
                for dc in range(8):
                    b = 6 + dc % 2 if False else dc % 2
                    for fc in range(NEC):
                        P.add("pe", (lambda b, dc, fc: lambda e: e.matmul(bank(b), lhsT=wout[:, fc, dc * 128:(dc + 1) * 128], rhs=uu[:, fc, :],
                                                                          start=(fc == 0), stop=(fc == NEC - 1)))(b, dc, fc),
                              reads=[b_uu], writes=[PB[b]])
                    P.add("act", (lambda b, dc: lambda e: e.activation(out=msb[:, dc, :], in_=bank(b), func=AF.Copy))(b, dc),
                          reads=[PB[b]], writes=[b_msb])
                    P.add("act", (lambda b, dc: lambda e: e.activation(out=xn[:, dc, :], in_=bank(b), func=AF.Square))(b, dc),
                          reads=[PB[b]], writes=[b_xn])
                postnorm_residual(P, msb, b_msb, xn, b_xn, rt, rs, b_rt, b_rs, 5, lambda dc: xt[:, dc, :], b_xt,
                                  lambda dc: msb[:, dc, :], b_msb, tmp, b_tmp)
                P.add("sp", (lambda n: lambda e: e.dma_start(out=fm(xmid1)[:, :, n * 512:(n + 1) * 512], in_=msb[:, :, :]))(n),
                      reads=[b_msb], dma=True, key="st_msb")
            P.emit()

        if _STOP < 7:
            return nc
        if _TESTF == 3:
            ffn_phase(1, xmid0, x1s, False)
            return nc
        if _TESTF == 4:
            ffn_phase(0, xmid1, x1s, False)
            return nc
        ffn_phase(1, xmid1, yT, True)
    return nc


_CACHE = {}


def _prep_inputs(S, inp):
    f32 = np.float32
    T = S // 4
    TL = T + HALO
    NL = TL // 512
    NT = S // 512
    x = np.asarray(inp["x"], f32)
    wqkv_full = np.asarray(inp["attn_w_qkv"], f32)[0]
    pos = np.arange(S)
    c7, a3, b4 = pos // 128, (pos % 128) // 16, pos % 16
    tril = (np.arange(128)[:, None] <= np.arange(128)[None, :]).astype(f32)
    ii = np.arange(128)[:, None, None]
    vv = np.arange(4)[None, :, None]
    jj = np.arange(512)[None, None, :]
    maskd = np.where(jj >= 128 * vv + ii, 0.0, NEG).astype(f32)
    gl = [inp["norm_mix_pre"][0], inp["norm_mix_post"][0], inp["norm_ffn_pre"][0], inp["norm_ffn_post"][0],
          inp["norm_mix_pre"][1], inp["norm_mix_post"][1], inp["norm_ffn_pre"][1], inp["norm_ffn_post"][1]]
    gains = np.ascontiguousarray(np.stack([np.asarray(g, f32).reshape(8, 128).T for g in gl], axis=1))
    lamv = np.stack([inp["attn_lambda_q1"][0], inp["attn_lambda_k1"][0], inp["attn_lambda_q2"][0], inp["attn_lambda_k2"][0]])[None].astype(f32)
    subln = np.asarray(inp["attn_subln"], f32)[0].reshape(128, 1)
    w_o = np.ascontiguousarray(np.asarray(inp["attn_w_o"], f32)[0].reshape(8, 128, D).transpose(1, 0, 2))
    w_up = np.ascontiguousarray(np.asarray(inp["ffn_w_up"], f32).reshape(2, 8, 128, 2 * FF).transpose(0, 2, 1, 3))
    w_dn = np.ascontiguousarray(np.asarray(inp["ffn_w_down"], f32).reshape(2, NFC, 128, D).transpose(0, 2, 1, 3))
    convw = np.ascontiguousarray(np.asarray(inp["ffn_conv_w"], f32).reshape(2, 3, NFC, 128).transpose(0, 3, 2, 1))
    convb = np.ascontiguousarray(np.asarray(inp["ffn_conv_b"], f32).reshape(2, NFC, 128).transpose(0, 2, 1))
    w_in = np.ascontiguousarray(np.asarray(inp["sgu_w_in"], f32)[0].reshape(8, 128, 2 * E).transpose(1, 0, 2))
    lng = np.ascontiguousarray(np.broadcast_to(np.asarray(inp["sgu_ln_g"], f32)[0][None, :], (128, E)))
    lnb = np.ascontiguousarray(np.broadcast_to(np.asarray(inp["sgu_ln_b"], f32)[0][None, :], (128, E)))
    wsT = np.ascontiguousarray(np.asarray(inp["sgu_w_s"], f32)[0].transpose(2, 0, 1))
    bsb = np.ascontiguousarray(np.broadcast_to(np.asarray(inp["sgu_b_s"], f32)[0][None], (128, 8, 128)))
    w_out = np.ascontiguousarray(np.asarray(inp["sgu_w_out"], f32)[0].reshape(NEC, 128, D).transpose(1, 0, 2))
    xT = [np.ascontiguousarray(x[b].T) for b in range(2)]
    maps = []
    for c in range(8):
        b, r = c // 4, c % 4
        t0 = r * T - HALO
        xl = np.zeros((D, TL), f32)
        lo = max(t0, 0)
        xl[:, lo - t0:] = xT[b][:, lo:t0 + TL]
        valid0 = np.ones((128, 512), f32) if r > 0 else np.zeros((128, 512), f32)
        wq = np.zeros((2, 5, D, 128), f32)
        ka = np.zeros((2, 6, S), ml_dtypes.bfloat16)
        qa = np.zeros((2, 6, S), ml_dtypes.bfloat16)
        for s in range(2):
            hd = r if s == 0 else 4 + r
            slope = 2.0 ** (-(hd + 1))
            for part, base in ((0, 0), (1, D)):
                A = wqkv_full[:, base + hd * 128: base + (hd + 1) * 128]
                wq[s, 2 * part + 0] = A
                wq[s, 2 * part + 1] = np.concatenate([A[:, 64:], A[:, :64]], axis=1)
            wq[s, 4] = wqkv_full[:, 2 * D + hd * 128: 2 * D + (hd + 1) * 128]
            ka[s, 0:3] = 1.0
            ka[s, 3] = (slope * 128.0 * c7).astype(ml_dtypes.bfloat16)
            ka[s, 4] = (slope * 16.0 * a3).astype(ml_dtypes.bfloat16)
            ka[s, 5] = (slope * b4).astype(ml_dtypes.bfloat16)
            qa[s, 0] = (-slope * 128.0 * c7).astype(ml_dtypes.bfloat16)
            qa[s, 1] = (-slope * 16.0 * a3).astype(ml_dtypes.bfloat16)
            qa[s, 2] = (-slope * b4).astype(ml_dtypes.bfloat16)
            qa[s, 3:6] = 1.0
        idx = np.zeros((128, NL * 8), np.int32)
        for n in range(NL):
            tile = max(r * (T // 512) + n - 1, 0)
            for hd in range(8):
                rank, s = hd % 4, hd // 4
                unit = s * NT + tile
                idx[:, n * 8 + hd] = ((unit // 4) * 4 + rank) * 512 + (unit % 4) * 128 + np.arange(128)
        maps.append({
            "xT_full": xT[b], "xT_loc": xl, "valid0": valid0, "kaug0": ka[0], "kaug1": ka[1], "qaug0": qa[0], "qaug1": qa[1], "maskd": maskd,
            **{f"wqkv{s_}_{w_}": np.ascontiguousarray(wq[s_, w_]) for s_ in range(2) for w_ in range(5)},
            "lamv": lamv, "subln": subln, "gains": gains, "idxtab": idx, "w_o": w_o, "w_up0": w_up[0], "w_up1": w_up[1], "w_dn0": w_dn[0], "w_dn1": w_dn[1],
            "convw0": convw[0], "convw1": convw[1], "convb0": convb[0], "convb1": convb[1], "w_in": w_in, "lng": lng, "lnb": lnb, "wsT": wsT, "tril": tril,
            "bsb": bsb, "w_out": w_out,
        })
    return maps


def kernel(**inputs):
    x = inputs["x"]
    S = x.shape[1]
    if S not in _CACHE:
        _CACHE[S] = build_program(S)
    nc = _CACHE[S]
    maps = _prep_inputs(S, inputs)
    res = run_bass_kernel_spmd(nc, maps, core_ids=list(range(8)))
    T = S // 4
    out = np.zeros((2, S, D), np.float32)
    for c in range(8):
        b, r = c // 4, c % 4
        out[b, r * T:(r + 1) * T, :] = np.asarray(res.results[c]["yT"]).T
    return out
```

```python
import contextlib
import numpy as np
import ml_dtypes
import concourse.bass as bass
import concourse.mybir as mybir
from concourse.bass_utils import run_bass_kernel_spmd

F32 = mybir.dt.float32
BF16 = mybir.dt.bfloat16
I32 = mybir.dt.int32
AF = mybir.ActivationFunctionType
ALU = mybir.AluOpType
AX = mybir.AxisListType

D = 1024
H = 8
DH = 64
FF = 2816
NFC = FF // 128
E = 2048
NEC = E // 128
NORM_EPS = 1e-6
LN_EPS = 1e-5
LAM_INIT = 0.2
WIN_A = 8
HALO = 512
NEG = -30000.0
_STOP = 99
_SKIP0 = True
_TESTF = 0
_ABL = 0
_OUTINT = False


class Buf:
    __slots__ = ("name", "lw", "rs")

    def __init__(self, name):
        self.name = name
        self.lw = None
        self.rs = []


class Op:
    __slots__ = ("eng", "fn", "dma", "key", "deps", "sig", "seq", "inc")

    def __init__(self, eng, fn, dma, key, inc):
        self.eng, self.fn, self.dma, self.key, self.inc = eng, fn, dma, key, inc
        self.deps = []
        self.sig = dma
        self.seq = 0


class Prog:
    ENGS = ("sp", "act", "dve", "pool", "pe")

    def __init__(self, nc, tag, semstack):
        self.nc = nc
        self.tag = tag
        self.semstack = semstack
        self.ops = []
        self.lastkey = {}

    def add(self, eng, fn, reads=(), writes=(), dma=False, key=None, inc=16):
        if dma and key is None:
            key = writes[0].name
        op = Op(eng, fn, dma, key, inc if dma else 1)
        hard, war = [], []
        for b in reads:
            if b.lw is not None:
                hard.append(b.lw)
        for b in writes:
            if b.lw is not None:
                hard.append(b.lw)
            war.extend(b.rs)
        deps = {}
        for d in hard:
            same = (d.eng == eng) and not d.dma and not dma
            if same and eng == "pe":
                continue
            deps[id(d)] = d
        for d in war:
            same = (d.eng == eng) and not d.dma and not dma
            if same:
                continue
            deps[id(d)] = d
        if dma and key in self.lastkey:
            d = self.lastkey[key]
            deps[id(d)] = d
        if dma:
            self.lastkey[key] = op
        for d in deps.values():
            d.sig = True
            op.deps.append(d)
        for b in reads:
            b.rs.append(op)
        for b in writes:
            b.lw = op
            b.rs = []
        self.ops.append(op)
        return op

    def emit(self):
        nc = self.nc
        st = self.semstack
        if True:
            esem = {e: st.enter_context(nc.semaphore(f"{self.tag}_{e}")) for e in self.ENGS}
            keys = []
            for op in self.ops:
                if op.dma and op.key not in keys:
                    keys.append(op.key)
            ksem = {k: st.enter_context(nc.semaphore(f"{self.tag}_k{i}")) for i, k in enumerate(keys)}
            cnt = {e: 0 for e in self.ENGS}
            kcnt = {k: 0 for k in keys}
            for op in self.ops:
                if op.dma:
                    kcnt[op.key] += op.inc
                    op.seq = kcnt[op.key]
                elif op.sig:
                    cnt[op.eng] += 1
                    op.seq = cnt[op.eng]
            per = {e: [o for o in self.ops if o.eng == e] for e in self.ENGS}

            def run(ename, e):
                waited = {}
                for op in per[ename]:
                    for d in op.deps:
                        sem = ksem[d.key] if d.dma else esem[d.eng]
                        sid = id(sem)
                        if waited.get(sid, 0) >= d.seq:
                            continue
                        e.wait_ge(sem, d.seq)
                        waited[sid] = d.seq
                    ins = op.fn(e)
                    if op.dma:
                        ins.then_inc(ksem[op.key], op.inc)
                    elif op.sig:
                        ins.then_inc(esem[op.eng], 1)
                if ename == "sp":
                    for k in keys:
                        if kcnt[k] > 0:
                            e.wait_ge(ksem[k], kcnt[k])

            with nc.Block() as block:
                @block.sync
                def _(e):
                    run("sp", e)

                @block.scalar
                def _(e):
                    run("act", e)

                @block.vector
                def _(e):
                    run("dve", e)

                @block.gpsimd
                def _(e):
                    run("pool", e)

                @block.tensor
                def _(e):
                    run("pe", e)


def build_program(S):
    NT = S // 512
    NKB = S // 128
    T = S // 4
    TL = T + HALO
    NL = TL // 512

    nc = bass.Bass("TRN2", target_bir_lowering=False)
    din = lambda n, s, d: nc.dram_tensor(n, s, d, kind="ExternalInput")
    xT_full = din("xT_full", [D, S], F32)
    xT_loc = din("xT_loc", [D, TL], F32)
    valid0 = din("valid0", [128, 512], F32)
    wqkvL = [[din(f"wqkv{s_}_{w_}", [D, 128], F32) for w_ in range(5)] for s_ in range(2)]
    kaugL = [din(f"kaug{s_}", [6, S], BF16) for s_ in range(2)]
    qaugL = [din(f"qaug{s_}", [6, S], BF16) for s_ in range(2)]
    maskd = din("maskd", [128, 4, 512], F32)
    lamv = din("lamv", [1, 4, 64], F32)
    subln = din("subln", [128, 1], F32)
    gains = din("gains", [128, 8, 8], F32)
    idxtab = din("idxtab", [128, NL * 8], I32)
    w_o = din("w_o", [128, 8, D], F32)
    w_upL = [din(f"w_up{l}", [128, 8, 2 * FF], F32) for l in range(2)]
    w_dnL = [din(f"w_dn{l}", [128, NFC, D], F32) for l in range(2)]
    convwL = [din(f"convw{l}", [128, NFC, 3], F32) for l in range(2)]
    convbL = [din(f"convb{l}", [128, NFC], F32) for l in range(2)]
    w_in = din("w_in", [128, 8, 2 * E], F32)
    lng = din("lng", [128, E], F32)
    lnb = din("lnb", [128, E], F32)
    wsT = din("wsT", [128, 8, 128], F32)
    tril = din("tril", [128, 128], F32)
    bsb = din("bsb", [128, 8, 128], F32)
    w_out = din("w_out", [128, NEC, D], F32)
    yT = nc.dram_tensor("yT", [D, T], F32, kind="ExternalOutput")

    ag_in = nc.dram_tensor("ag_in", [2 * NT * 128, 512], BF16)
    ag_out = nc.dram_tensor("ag_out", [4 * 2 * NT * 128, 512], BF16)
    xmid0 = nc.dram_tensor("xmid0", [D, TL], F32)
    x1s = nc.dram_tensor("x1s", [D, TL], F32)
    xmid1 = nc.dram_tensor("xmid1", [D, TL], F32)

    def fm(t):
        return t.ap().rearrange("(c p) t -> p c t", p=128)

    with contextlib.ExitStack() as top:
        _uid = iter(range(1 << 30))
        sb = lambda st, n, s, d: st.enter_context(nc.sbuf_tensor(f"{n}_u{next(_uid)}", s, d))
        ps = top.enter_context(nc.psum_tensor("ps", [128, 8 * 512], F32))
        PB = [Buf(f"ps{i}") for i in range(8)]

        def newP(tag):
            for b_ in PB:
                b_.lw = None
                b_.rs = []
            return Prog(nc, tag, top)

        def bank(i, rows=128, cols=512, c0=0):
            return ps[0:rows, i * 512 + c0:i * 512 + c0 + cols]

        ones_m = sb(top, "ones_m", [128, 128], BF16)
        ones_v = sb(top, "ones_v", [128, 128], BF16)
        ones_f = sb(top, "ones_f", [128, 128], F32)
        ones_b = sb(top, "ones_b", [128, 128], BF16)
        gn = sb(top, "gn", [128, 8, 8], F32)
        neglam = sb(top, "neglam", [128, 1], F32)
        subg = sb(top, "subg", [128, 1], F32)
        B_const = Buf("const")

        with contextlib.ExitStack() as st:
            P = newP("c0")
            lv = sb(st, "lv", [1, 4, 64], F32)
            pr = sb(st, "pr", [1, 2, 64], F32)
            sm = sb(st, "sm", [1, 2], F32)
            ex = sb(st, "ex", [1, 2], F32)
            nl = sb(st, "nl", [1, 1], F32)
            b_lv, b_pr, b_sm, b_ex, b_nl = (Buf(n) for n in ("lv", "pr", "sm", "ex", "nl"))
            b_gn, b_sub, b_ones = Buf("gn"), Buf("sub"), Buf("ones")
            P.add("sp", lambda e: e.dma_start(out=lv[:, :, :], in_=lamv.ap()), writes=[b_lv], dma=True, key="c")
            P.add("sp", lambda e: e.dma_start(out=gn[:, :, :], in_=gains.ap()), writes=[b_gn], dma=True, key="c")
            P.add("sp", lambda e: e.dma_start(out=subg[:, :], in_=subln.ap()), writes=[b_sub], dma=True, key="c")
            P.add("pool", lambda e: e.memset(ones_m[:, :], 1.0 / 1024.0), writes=[b_ones])
            P.add("pool", lambda e: e.memset(ones_v[:, :], 1.0 / 128.0), writes=[b_ones])
            P.add("pool", lambda e: e.memset(ones_f[:, :], 1.0), writes=[b_ones])
            P.add("pool", lambda e: e.memset(ones_b[:, :], 1.0), writes=[b_ones])
            P.add("dve", lambda e: e.tensor_tensor(out=pr[:, 0, :], in0=lv[:, 0, :], in1=lv[:, 1, :], op=ALU.mult),
                  reads=[b_lv], writes=[b_pr])
            P.add("dve", lambda e: e.tensor_tensor(out=pr[:, 1, :], in0=lv[:, 2, :], in1=lv[:, 3, :], op=ALU.mult),
                  reads=[b_lv], writes=[b_pr])
            P.add("dve", lambda e: e.reduce_sum(out=sm[:, 0:1], in_=pr[:, 0, :], axis=AX.X), reads=[b_pr], writes=[b_sm])
            P.add("dve", lambda e: e.reduce_sum(out=sm[:, 1:2], in_=pr[:, 1, :], axis=AX.X), reads=[b_pr], writes=[b_sm])
            P.add("act", lambda e: e.activation(out=ex[:, :], in_=sm[:, :], func=AF.Exp), reads=[b_sm], writes=[b_ex])
            P.add("dve", lambda e: e.scalar_tensor_tensor(out=nl[:, :], in0=ex[:, 1:2], scalar=-LAM_INIT, in1=ex[:, 0:1],
                                                         op0=ALU.add, op1=ALU.subtract), reads=[b_ex], writes=[b_nl])
            P.add("pe", lambda e: e.matmul(bank(0, 128, 1), lhsT=ones_f[0:1, :], rhs=nl[0:1, 0:1], start=True, stop=True),
                  reads=[b_nl, b_ones], writes=[PB[0]])
            P.add("dve", lambda e: e.tensor_copy(out=neglam[:, :], in_=bank(0, 128, 1)), reads=[PB[0]], writes=[B_const])
            P.add("dve", lambda e: e.tensor_scalar(out=subg[:, :], in0=subg[:, :], scalar1=1.0 - LAM_INIT, scalar2=None,
                                                   op0=ALU.mult), reads=[b_sub], writes=[b_sub])
            P.emit()

        def rstd_from_sq(P, sq_ap_fn, nch, ones, bankid, sq_buf, tmp, rstd, b_tmp, b_rstd, eps):
            for c in range(nch):
                P.add("pe", (lambda c: lambda e: e.matmul(bank(bankid), lhsT=ones[:, :], rhs=sq_ap_fn(c),
                                                            start=(c == 0), stop=(c == nch - 1)))(c),
                      reads=[sq_buf], writes=[PB[bankid]])
            P.add("act", lambda e: e.activation(out=tmp[:, :], in_=bank(bankid), func=AF.Sqrt, bias=eps, scale=1.0),
                  reads=[PB[bankid]], writes=[b_tmp])
            P.add("dve", lambda e: e.reciprocal(out=rstd[:, :], in_=tmp[:, :]), reads=[b_tmp], writes=[b_rstd])

        for slot in range(2):
            if _STOP < 1 + slot:
                return nc
            with contextlib.ExitStack() as st:
                KT = [sb(st, f"KT{m}", [70, S], BF16) for m in range(2)]
                Vt = sb(st, "Vt", [128, S], BF16)
                wq = [sb(st, f"wq{m}", [128, 8, 128], BF16) for m in range(2)]
                wk = [sb(st, f"wk{m}", [128, 8, 128], BF16) for m in range(2)]
                wv = sb(st, "wv", [128, 8, 128], BF16)
                stg = sb(st, "stg", [128, 8, 128], F32)
                xt = [sb(st, "xt0", [128, 8, 512], F32)] * 2
                xsq = sb(st, "xsq", [128, 8, 512], BF16)
                xn = [sb(st, f"xn{i}", [128, 8, 512], BF16) for i in range(2)]
                rtmp = sb(st, "rtmp", [128, 512], F32)
                rstd = sb(st, "rstd", [128, 512], F32)
                QT = [[sb(st, f"QT{m}_{i}", [70, 512], BF16) for i in range(2)] for m in range(2)]
                PT = [sb(st, f"PT{i}", [128, 1024], BF16) for i in range(3)]
                tmpS = [sb(st, f"tmpS{i}", [128, 1024], F32) for i in range(2)]
                acc = [sb(st, f"acc{m}", [128, 512], F32) for m in range(2)]
                mk = sb(st, "mk", [128, 4, 512], F32)
                rr = [sb(st, f"rr{m}", [128, 512], F32) for m in range(2)]
                tt = [sb(st, f"tt{m}", [128, 512], F32) for m in range(2)]
                dd = sb(st, "dd", [128, 512], F32)
                dsq = sb(st, "dsq", [128, 512], BF16)
                r2t = sb(st, "r2t", [128, 512], F32)
                r2 = sb(st, "r2", [128, 512], F32)
                oT = [sb(st, f"oT{i}", [128, 512], BF16) for i in range(2)]

                P = newP(f"a{slot}")
                b_KT = [Buf(f"KT{m}") for m in range(2)]
                b_KTa = Buf("KTaug")
                b_V = Buf("V")
                b_w = Buf("w")
                b_stg = Buf("stg")
                b_xt = [Buf("xt0")] * 2
                b_xsq = Buf("xsq")
                b_xn = [Buf(f"xn{i}") for i in range(2)]
                b_rtmp, b_rstd = Buf("rtmp"), Buf("rstd")
                b_QT = [[Buf(f"QT{m}_{i}") for i in range(2)] for m in range(2)]
                b_QTa = [[Buf(f"QTa{m}_{i}") for i in range(2)] for m in range(2)]
                b_PT = [[Buf(f"PT{i}_{m_}") for m_ in range(2)] for i in range(3)]
                b_tmpS = [[Buf(f"tmpS{i}_{m_}") for m_ in range(2)] for i in range(2)]
                b_acc = [Buf(f"acc{m}") for m in range(2)]
                b_mk = Buf("mk")
                b_rr = [Buf(f"rr{m}") for m in range(2)]
                b_tt = [Buf(f"tt{m}") for m in range(2)]
                b_dd, b_dsq, b_r2t, b_r2 = Buf("dd"), Buf("dsq"), Buf("r2t"), Buf("r2")
                b_oT = [Buf(f"oT{i}") for i in range(2)]

                dests = [wq[0], wq[1], wk[0], wk[1], wv]
                for wi in range(5):
                    P.add("sp", (lambda wi: lambda e: e.dma_start(
                        out=stg[:, :, :], in_=wqkvL[slot][wi].ap().rearrange("(c p) n -> p c n", p=128)))(wi),
                        writes=[b_stg], dma=True, key="ld")
                    for c in range(8):
                        sc2 = 0.125 if wi < 2 else 1.0
                        P.add("dve", (lambda wi, c, sc2: lambda e: e.tensor_scalar(
                            out=dests[wi][:, c, :], in0=stg[:, c, :], scalar1=gn[:, 0, c:c + 1], scalar2=sc2,
                            op0=ALU.mult, op1=ALU.mult))(wi, c, sc2), reads=[b_stg], writes=[b_w])
                for m in range(2):
                    P.add("sp", (lambda m: lambda e: e.dma_start(out=KT[m][64:70, :], in_=kaugL[slot].ap()))(m),
                          writes=[b_KTa], dma=True, key="ld")
                P.add("sp", lambda e: e.dma_start(out=mk[:, :, :], in_=maskd.ap()), writes=[b_mk], dma=True, key="ld")

                def prep(g):
                    i = g % 2
                    P.add("sp", lambda e: e.dma_start(out=xt[i][:, :, :], in_=fm(xT_full)[:, :, g * 512:(g + 1) * 512]),
                          writes=[b_xt[i]], dma=True)
                    P.add("act", lambda e: e.activation(out=xsq[:, :, :], in_=xt[i][:, :, :], func=AF.Square),
                          reads=[b_xt[i]], writes=[b_xsq])
                    rstd_from_sq(P, lambda c: xsq[:, c, :], 8, ones_m, 7, b_xsq, rtmp, rstd, b_rtmp, b_rstd, NORM_EPS)
                    for c in range(8):
                        P.add("dve", (lambda c: lambda e: e.tensor_tensor(out=xn[i][:, c, :], in0=xt[i][:, c, :],
                                                                         in1=rstd[:, :], op=ALU.mult))(c),
                              reads=[b_xt[i], b_rstd], writes=[b_xn[i]])
                    for m in range(2):
                        P.add("sp", (lambda m: lambda e: e.dma_start(out=QT[m][i][64:70, :],
                                                                      in_=qaugL[slot].ap()[:, g * 512:(g + 1) * 512]))(m),
                              writes=[b_QTa[m][i]], dma=True, key=f"qa{i}")

                def qkv(g):
                    i = g % 2
                    bk = [6, 7]
                    n = 0
                    for m in range(2):
                        for kind in range(2):
                            w = wq[m] if kind == 0 else wk[m]
                            b = bk[n % 2]
                            n += 1
                            for c in range(8):
                                P.add("pe", (lambda w, b, c: lambda e: e.matmul(bank(b), lhsT=w[:, c, :], rhs=xn[i][:, c, :],
                                                                                 start=(c == 0), stop=(c == 7)))(w, b, c),
                                      reads=[b_w, b_xn[i]], writes=[PB[b]])
                            if kind == 0:
                                P.add("act", (lambda m, b: lambda e: e.activation(out=QT[m][i][0:64, :], in_=bank(b, 64),
                                                                                   func=AF.Copy))(m, b),
                                      reads=[PB[b]], writes=[b_QT[m][i]])
                            else:
                                P.add("act", (lambda m, b: lambda e: e.activation(
                                    out=KT[m][0:64, g * 512:(g + 1) * 512], in_=bank(b, 64), func=AF.Copy))(m, b),
                                    reads=[PB[b]], writes=[b_KT[m]])
                    b = bk[n % 2]
                    for j in range(4):
                        for c in range(8):
                            P.add("pe", (lambda b, j, c: lambda e: e.matmul(
                                bank(b, 128, 128, j * 128), lhsT=xn[i][:, c, j * 128:(j + 1) * 128], rhs=wv[:, c, :],
                                start=(c == 0), stop=(c == 7)))(b, j, c),
                                reads=[b_w, b_xn[i]], writes=[PB[b]])
                    P.add("act", (lambda b: lambda e: e.activation(out=Vt[:, g * 512:(g + 1) * 512], in_=bank(b), func=AF.Copy))(b),
                          reads=[PB[b]], writes=[b_V])

                ucount = [0]

                def attention(g):
                    i = g % 2
                    kb0 = max(0, 4 * g - WIN_A) if slot == 0 else 0
                    kbl = 4 * g + 3
                    kbs = list(range(kb0, kbl + 1))
                    us = []
                    for _ in kbs:
                        us.append(ucount[0])
                        ucount[0] += 1

                    def emit_qk(kb, u):
                        pb = u % 2
                        for m in range(2):
                            P.add("pe", (lambda m, kb, pb: lambda e: e.matmul(
                                bank(2 * pb + m), lhsT=KT[m][0:70, kb * 128:(kb + 1) * 128], rhs=QT[m][i][0:70, :],
                                start=True, stop=True))(m, kb, pb),
                                reads=[b_KT[m], b_KTa, b_QT[m][i], b_QTa[m][i]], writes=[PB[2 * pb + m]])

                    def emit_exp(kb, u):
                        pb = u % 2
                        pt = u % 3
                        for m in range(2):
                            if kb >= 4 * g:
                                v = kb - 4 * g
                                P.add("dve", (lambda m, pb, v: lambda e: e.tensor_tensor(
                                    out=tmpS[pb][:, m * 512:(m + 1) * 512], in0=bank(2 * pb + m), in1=mk[:, v, :],
                                    op=ALU.add))(m, pb, v),
                                    reads=[PB[2 * pb + m], b_mk], writes=[b_tmpS[pb][m]])
                                P.add("act", (lambda m, pb, pt: lambda e: e.activation(
                                    out=PT[pt][:, m * 512:(m + 1) * 512], in_=tmpS[pb][:, m * 512:(m + 1) * 512], func=AF.Exp))(m, pb, pt),
                                    reads=[b_tmpS[pb][m]], writes=[b_PT[pt][m]])
                            else:
                                P.add("act", (lambda m, pb, pt: lambda e: e.activation(
                                    out=PT[pt][:, m * 512:(m + 1) * 512], in_=bank(2 * pb + m), func=AF.Exp))(m, pb, pt),
                                    reads=[PB[2 * pb + m]], writes=[b_PT[pt][m]])

                    def emit_pv(kb, u):
                        pt = u % 3
                        for m in range(2):
                            P.add("pe", (lambda m, kb, pt: lambda e: e.matmul(
                                bank(4 + m), lhsT=Vt[:, kb * 128:(kb + 1) * 128], rhs=PT[pt][:, m * 512:(m + 1) * 512],
                                start=(kb == kb0), stop=(kb == kbl)))(m, kb, pt),
                                reads=[b_V, b_PT[pt][m]], writes=[PB[4 + m]])
                        if kb == kb0:
                            P.add("dve", (lambda pt: lambda e: e.tensor_copy(out=acc[0][:, :], in_=PT[pt][:, 0:512]))(pt),
                                  reads=[b_PT[pt][0]], writes=[b_acc[0]])
                        else:
                            P.add("dve", (lambda pt: lambda e: e.tensor_tensor(
                                out=acc[0][:, :], in0=acc[0][:, :], in1=PT[pt][:, 0:512], op=ALU.add))(pt),
                                reads=[b_PT[pt][0], b_acc[0]], writes=[b_acc[0]])
                        P.add("pe", (lambda kb, pt: lambda e: e.matmul(
                            bank(6), lhsT=ones_b[:, :], rhs=PT[pt][:, 512:1024], start=(kb == kb0), stop=(kb == kbl)))(kb, pt),
                            reads=[b_PT[pt][1]], writes=[PB[6]])

                    emit_qk(kbs[0], us[0])
                    for n_, kb in enumerate(kbs):
                        if n_ + 1 < len(kbs):
                            emit_qk(kbs[n_ + 1], us[n_ + 1])
                        emit_exp(kb, us[n_])
                        emit_pv(kb, us[n_])

                def finalize(g):
                    i = g % 2
                    P.add("pe", lambda e: e.matmul(bank(7), lhsT=ones_f[:, :], rhs=acc[0][:, :], start=True, stop=True),
                          reads=[b_acc[0]], writes=[PB[7]])
                    for m in range(2):
                        P.add("dve", (lambda m: lambda e: e.reciprocal(out=rr[m][:, :], in_=bank(7 - m)))(m),
                              reads=[PB[7 - m]], writes=[b_rr[m]])
                        P.add("dve", (lambda m: lambda e: e.tensor_tensor(out=tt[m][:, :], in0=bank(4 + m), in1=rr[m][:, :],
                                                                         op=ALU.mult))(m),
                              reads=[PB[4 + m], b_rr[m]], writes=[b_tt[m]])
                    P.add("dve", lambda e: e.scalar_tensor_tensor(out=dd[:, :], in0=tt[1][:, :], scalar=neglam[:, 0:1],
                                                                 in1=tt[0][:, :], op0=ALU.mult, op1=ALU.add),
                          reads=[b_tt[0], b_tt[1]], writes=[b_dd])
                    P.add("act", lambda e: e.activation(out=dsq[:, :], in_=dd[:, :], func=AF.Square), reads=[b_dd], writes=[b_dsq])
                    rstd_from_sq(P, lambda c: dsq[:, :], 1, ones_v, 6, b_dsq, r2t, r2, b_r2t, b_r2, NORM_EPS)
                    P.add("dve", lambda e: e.scalar_tensor_tensor(out=oT[i][:, :], in0=dd[:, :], scalar=subg[:, 0:1],
                                                                 in1=r2[:, :], op0=ALU.mult, op1=ALU.mult),
                          reads=[b_dd, b_r2], writes=[b_oT[i]])
                    row0 = (slot * NT + g) * 128
                    P.add("sp", lambda e: e.dma_start(out=ag_in.ap()[row0:row0 + 128, :], in_=oT[i][:, :]),
                          reads=[b_oT[i]], dma=True, key=f"st_oT{i}")

                prep(0)
                for g in range(NT):
                    qkv(g)
                    if g + 1 < NT:
                        prep(g + 1)
                    attention(g)
                    finalize(g)
                P.emit()

        if _STOP < 3:
            return nc
        if True:
            csem = top.enter_context(nc.semaphore("csem"))
            with nc.Block() as block:
                @block.gpsimd
                def _(g):
                    RP = 512
                    for k in range(2 * NT * 128 // RP):
                        g.collective_compute("AllGather", ALU.bypass, replica_groups=[[0, 1, 2, 3], [4, 5, 6, 7]],
                                             ins=[ag_in.ap()[k * RP:(k + 1) * RP, :].opt()],
                                             outs=[ag_out.ap()[k * 4 * RP:(k + 1) * 4 * RP, :].opt()]).then_inc(csem)
                        g.wait_ge(csem, k + 1)

        if _STOP < 4:
            return nc
        def postnorm_residual(P, msb, b_msb, msq, b_msq, rt, rs, b_rt, b_rs, gvec, resid_fn, b_resid, out_fn, b_out, tmp, b_tmp2):
            rstd_from_sq(P, lambda c: msq[:, c, :], 8, ones_m, 7, b_msq, rt, rs, b_rt, b_rs, NORM_EPS)
            for dc in range(8):
                P.add("dve", (lambda dc: lambda e: e.scalar_tensor_tensor(
                    out=tmp[:, :], in0=msb[:, dc, :], scalar=gn[:, gvec, dc:dc + 1], in1=rs[:, :],
                    op0=ALU.mult, op1=ALU.mult))(dc), reads=[b_msb, b_rs], writes=[b_tmp2])
                P.add("dve", (lambda dc: lambda e: e.tensor_tensor(out=out_fn(dc), in0=tmp[:, :], in1=resid_fn(dc), op=ALU.add))(dc),
                      reads=[b_tmp2, b_resid], writes=[b_out])

        with contextlib.ExitStack() as st:
            wo = sb(st, "wo", [128, 8, D], BF16)
            b_wo = Buf("wo")
            with contextlib.ExitStack() as s2:
                stg = sb(s2, "stgo", [128, 8, D], F32)
                b_stg = Buf("stg")
                P = newP("wo0")
                P.add("sp", lambda e: e.dma_start(out=stg[:, :, :], in_=w_o.ap()), writes=[b_stg], dma=True)
                for c in range(8):
                    P.add("dve", (lambda c: lambda e: e.tensor_copy(out=wo[:, c, :], in_=stg[:, c, :]))(c),
                          reads=[b_stg], writes=[b_wo])
                P.emit()
            idx = sb(st, "idx", [128, NL * 8], I32)
            og = sb(st, "og", [128, 8, 512], BF16)
            xt = sb(st, "xtw", [128, 8, 512], F32)
            msb = sb(st, "msb", [128, 8, 512], F32)
            msq = sb(st, "msq", [128, 8, 512], BF16)
            rt = sb(st, "rtw", [128, 512], F32)
            rs = sb(st, "rsw", [128, 512], F32)
            tmp = sb(st, "tmpw", [128, 512], F32)
            b_idx, b_og, b_xt, b_msb, b_msq, b_rt, b_rs, b_tmp = (Buf(n) for n in ("idx", "og", "xt", "msb", "msq", "rt", "rs", "tmp"))
            P = newP("wo1")
            P.add("sp", lambda e: e.dma_start(out=idx[:, :], in_=idxtab.ap()), writes=[b_idx], dma=True, key="ld")
            for n in range(NL):
                for hd in range(8):
                    P.add("pool", (lambda n, hd: lambda e: e.indirect_dma_start(
                        out=og[:, hd, :], out_offset=None, in_=ag_out.ap(),
                        in_offset=bass.IndirectOffsetOnAxis(ap=idx[:, n * 8 + hd:n * 8 + hd + 1], axis=0)))(n, hd),
                        reads=[b_idx], writes=[b_og], dma=True, key="og")
                P.add("sp", (lambda n: lambda e: e.dma_start(out=xt[:, :, :], in_=fm(xT_loc)[:, :, n * 512:(n + 1) * 512]))(n),
                      writes=[b_xt], dma=True)
                for dc in range(8):
                    b = dc % 2
                    for hd in range(8):
                        P.add("pe", (lambda b, dc, hd: lambda e: e.matmul(bank(b), lhsT=wo[:, hd, dc * 128:(dc + 1) * 128],
                                                                          rhs=og[:, hd, :], start=(hd == 0), stop=(hd == 7)))(b, dc, hd),
                              reads=[b_og], writes=[PB[b]])
                    P.add("act", (lambda b, dc: lambda e: e.activation(out=msb[:, dc, :], in_=bank(b), func=AF.Copy))(b, dc),
                          reads=[PB[b]], writes=[b_msb])
                    P.add("act", (lambda b, dc: lambda e: e.activation(out=msq[:, dc, :], in_=bank(b), func=AF.Square))(b, dc),
                          reads=[PB[b]], writes=[b_msq])
                postnorm_residual(P, msb, b_msb, msq, b_msq, rt, rs, b_rt, b_rs, 1, lambda dc: xt[:, dc, :], b_xt,
                                  lambda dc: msb[:, dc, :], b_msb, tmp, b_tmp)
                P.add("sp", (lambda n: lambda e: e.dma_start(out=fm(xmid0)[:, :, n * 512:(n + 1) * 512], in_=msb[:, :, :]))(n),
                      reads=[b_msb], dma=True, key="st_msb")
            P.emit()

        if _STOP < 5:
            return nc
        def ffn_phase(layer, src, dst, dst_is_out):
            gpre, gpost = (2, 3) if layer == 0 else (6, 7)
            with contextlib.ExitStack() as st:
                wup = sb(st, "wup", [128, 8, 2 * FF], BF16)
                wdn = sb(st, "wdn", [128, NFC, D], BF16)
                cw = sb(st, "cw", [128, NFC, 3], F32)
                cb = sb(st, "cb", [128, NFC], F32)
                vl0 = sb(st, "vl0", [128, 512], F32)
                b_w = Buf("w")
                with contextlib.ExitStack() as s2:
                    stg = [sb(s2, f"stgf{i}", [128, 2 * FF], F32) for i in range(2)]
                    b_stg = [Buf(f"stg{i}") for i in range(2)]
                    P = newP(f"fw{layer}")
                    P.add("sp", lambda e: e.dma_start(out=cw[:, :, :], in_=convwL[layer].ap()), writes=[b_w], dma=True, key="misc")
                    P.add("sp", lambda e: e.dma_start(out=cb[:, :], in_=convbL[layer].ap()), writes=[b_w], dma=True, key="misc")
                    P.add("sp", lambda e: e.dma_start(out=vl0[:, :], in_=valid0.ap()), writes=[b_w], dma=True, key="misc")
                    for c in range(8):
                        i = c % 2
                        P.add("sp", (lambda c, i: lambda e: e.dma_start(out=stg[i][:, :], in_=w_upL[layer].ap()[:, c, :]))(c, i),
                              writes=[b_stg[i]], dma=True)
                        P.add("dve", (lambda c, i: lambda e: e.tensor_scalar(out=wup[:, c, :], in0=stg[i][:, :],
                                                                            scalar1=gn[:, gpre, c:c + 1], scalar2=None, op0=ALU.mult))(c, i),
                              reads=[b_stg[i]], writes=[b_w])
                    for q in range(5):
                        f0, f1 = q * 5, min(NFC, q * 5 + 5)
                        nf = f1 - f0
                        i = q % 2
                        P.add("sp", (lambda f0, f1, nf, i: lambda e: e.dma_start(
                            out=stg[i][:, 0:nf * D].rearrange("p (f n) -> p f n", n=D), in_=w_dnL[layer].ap()[:, f0:f1, :]))(f0, f1, nf, i),
                            writes=[b_stg[i]], dma=True)
                        P.add("pool", (lambda f0, f1, nf, i: lambda e: e.tensor_copy(
                            out=wdn[:, f0:f1, :], in_=stg[i][:, 0:nf * D].rearrange("p (f n) -> p f n", n=D)))(f0, f1, nf, i),
                            reads=[b_stg[i]], writes=[b_w])
                    P.emit()
                xf = sb(st, "xf", [128, 8, 512], F32)
                xn = sb(st, "xnf", [128, 8, 512], BF16)
                hh = sb(st, "hh", [128, NFC, 512], BF16)
                abufL = [sb(st, f"abuf{i_}", [128, 514], F32) for i_ in range(2)]
                acL = [sb(st, f"ac{i_}", [128, 512], F32) for i_ in range(2)]
                geL = [sb(st, f"ge{i_}", [128, 512], F32) for i_ in range(2)]
                xr = [sb(st, f"xr{i}", [128, 512], F32) for i in range(2)]
                rt = sb(st, "rtf", [128, 512], F32)
                rs = sb(st, "rsf", [128, 512], F32)
                tmp = sb(st, "tmpf", [128, 512], F32)
                ahalo = sb(st, "ahalo", [128, NFC, 2], F32)
                b_xf, b_xn, b_hh = (Buf(n) for n in ("xf", "xn", "hh"))
                b_abufL = [Buf(f"abuf{i_}") for i_ in range(2)]
                b_acL = [Buf(f"ac{i_}") for i_ in range(2)]
                b_geL = [Buf(f"ge{i_}") for i_ in range(2)]
                b_xr = [Buf(f"xr{i}") for i in range(2)]
                b_rt, b_rs, b_tmp, b_ah = Buf("rt"), Buf("rs"), Buf("tmp"), Buf("ahalo")
                P = newP(f"ff{layer}")
                P.add("pool", lambda e: e.memset(ahalo[:, :, :], 0.0), writes=[b_ah])
                for n in range(NL):
                    P.add("sp", (lambda n: lambda e: e.dma_start(out=xf[:, :, :], in_=fm(src)[:, :, n * 512:(n + 1) * 512]))(n),
                          writes=[b_xf], dma=True)
                    P.add("act", lambda e: e.activation(out=xn[:, :, :], in_=xf[:, :, :], func=AF.Square), reads=[b_xf], writes=[b_xn])
                    rstd_from_sq(P, lambda c: xn[:, c, :], 8, ones_m, 7, b_xn, rt, rs, b_rt, b_rs, NORM_EPS)
                    if n == 0:
                        P.add("dve", lambda e: e.tensor_tensor(out=rs[:, :], in0=rs[:, :], in1=vl0[:, :], op=ALU.mult),
                              reads=[b_rs], writes=[b_rs])
                    for c in range(8):
                        P.add("dve", (lambda c: lambda e: e.tensor_tensor(out=xn[:, c, :], in0=xf[:, c, :], in1=rs[:, :], op=ALU.mult))(c),
                              reads=[b_xf, b_rs], writes=[b_xn])
                    for fc in range(NFC):
                        ba = (2 * fc) % 4
                        bg = ba + 1
                        abuf, ac, ge = abufL[fc % 2], acL[fc % 2], geL[fc % 2]
                        b_abuf, b_ac, b_ge = b_abufL[fc % 2], b_acL[fc % 2], b_geL[fc % 2]
                        for (b, col0) in ((ba, fc * 128), (bg, FF + fc * 128)):
                            for c in range(8):
                                P.add("pe", (lambda b, col0, c: lambda e: e.matmul(bank(b), lhsT=wup[:, c, col0:col0 + 128],
                                                                                   rhs=xn[:, c, :], start=(c == 0), stop=(c == 7)))(b, col0, c),
                                      reads=[b_xn], writes=[PB[b]])
                        P.add("pool", (lambda fc, abuf: lambda e: e.tensor_copy(out=abuf[:, 0:2], in_=ahalo[:, fc, :]))(fc, abuf),
                              reads=[b_ah], writes=[b_abuf])
                        P.add("act", (lambda ba, abuf: lambda e: e.activation(out=abuf[:, 2:514], in_=bank(ba), func=AF.Copy))(ba, abuf),
                              reads=[PB[ba]], writes=[b_abuf])
                        P.add("pool", (lambda fc, abuf: lambda e: e.tensor_copy(out=ahalo[:, fc, :], in_=abuf[:, 512:514]))(fc, abuf),
                              reads=[b_abuf], writes=[b_ah])
                        P.add("dve", (lambda fc, abuf, ac: lambda e: e.tensor_scalar(out=ac[:, :], in0=abuf[:, 2:514], scalar1=cw[:, fc, 2:3],
                                                                          scalar2=cb[:, fc:fc + 1], op0=ALU.mult, op1=ALU.add))(fc, abuf, ac),
                              reads=[b_abuf], writes=[b_ac])
                        P.add("dve", (lambda fc, abuf, ac: lambda e: e.scalar_tensor_tensor(out=ac[:, :], in0=abuf[:, 1:513], scalar=cw[:, fc, 1:2],
                                                                                 in1=ac[:, :], op0=ALU.mult, op1=ALU.add))(fc, abuf, ac),
                              reads=[b_abuf, b_ac], writes=[b_ac])
                        P.add("dve", (lambda fc, abuf, ac: lambda e: e.scalar_tensor_tensor(out=ac[:, :], in0=abuf[:, 0:512], scalar=cw[:, fc, 0:1],
                                                                                 in1=ac[:, :], op0=ALU.mult, op1=ALU.add))(fc, abuf, ac),
                              reads=[b_abuf, b_ac], writes=[b_ac])
                        P.add("act", (lambda ac, ge: lambda e: e.activation(out=ge[:, :], in_=ac[:, :], func=AF.Gelu_apprx_tanh))(ac, ge), reads=[b_ac], writes=[b_ge])
                        P.add("dve", (lambda fc, bg, ge: lambda e: e.tensor_tensor(out=hh[:, fc, :], in0=bank(bg), in1=ge[:, :], op=ALU.mult))(fc, bg, ge),
                              reads=[PB[bg], b_ge], writes=[b_hh])
                    if layer == 1 and n == 0 and _SKIP0:
                        continue
                    for dc in range(8):
                        b = 4 + dc % 2
                        for fc in range(NFC):
                            P.add("pe", (lambda b, dc, fc: lambda e: e.matmul(bank(b), lhsT=wdn[:, fc, dc * 128:(dc + 1) * 128],
                                                                              rhs=hh[:, fc, :], start=(fc == 0), stop=(fc == NFC - 1)))(b, dc, fc),
                                  reads=[b_hh], writes=[PB[b]])
                        P.add("act", (lambda b, dc: lambda e: e.activation(out=xf[:, dc, :], in_=bank(b), func=AF.Copy))(b, dc),
                              reads=[PB[b]], writes=[b_xf])
                        P.add("act", (lambda b, dc: lambda e: e.activation(out=xn[:, dc, :], in_=bank(b), func=AF.Square))(b, dc),
                              reads=[PB[b]], writes=[b_xn])
                    rstd_from_sq(P, lambda c: xn[:, c, :], 8, ones_m, 7, b_xn, rt, rs, b_rt, b_rs, NORM_EPS)
                    for dc in range(8):
                        i = dc % 2
                        P.add("sp", (lambda n, dc, i: lambda e: e.dma_start(
                            out=xr[i][:, :], in_=src.ap()[dc * 128:(dc + 1) * 128, n * 512:(n + 1) * 512]))(n, dc, i),
                            writes=[b_xr[i]], dma=True)
                        P.add("dve", (lambda dc: lambda e: e.scalar_tensor_tensor(
                            out=tmp[:, :], in0=xf[:, dc, :], scalar=gn[:, gpost, dc:dc + 1], in1=rs[:, :],
                            op0=ALU.mult, op1=ALU.mult))(dc), reads=[b_xf, b_rs], writes=[b_tmp])
                        P.add("dve", (lambda dc, i: lambda e: e.tensor_tensor(out=xf[:, dc, :], in0=tmp[:, :], in1=xr[i][:, :], op=ALU.add))(dc, i),
                              reads=[b_tmp, b_xr[i]], writes=[b_xf])
                    if dst_is_out and _OUTINT:
                        pass
                    elif dst_is_out:
                        P.add("sp", (lambda n: lambda e: e.dma_start(out=fm(dst)[:, :, (n - 1) * 512:n * 512], in_=xf[:, :, :]))(n),
                              reads=[b_xf], dma=True, key="st_xf")
                    else:
                        P.add("sp", (lambda n: lambda e: e.dma_start(out=fm(dst)[:, :, n * 512:(n + 1) * 512], in_=xf[:, :, :]))(n),
                              reads=[b_xf], dma=True, key="st_xf")
                P.emit()

        ffn_phase(0, xmid0, x1s, False)
        if _TESTF == 1:
            ffn_phase(0, xmid0, x1s, False)
            return nc
        if _TESTF == 2:
            ffn_phase(1, xmid0, x1s, False)
            return nc
        if _STOP < 6:
            return nc

        with contextlib.ExitStack() as st:
            win = sb(st, "win", [128, 8, 2 * E], BF16)
            wout = sb(st, "wout", [128, NEC, D], BF16)
            wsm = sb(st, "wsm", [128, 8, 128], BF16)
            bs = sb(st, "bs", [128, 8, 128], F32)
            lg = sb(st, "lg", [128, E], F32)
            lb = sb(st, "lb", [128, E], F32)
            b_w = Buf("w")
            with contextlib.ExitStack() as s2:
                stg = [sb(s2, f"stgs{i}", [128, 2 * E], F32) for i in range(2)]
                trl = sb(s2, "trl", [128, 128], F32)
                b_stg = [Buf(f"stg{i}") for i in range(2)]
                b_trl = Buf("trl")
                P = newP("sw")
                P.add("sp", lambda e: e.dma_start(out=bs[:, :, :], in_=bsb.ap()), writes=[b_w], dma=True, key="misc")
                P.add("sp", lambda e: e.dma_start(out=lg[:, :], in_=lng.ap()), writes=[b_w], dma=True, key="misc")
                P.add("sp", lambda e: e.dma_start(out=lb[:, :], in_=lnb.ap()), writes=[b_w], dma=True, key="misc")
                P.add("sp", lambda e: e.dma_start(out=trl[:, :], in_=tril.ap()), writes=[b_trl], dma=True, key="misc")
                P.add("sp", lambda e: e.dma_start(out=stg[0][:, 0:1024].rearrange("p (g t) -> p g t", t=128), in_=wsT.ap()),
                      writes=[b_stg[0]], dma=True)
                for g8 in range(8):
                    P.add("dve", (lambda g8: lambda e: e.tensor_tensor(out=wsm[:, g8, :], in0=stg[0][:, g8 * 128:(g8 + 1) * 128],
                                                                      in1=trl[:, :], op=ALU.mult))(g8),
                          reads=[b_stg[0], b_trl], writes=[b_w])
                for c in range(8):
                    i = (c + 1) % 2
                    P.add("sp", (lambda c, i: lambda e: e.dma_start(out=stg[i][:, :], in_=w_in.ap()[:, c, :]))(c, i),
                          writes=[b_stg[i]], dma=True)
                    P.add("dve", (lambda c, i: lambda e: e.tensor_scalar(out=win[:, c, :], in0=stg[i][:, :],
                                                                        scalar1=gn[:, 4, c:c + 1], scalar2=None, op0=ALU.mult))(c, i),
                          reads=[b_stg[i]], writes=[b_w])
                for q in range(4):
                    i = (q + 1) % 2
                    P.add("sp", (lambda q, i: lambda e: e.dma_start(
                        out=stg[i][:, 0:4 * D].rearrange("p (f n) -> p f n", n=D), in_=w_out.ap()[:, q * 4:(q + 1) * 4, :]))(q, i),
                        writes=[b_stg[i]], dma=True)
                    P.add("pool", (lambda q, i: lambda e: e.tensor_copy(
                        out=wout[:, q * 4:(q + 1) * 4, :], in_=stg[i][:, 0:4 * D].rearrange("p (f n) -> p f n", n=D)))(q, i),
                        reads=[b_stg[i]], writes=[b_w])
                P.emit()
            xt = sb(st, "xts", [128, 8, 512], F32)
            xn = sb(st, "xns", [128, 8, 512], BF16)
            uu = sb(st, "uu", [128, NEC, 512], BF16)
            vt = sb(st, "vt", [128, E], F32)
            junk = sb(st, "junk", [128, E], BF16)
            vnb = sb(st, "vnb", [128, E], BF16)
            vsum = sb(st, "vsum", [128, 4], F32)
            st1 = sb(st, "st1", [128, 4], F32)
            msb = sb(st, "msbs", [128, 8, 512], F32)
            rt = sb(st, "rts", [128, 512], F32)
            rs = sb(st, "rss", [128, 512], F32)
            tmp = sb(st, "tmps", [128, 512], F32)
            ts = sb(st, "ts", [128, 128], F32)
            b_xt, b_xn, b_uu, b_vt, b_junk, b_vnb, b_vsum, b_st1, b_msb, b_rt, b_rs, b_tmp, b_ts = (
                Buf(n) for n in ("xt", "xn", "uu", "vt", "junk", "vnb", "vsum", "st1", "msb", "rt", "rs", "tmp", "ts"))
            P = newP("sg")
            for n in range(NL):
                P.add("sp", (lambda n: lambda e: e.dma_start(out=xt[:, :, :], in_=fm(x1s)[:, :, n * 512:(n + 1) * 512]))(n),
                      writes=[b_xt], dma=True)
                P.add("act", lambda e: e.activation(out=xn[:, :, :], in_=xt[:, :, :], func=AF.Square), reads=[b_xt], writes=[b_xn])
                rstd_from_sq(P, lambda c: xn[:, c, :], 8, ones_m, 7, b_xn, rt, rs, b_rt, b_rs, NORM_EPS)
                for c in range(8):
                    P.add("dve", (lambda c: lambda e: e.tensor_tensor(out=xn[:, c, :], in0=xt[:, c, :], in1=rs[:, :], op=ALU.mult))(c),
                          reads=[b_xt, b_rs], writes=[b_xn])
                for uc in range(NEC):
                    b = uc % 2
                    for c in range(8):
                        P.add("pe", (lambda b, uc, c: lambda e: e.matmul(bank(b), lhsT=win[:, c, uc * 128:(uc + 1) * 128], rhs=xn[:, c, :],
                                                                         start=(c == 0), stop=(c == 7)))(b, uc, c),
                              reads=[b_xn], writes=[PB[b]])
                    P.add("act", (lambda b, uc: lambda e: e.activation(out=uu[:, uc, :], in_=bank(b), func=AF.Gelu_apprx_tanh))(b, uc),
                          reads=[PB[b]], writes=[b_uu])
                for j in range(4):
                    for q in range(4):
                        b = 2 + q % 2
                        for c in range(8):
                            P.add("pe", (lambda b, q, c, j: lambda e: e.matmul(
                                bank(b), lhsT=xn[:, c, j * 128:(j + 1) * 128], rhs=win[:, c, E + q * 512:E + (q + 1) * 512],
                                start=(c == 0), stop=(c == 7)))(b, q, c, j),
                                reads=[b_xn], writes=[PB[b]])
                        P.add("act", (lambda b, q: lambda e: e.activation(out=vt[:, q * 512:(q + 1) * 512], in_=bank(b),
                                                                          func=AF.Gelu_apprx_tanh, accum_out=vsum[:, q:q + 1]))(b, q),
                              reads=[PB[b]], writes=[b_vt, b_vsum])
                    P.add("dve", lambda e: e.reduce_sum(out=st1[:, 0:1], in_=vsum[:, :], axis=AX.X), reads=[b_vsum], writes=[b_st1])
                    P.add("dve", lambda e: e.tensor_scalar(out=st1[:, 0:1], in0=st1[:, 0:1], scalar1=-1.0 / E, scalar2=None, op0=ALU.mult),
                          reads=[b_st1], writes=[b_st1])
                    P.add("dve", lambda e: e.tensor_scalar(out=vt[:, :], in0=vt[:, :], scalar1=st1[:, 0:1], scalar2=None, op0=ALU.add),
                          reads=[b_vt, b_st1], writes=[b_vt])
                    P.add("act", lambda e: e.activation(out=junk[:, :], in_=vt[:, :], func=AF.Square, accum_out=st1[:, 1:2]),
                          reads=[b_vt], writes=[b_junk, b_st1])
                    P.add("dve", lambda e: e.tensor_scalar(out=st1[:, 2:3], in0=st1[:, 1:2], scalar1=1.0 / E, scalar2=LN_EPS,
                                                           op0=ALU.mult, op1=ALU.add), reads=[b_st1], writes=[b_st1])
                    P.add("act", lambda e: e.activation(out=st1[:, 3:4], in_=st1[:, 2:3], func=AF.Sqrt), reads=[b_st1], writes=[b_st1])
                    P.add("dve", lambda e: e.reciprocal(out=st1[:, 2:3], in_=st1[:, 3:4]), reads=[b_st1], writes=[b_st1])
                    P.add("dve", lambda e: e.scalar_tensor_tensor(out=vt[:, :], in0=vt[:, :], scalar=st1[:, 2:3], in1=lg[:, :],
                                                                 op0=ALU.mult, op1=ALU.mult), reads=[b_vt, b_st1], writes=[b_vt])
                    P.add("dve", lambda e: e.tensor_tensor(out=vnb[:, :], in0=vt[:, :], in1=lb[:, :], op=ALU.add),
                          reads=[b_vt], writes=[b_vnb])
                    for fc in range(NEC):
                        b = 4 + fc % 2
                        g8 = fc // 2
                        P.add("pe", (lambda b, fc, g8: lambda e: e.matmul(bank(b, 128, 128), lhsT=vnb[:, fc * 128:(fc + 1) * 128],
                                                                          rhs=wsm[:, g8, :], start=True, stop=True))(b, fc, g8),
                              reads=[b_vnb], writes=[PB[b]])
                        P.add("dve", (lambda b, g8: lambda e: e.tensor_tensor(out=ts[:, :], in0=bank(b, 128, 128), in1=bs[:, g8, :], op=ALU.add))(b, g8),
                              reads=[PB[b]], writes=[b_ts])
                        P.add("dve", (lambda fc, j: lambda e: e.tensor_tensor(out=uu[:, fc, j * 128:(j + 1) * 128], in0=ts[:, :],
                                                                             in1=uu[:, fc, j * 128:(j + 1) * 128], op=ALU.mult))(fc, j),
                              reads=[b_ts, b_uu], writes=[b_uu])
                for dc in range(8):
                    b = 6 + dc % 2 if False else dc % 2
                    for fc in range(NEC):
                        P.add("pe", (lambda b, dc, fc: lambda e: e.matmul(bank(b), lhsT=wout[:, fc, dc * 128:(dc + 1) * 128], rhs=uu[:, fc, :],
                                                                          start=(fc == 0), stop=(fc == NEC - 1)))(b, dc, fc),
                              reads=[b_uu], writes=[PB[b]])
                    P.add("act", (lambda b, dc: lambda e: e.activation(out=msb[:, dc, :], in_=bank(b), func=AF.Copy))(b, dc),
                          reads=[PB[b]], writes=[b_msb])
                    P.add("act", (lambda b, dc: lambda e: e.activation(out=xn[:, dc, :], in_=bank(b), func=AF.Square))(b, dc),
                          reads=[PB[b]], writes=[b_xn])
                postnorm_residual(P, msb, b_msb, xn, b_xn, rt, rs, b_rt, b_rs, 5, lambda dc: xt[:, dc, :], b_xt,
                                  lambda dc: msb[:, dc, :], b_msb, tmp, b_tmp)
                P.add("sp", (lambda n: lambda e: e.dma_start(out=fm(xmid1)[:, :, n * 512:(n + 1) * 512], in_=msb[:, :, :]))(n),
                      reads=[b_msb], dma=True, key="st_msb")
            P.emit()

        if _STOP < 7:
            return nc
        if _TESTF == 3:
            ffn_phase(1, xmid0, x1s, False)
            return nc
        if _TESTF == 4:
            ffn_phase(0, xmid1, x1s, False)
            return nc
        ffn_phase(1, xmid1, yT, True)
    return nc


_CACHE = {}


def _prep_inputs(S, inp):
    f32 = np.float32
    T = S // 4
    TL = T + HALO
    NL = TL // 512
    NT = S // 512
    x = np.asarray(inp["x"], f32)
    wqkv_full = np.asarray(inp["attn_w_qkv"], f32)[0]
    pos = np.arange(S)
    c7, a3, b4 = pos // 128, (pos % 128) // 16, pos % 16
    tril = (np.arange(128)[:, None] <= np.arange(128)[None, :]).astype(f32)
    ii = np.arange(128)[:, None, None]
    vv = np.arange(4)[None, :, None]
    jj = np.arange(512)[None, None, :]
    maskd = np.where(jj >= 128 * vv + ii, 0.0, NEG).astype(f32)
    gl = [inp["norm_mix_pre"][0], inp["norm_mix_post"][0], inp["norm_ffn_pre"][0], inp["norm_ffn_post"][0],
          inp["norm_mix_pre"][1], inp["norm_mix_post"][1], inp["norm_ffn_pre"][1], inp["norm_ffn_post"][1]]
    gains = np.ascontiguousarray(np.stack([np.asarray(g, f32).reshape(8, 128).T for g in gl], axis=1))
    lamv = np.stack([inp["attn_lambda_q1"][0], inp["attn_lambda_k1"][0], inp["attn_lambda_q2"][0], inp["attn_lambda_k2"][0]])[None].astype(f32)
    subln = np.asarray(inp["attn_subln"], f32)[0].reshape(128, 1)
    w_o = np.ascontiguousarray(np.asarray(inp["attn_w_o"], f32)[0].reshape(8, 128, D).transpose(1, 0, 2))
    w_up = np.ascontiguousarray(np.asarray(inp["ffn_w_up"], f32).reshape(2, 8, 128, 2 * FF).transpose(0, 2, 1, 3))
    w_dn = np.ascontiguousarray(np.asarray(inp["ffn_w_down"], f32).reshape(2, NFC, 128, D).transpose(0, 2, 1, 3))
    convw = np.ascontiguousarray(np.asarray(inp["ffn_conv_w"], f32).reshape(2, 3, NFC, 128).transpose(0, 3, 2, 1))
    convb = np.ascontiguousarray(np.asarray(inp["ffn_conv_b"], f32).reshape(2, NFC, 128).transpose(0, 2, 1))
    w_in = np.ascontiguousarray(np.asarray(inp["sgu_w_in"], f32)[0].reshape(8, 128, 2 * E).transpose(1, 0, 2))
    lng = np.ascontiguousarray(np.broadcast_to(np.asarray(inp["sgu_ln_g"], f32)[0][None, :], (128, E)))
    lnb = np.ascontiguousarray(np.broadcast_to(np.asarray(inp["sgu_ln_b"], f32)[0][None, :], (128, E)))
    wsT = np.ascontiguousarray(np.asarray(inp["sgu_w_s"], f32)[0].transpose(2, 0, 1))
    bsb = np.ascontiguousarray(np.broadcast_to(np.asarray(inp["sgu_b_s"], f32)[0][None], (128, 8, 128)))
    w_out = np.ascontiguousarray(np.asarray(inp["sgu_w_out"], f32)[0].reshape(NEC, 128, D).transpose(1, 0, 2))
    xT = [np.ascontiguousarray(x[b].T) for b in range(2)]
    maps = []
    for c in range(8):
        b, r = c // 4, c % 4
        t0 = r * T - HALO
        xl = np.zeros((D, TL), f32)
        lo = max(t0, 0)
        xl[:, lo - t0:] = xT[b][:, lo:t0 + TL]
        valid0 = np.ones((128, 512), f32) if r > 0 else np.zeros((128, 512), f32)
        wq = np.zeros((2, 5, D, 128), f32)
        ka = np.zeros((2, 6, S), ml_dtypes.bfloat16)
        qa = np.zeros((2, 6, S), ml_dtypes.bfloat16)
        for s in range(2):
            hd = r if s == 0 else 4 + r
            slope = 2.0 ** (-(hd + 1))
            for part, base in ((0, 0), (1, D)):
                A = wqkv_full[:, base + hd * 128: base + (hd + 1) * 128]
                wq[s, 2 * part + 0] = A
                wq[s, 2 * part + 1] = np.concatenate([A[:, 64:], A[:, :64]], axis=1)
            wq[s, 4] = wqkv_full[:, 2 * D + hd * 128: 2 * D + (hd + 1) * 128]
            ka[s, 0:3] = 1.0
            ka[s, 3] = (slope * 128.0 * c7).astype(ml_dtypes.bfloat16)
            ka[s, 4] = (slope * 16.0 * a3).astype(ml_dtypes.bfloat16)
            ka[s, 5] = (slope * b4).astype(ml_dtypes.bfloat16)
            qa[s, 0] = (-slope * 128.0 * c7).astype(ml_dtypes.bfloat16)
            qa[s, 1] = (-slope * 16.0 * a3).astype(ml_dtypes.bfloat16)
            qa[s, 2] = (-slope * b4).astype(ml_dtypes.bfloat16)
            qa[s, 3:6] = 1.0
        idx = np.zeros((128, NL * 8), np.int32)
        for n in range(NL):
            tile = max(r * (T // 512) + n - 1, 0)
            for hd in range(8):
                rank, s = hd % 4, hd // 4
                unit = s * NT + tile
                idx[:, n * 8 + hd] = ((unit // 4) * 4 + rank) * 512 + (unit % 4) * 128 + np.arange(128)
        maps.append({
            "xT_full": xT[b], "xT_loc": xl, "valid0": valid0, "kaug0": ka[0], "kaug1": ka[1], "qaug0": qa[0], "qaug1": qa[1], "maskd": maskd,
            **{f"wqkv{s_}_{w_}": np.ascontiguousarray(wq[s_, w_]) for s_ in range(2) for w_ in range(5)},
            "lamv": lamv, "subln": subln, "gains": gains, "idxtab": idx, "w_o": w_o, "w_up0": w_up[0], "w_up1": w_up[1], "w_dn0": w_dn[0], "w_dn1": w_dn[1],
            "convw0": convw[0], "convw1": convw[1], "convb0": convb[0], "convb1": convb[1], "w_in": w_in, "lng": lng, "lnb": lnb, "wsT": wsT, "tril": tril,
            "bsb": bsb, "w_out": w_out,
        })
    return maps


def kernel(**inputs):
    x = inputs["x"]
    S = x.shape[1]
    if S not in _CACHE:
        _CACHE[S] = build_program(S)
    nc = _CACHE[S]
    maps = _prep_inputs(S, inputs)
    res = run_bass_kernel_spmd(nc, maps, core_ids=list(range(8)))
    T = S // 4
    out = np.zeros((2, S, D), np.float32)
    for c in range(8):
        b, r = c // 4, c % 4
        out[b, r * T:(r + 1) * T, :] = np.asarray(res.results[c]["yT"]).T
    return out
```

```python
import contextlib
import numpy as np
import ml_dtypes
import concourse.bass as bass
import concourse.mybir as mybir
from concourse.bass_utils import run_bass_kernel_spmd

F32 = mybir.dt.float32
BF16 = mybir.dt.bfloat16
I32 = mybir.dt.int32
AF = mybir.ActivationFunctionType
ALU = mybir.AluOpType
AX = mybir.AxisListType

D = 1024
H = 8
DH = 64
FF = 2816
NFC = FF // 128
E = 2048
NEC = E // 128
NORM_EPS = 1e-6
LN_EPS = 1e-5
LAM_INIT = 0.2
WIN_A = 8
HALO = 512
NEG = -30000.0
_STOP = 99
_SKIP0 = True
_TESTF = 0
_ABL = 0
_OUTINT = False


class Buf:
    __slots__ = ("name", "lw", "rs")

    def __init__(self, name):
        self.name = name
        self.lw = None
        self.rs = []


class Op:
    __slots__ = ("eng", "fn", "dma", "key", "deps", "sig", "seq", "inc")

    def __init__(self, eng, fn, dma, key, inc):
        self.eng, self.fn, self.dma, self.key, self.inc = eng, fn, dma, key, inc
        self.deps = []
        self.sig = dma
        self.seq = 0


class Prog:
    ENGS = ("sp", "act", "dve", "pool", "pe")

    def __init__(self, nc, tag, semstack):
        self.nc = nc
        self.tag = tag
        self.semstack = semstack
        self.ops = []
        self.lastkey = {}

    def add(self, eng, fn, reads=(), writes=(), dma=False, key=None, inc=16):
        if dma and key is None:
            key = writes[0].name
        op = Op(eng, fn, dma, key, inc if dma else 1)
        hard, war = [], []
        for b in reads:
            if b.lw is not None:
                hard.append(b.lw)
        for b in writes:
            if b.lw is not None:
                hard.append(b.lw)
            war.extend(b.rs)
        deps = {}
        for d in hard:
            same = (d.eng == eng) and not d.dma and not dma
            if same and eng == "pe":
                continue
            deps[id(d)] = d
        for d in war:
            same = (d.eng == eng) and not d.dma and not dma
            if same:
                continue
            deps[id(d)] = d
        if dma and key in self.lastkey:
            d = self.lastkey[key]
            deps[id(d)] = d
        if dma:
            self.lastkey[key] = op
        for d in deps.values():
            d.sig = True
            op.deps.append(d)
        for b in reads:
            b.rs.append(op)
        for b in writes:
            b.lw = op
            b.rs = []
        self.ops.append(op)
        return op

    def emit(self):
        nc = self.nc
        st = self.semstack
        if True:
            esem = {e: st.enter_context(nc.semaphore(f"{self.tag}_{e}")) for e in self.ENGS}
            keys = []
            for op in self.ops:
                if op.dma and op.key not in keys:
                    keys.append(op.key)
            ksem = {k: st.enter_context(nc.semaphore(f"{self.tag}_k{i}")) for i, k in enumerate(keys)}
            cnt = {e: 0 for e in self.ENGS}
            kcnt = {k: 0 for k in keys}
            for op in self.ops:
                if op.dma:
                    kcnt[op.key] += op.inc
                    op.seq = kcnt[op.key]
                elif op.sig:
                    cnt[op.eng] += 1
                    op.seq = cnt[op.eng]
            per = {e: [o for o in self.ops if o.eng == e] for e in self.ENGS}

            def run(ename, e):
                waited = {}
                for op in per[ename]:
                    for d in op.deps:
                        sem = ksem[d.key] if d.dma else esem[d.eng]
                        sid = id(sem)
                        if waited.get(sid, 0) >= d.seq:
                            continue
                        e.wait_ge(sem, d.seq)
                        waited[sid] = d.seq
                    ins = op.fn(e)
                    if op.dma:
                        ins.then_inc(ksem[op.key], op.inc)
                    elif op.sig:
                        ins.then_inc(esem[op.eng], 1)
                if ename == "sp":
                    for k in keys:
                        if kcnt[k] > 0:
                            e.wait_ge(ksem[k], kcnt[k])

            with nc.Block() as block:
                @block.sync
                def _(e):
                    run("sp", e)

                @block.scalar
                def _(e):
                    run("act", e)

                @block.vector
                def _(e):
                    run("dve", e)

                @block.gpsimd
                def _(e):
                    run("pool", e)

                @block.tensor
                def _(e):
                    run("pe", e)


def build_program(S):
    NT = S // 512
    NKB = S // 128
    T = S // 4
    TL = T + HALO
    NL = TL // 512

    nc = bass.Bass("TRN2", target_bir_lowering=False)
    din = lambda n, s, d: nc.dram_tensor(n, s, d, kind="ExternalInput")
    xT_full = din("xT_full", [D, S], F32)
    xT_loc = din("xT_loc", [D, TL], F32)
    valid0 = din("valid0", [128, 512], F32)
    wqkvL = [[din(f"wqkv{s_}_{w_}", [D, 128], F32) for w_ in range(5)] for s_ in range(2)]
    kaugL = [din(f"kaug{s_}", [6, S], BF16) for s_ in range(2)]
    qaugL = [din(f"qaug{s_}", [6, S], BF16) for s_ in range(2)]
    maskd = din("maskd", [128, 4, 512], F32)
    lamv = din("lamv", [1, 4, 64], F32)
    subln = din("subln", [128, 1], F32)
    gains = din("gains", [128, 8, 8], F32)
    idxtab = din("idxtab", [128, NL * 8], I32)
    w_o = din("w_o", [128, 8, D], F32)
    w_upL = [din(f"w_up{l}", [128, 8, 2 * FF], F32) for l in range(2)]
    w_dnL = [din(f"w_dn{l}", [128, NFC, D], F32) for l in range(2)]
    convwL = [din(f"convw{l}", [128, NFC, 3], F32) for l in range(2)]
    convbL = [din(f"convb{l}", [128, NFC], F32) for l in range(2)]
    w_in = din("w_in", [128, 8, 2 * E], F32)
    lng = din("lng", [128, E], F32)
    lnb = din("lnb", [128, E], F32)
    wsT = din("wsT", [128, 8, 128], F32)
    tril = din("tril", [128, 128], F32)
    bsb = din("bsb", [128, 8, 128], F32)
    w_out = din("w_out", [128, NEC, D], F32)
    yT = nc.dram_tensor("yT", [D, T], F32, kind="ExternalOutput")

    ag_in = nc.dram_tensor("ag_in", [2 * NT * 128, 512], BF16)
    ag_out = nc.dram_tensor("ag_out", [4 * 2 * NT * 128, 512], BF16)
    xmid0 = nc.dram_tensor("xmid0", [D, TL], F32)
    x1s = nc.dram_tensor("x1s", [D, TL], F32)
    xmid1 = nc.dram_tensor("xmid1", [D, TL], F32)

    def fm(t):
        return t.ap().rearrange("(c p) t -> p c t", p=128)

    with contextlib.ExitStack() as top:
        _uid = iter(range(1 << 30))
        sb = lambda st, n, s, d: st.enter_context(nc.sbuf_tensor(f"{n}_u{next(_uid)}", s, d))
        ps = top.enter_context(nc.psum_tensor("ps", [128, 8 * 512], F32))
        PB = [Buf(f"ps{i}") for i in range(8)]

        def newP(tag):
            for b_ in PB:
                b_.lw = None
                b_.rs = []
            return Prog(nc, tag, top)

        def bank(i, rows=128, cols=512, c0=0):
            return ps[0:rows, i * 512 + c0:i * 512 + c0 + cols]

        ones_m = sb(top, "ones_m", [128, 128], BF16)
        ones_v = sb(top, "ones_v", [128, 128], BF16)
        ones_f = sb(top, "ones_f", [128, 128], F32)
        ones_b = sb(top, "ones_b", [128, 128], BF16)
        gn = sb(top, "gn", [128, 8, 8], F32)
        neglam = sb(top, "neglam", [128, 1], F32)
        subg = sb(top, "subg", [128, 1], F32)
        B_const = Buf("const")

        with contextlib.ExitStack() as st:
            P = newP("c0")
            lv = sb(st, "lv", [1, 4, 64], F32)
            pr = sb(st, "pr", [1, 2, 64], F32)
            sm = sb(st, "sm", [1, 2], F32)
            ex = sb(st, "ex", [1, 2], F32)
            nl = sb(st, "nl", [1, 1], F32)
            b_lv, b_pr, b_sm, b_ex, b_nl = (Buf(n) for n in ("lv", "pr", "sm", "ex", "nl"))
            b_gn, b_sub, b_ones = Buf("gn"), Buf("sub"), Buf("ones")
            P.add("sp", lambda e: e.dma_start(out=lv[:, :, :], in_=lamv.ap()), writes=[b_lv], dma=True, key="c")
            P.add("sp", lambda e: e.dma_start(out=gn[:, :, :], in_=gains.ap()), writes=[b_gn], dma=True, key="c")
            P.add("sp", lambda e: e.dma_start(out=subg[:, :], in_=subln.ap()), writes=[b_sub], dma=True, key="c")
            P.add("pool", lambda e: e.memset(ones_m[:, :], 1.0 / 1024.0), writes=[b_ones])
            P.add("pool", lambda e: e.memset(ones_v[:, :], 1.0 / 128.0), writes=[b_ones])
            P.add("pool", lambda e: e.memset(ones_f[:, :], 1.0), writes=[b_ones])
            P.add("pool", lambda e: e.memset(ones_b[:, :], 1.0), writes=[b_ones])
            P.add("dve", lambda e: e.tensor_tensor(out=pr[:, 0, :], in0=lv[:, 0, :], in1=lv[:, 1, :], op=ALU.mult),
                  reads=[b_lv], writes=[b_pr])
            P.add("dve", lambda e: e.tensor_tensor(out=pr[:, 1, :], in0=lv[:, 2, :], in1=lv[:, 3, :], op=ALU.mult),
                  reads=[b_lv], writes=[b_pr])
            P.add("dve", lambda e: e.reduce_sum(out=sm[:, 0:1], in_=pr[:, 0, :], axis=AX.X), reads=[b_pr], writes=[b_sm])
            P.add("dve", lambda e: e.reduce_sum(out=sm[:, 1:2], in_=pr[:, 1, :], axis=AX.X), reads=[b_pr], writes=[b_sm])
            P.add("act", lambda e: e.activation(out=ex[:, :], in_=sm[:, :], func=AF.Exp), reads=[b_sm], writes=[b_ex])
            P.add("dve", lambda e: e.scalar_tensor_tensor(out=nl[:, :], in0=ex[:, 1:2], scalar=-LAM_INIT, in1=ex[:, 0:1],
                                                         op0=ALU.add, op1=ALU.subtract), reads=[b_ex], writes=[b_nl])
            P.add("pe", lambda e: e.matmul(bank(0, 128, 1), lhsT=ones_f[0:1, :], rhs=nl[0:1, 0:1], start=True, stop=True),
                  reads=[b_nl, b_ones], writes=[PB[0]])
            P.add("dve", lambda e: e.tensor_copy(out=neglam[:, :], in_=bank(0, 128, 1)), reads=[PB[0]], writes=[B_const])
            P.add("dve", lambda e: e.tensor_scalar(out=subg[:, :], in0=subg[:, :], scalar1=1.0 - LAM_INIT, scalar2=None,
                                                   op0=ALU.mult), reads=[b_sub], writes=[b_sub])
            P.emit()

        def rstd_from_sq(P, sq_ap_fn, nch, ones, bankid, sq_buf, tmp, rstd, b_tmp, b_rstd, eps):
            for c in range(nch):
                P.add("pe", (lambda c: lambda e: e.matmul(bank(bankid), lhsT=ones[:, :], rhs=sq_ap_fn(c),
                                                            start=(c == 0), stop=(c == nch - 1)))(c),
                      reads=[sq_buf], writes=[PB[bankid]])
            P.add("act", lambda e: e.activation(out=tmp[:, :], in_=bank(bankid), func=AF.Sqrt, bias=eps, scale=1.0),
                  reads=[PB[bankid]], writes=[b_tmp])
            P.add("dve", lambda e: e.reciprocal(out=rstd[:, :], in_=tmp[:, :]), reads=[b_tmp], writes=[b_rstd])

        for slot in range(2):
            if _STOP < 1 + slot:
                return nc
            with contextlib.ExitStack() as st:
                KT = [sb(st, f"KT{m}", [70, S], BF16) for m in range(2)]
                Vt = sb(st, "Vt", [128, S], BF16)
                wq = [sb(st, f"wq{m}", [128, 8, 128], BF16) for m in range(2)]
                wk = [sb(st, f"wk{m}", [128, 8, 128], BF16) for m in range(2)]
                wv = sb(st, "wv", [128, 8, 128], BF16)
                stg = sb(st, "stg", [128, 8, 128], F32)
                xt = [sb(st, "xt0", [128, 8, 512], F32)] * 2
                xsq = sb(st, "xsq", [128, 8, 512], BF16)
                xn = [sb(st, f"xn{i}", [128, 8, 512], BF16) for i in range(2)]
                rtmp = sb(st, "rtmp", [128, 512], F32)
                rstd = sb(st, "rstd", [128, 512], F32)
                QT = [[sb(st, f"QT{m}_{i}", [70, 512], BF16) for i in range(2)] for m in range(2)]
                PT = [sb(st, f"PT{i}", [128, 1024], BF16) for i in range(3)]
                tmpS = [sb(st, f"tmpS{i}", [128, 1024], F32) for i in range(2)]
                acc = [sb(st, f"acc{m}", [128, 512], F32) for m in range(2)]
                mk = sb(st, "mk", [128, 4, 512], F32)
                rr = [sb(st, f"rr{m}", [128, 512], F32) for m in range(2)]
                tt = [sb(st, f"tt{m}", [128, 512], F32) for m in range(2)]
                dd = sb(st, "dd", [128, 512], F32)
                dsq = sb(st, "dsq", [128, 512], BF16)
                r2t = sb(st, "r2t", [128, 512], F32)
                r2 = sb(st, "r2", [128, 512], F32)
                oT = [sb(st, f"oT{i}", [128, 512], BF16) for i in range(2)]

                P = newP(f"a{slot}")
                b_KT = [Buf(f"KT{m}") for m in range(2)]
                b_KTa = Buf("KTaug")
                b_V = Buf("V")
                b_w = Buf("w")
                b_stg = Buf("stg")
                b_xt = [Buf("xt0")] * 2
                b_xsq = Buf("xsq")
                b_xn = [Buf(f"xn{i}") for i in range(2)]
                b_rtmp, b_rstd = Buf("rtmp"), Buf("rstd")
                b_QT = [[Buf(f"QT{m}_{i}") for i in range(2)] for m in range(2)]
                b_QTa = [[Buf(f"QTa{m}_{i}") for i in range(2)] for m in range(2)]
                b_PT = [[Buf(f"PT{i}_{m_}") for m_ in range(2)] for i in range(3)]
                b_tmpS = [[Buf(f"tmpS{i}_{m_}") for m_ in range(2)] for i in range(2)]
                b_acc = [Buf(f"acc{m}") for m in range(2)]
                b_mk = Buf("mk")
                b_rr = [Buf(f"rr{m}") for m in range(2)]
                b_tt = [Buf(f"tt{m}") for m in range(2)]
                b_dd, b_dsq, b_r2t, b_r2 = Buf("dd"), Buf("dsq"), Buf("r2t"), Buf("r2")
                b_oT = [Buf(f"oT{i}") for i in range(2)]

                dests = [wq[0], wq[1], wk[0], wk[1], wv]
                for wi in range(5):
                    P.add("sp", (lambda wi: lambda e: e.dma_start(
                        out=stg[:, :, :], in_=wqkvL[slot][wi].ap().rearrange("(c p) n -> p c n", p=128)))(wi),
                        writes=[b_stg], dma=True, key="ld")
                    for c in range(8):
                        sc2 = 0.125 if wi < 2 else 1.0
                        P.add("dve", (lambda wi, c, sc2: lambda e: e.tensor_scalar(
                            out=dests[wi][:, c, :], in0=stg[:, c, :], scalar1=gn[:, 0, c:c + 1], scalar2=sc2,
                            op0=ALU.mult, op1=ALU.mult))(wi, c, sc2), reads=[b_stg], writes=[b_w])
                for m in range(2):
                    P.add("sp", (lambda m: lambda e: e.dma_start(out=KT[m][64:70, :], in_=kaugL[slot].ap()))(m),
                          writes=[b_KTa], dma=True, key="ld")
                P.add("sp", lambda e: e.dma_start(out=mk[:, :, :], in_=maskd.ap()), writes=[b_mk], dma=True, key="ld")

                def prep(g):
                    i = g % 2
                    P.add("sp", lambda e: e.dma_start(out=xt[i][:, :, :], in_=fm(xT_full)[:, :, g * 512:(g + 1) * 512]),
                          writes=[b_xt[i]], dma=True)
                    P.add("act", lambda e: e.activation(out=xsq[:, :, :], in_=xt[i][:, :, :], func=AF.Square),
                          reads=[b_xt[i]], writes=[b_xsq])
                    rstd_from_sq(P, lambda c: xsq[:, c, :], 8, ones_m, 7, b_xsq, rtmp, rstd, b_rtmp, b_rstd, NORM_EPS)
                    for c in range(8):
                        P.add("dve", (lambda c: lambda e: e.tensor_tensor(out=xn[i][:, c, :], in0=xt[i][:, c, :],
                                                                         in1=rstd[:, :], op=ALU.mult))(c),
                              reads=[b_xt[i], b_rstd], writes=[b_xn[i]])
                    for m in range(2):
                        P.add("sp", (lambda m: lambda e: e.dma_start(out=QT[m][i][64:70, :],
                                                                      in_=qaugL[slot].ap()[:, g * 512:(g + 1) * 512]))(m),
                              writes=[b_QTa[m][i]], dma=True, key=f"qa{i}")

                def qkv(g):
                    i = g % 2
                    bk = [6, 7]
                    n = 0
                    for m in range(2):
                        for kind in range(2):
                            w = wq[m] if kind == 0 else wk[m]
                            b = bk[n % 2]
                            n += 1
                            for c in range(8):
                                P.add("pe", (lambda w, b, c: lambda e: e.matmul(bank(b), lhsT=w[:, c, :], rhs=xn[i][:, c, :],
                                                                                 start=(c == 0), stop=(c == 7)))(w, b, c),
                                      reads=[b_w, b_xn[i]], writes=[PB[b]])
                            if kind == 0:
                                P.add("act", (lambda m, b: lambda e: e.activation(out=QT[m][i][0:64, :], in_=bank(b, 64),
                                                                                   func=AF.Copy))(m, b),
                                      reads=[PB[b]], writes=[b_QT[m][i]])
                            else:
                                P.add("act", (lambda m, b: lambda e: e.activation(
                                    out=KT[m][0:64, g * 512:(g + 1) * 512], in_=bank(b, 64), func=AF.Copy))(m, b),
                                    reads=[PB[b]], writes=[b_KT[m]])
                    b = bk[n % 2]
                    for j in range(4):
                        for c in range(8):
                            P.add("pe", (lambda b, j, c: lambda e: e.matmul(
                                bank(b, 128, 128, j * 128), lhsT=xn[i][:, c, j * 128:(j + 1) * 128], rhs=wv[:, c, :],
                                start=(c == 0), stop=(c == 7)))(b, j, c),
                                reads=[b_w, b_xn[i]], writes=[PB[b]])
                    P.add("act", (lambda b: lambda e: e.activation(out=Vt[:, g * 512:(g + 1) * 512], in_=bank(b), func=AF.Copy))(b),
                          reads=[PB[b]], writes=[b_V])

                ucount = [0]

                def attention(g):
                    i = g % 2
                    kb0 = max(0, 4 * g - WIN_A) if slot == 0 else 0
                    kbl = 4 * g + 3
                    kbs = list(range(kb0, kbl + 1))
                    us = []
                    for _ in kbs:
                        us.append(ucount[0])
                        ucount[0] += 1

                    def emit_qk(kb, u, maps=(0, 1)):
                        pb = u % 2
                        for m in maps:
                            P.add("pe", (lambda m, kb, pb: lambda e: e.matmul(
                                bank(2 * pb + m), lhsT=KT[m][0:70, kb * 128:(kb + 1) * 128], rhs=QT[m][i][0:70, :],
                                start=True, stop=True))(m, kb, pb),
                                reads=[b_KT[m], b_KTa, b_QT[m][i], b_QTa[m][i]], writes=[PB[2 * pb + m]])

                    def emit_exp(kb, u):
                        pb = u % 2
                        pt = u % 3
                        for m in range(2):
                            if kb >= 4 * g:
                                v = kb - 4 * g
                                P.add("dve", (lambda m, pb, v: lambda e: e.tensor_tensor(
                                    out=tmpS[pb][:, m * 512:(m + 1) * 512], in0=bank(2 * pb + m), in1=mk[:, v, :],
                                    op=ALU.add))(m, pb, v),
                                    reads=[PB[2 * pb + m], b_mk], writes=[b_tmpS[pb][m]])
                                P.add("act", (lambda m, pb, pt: lambda e: e.activation(
                                    out=PT[pt][:, m * 512:(m + 1) * 512], in_=tmpS[pb][:, m * 512:(m + 1) * 512], func=AF.Exp))(m, pb, pt),
                                    reads=[b_tmpS[pb][m]], writes=[b_PT[pt][m]])
                            else:
                                P.add("act", (lambda m, pb, pt: lambda e: e.activation(
                                    out=PT[pt][:, m * 512:(m + 1) * 512], in_=bank(2 * pb + m), func=AF.Exp))(m, pb, pt),
                                    reads=[PB[2 * pb + m]], writes=[b_PT[pt][m]])

                    def emit_pv(kb, u, maps=(0, 1), sums=True):
                        pt = u % 3
                        for m in maps:
                            P.add("pe", (lambda m, kb, pt: lambda e: e.matmul(
                                bank(4 + m), lhsT=Vt[:, kb * 128:(kb + 1) * 128], rhs=PT[pt][:, m * 512:(m + 1) * 512],
                                start=(kb == kb0), stop=(kb == kbl)))(m, kb, pt),
                                reads=[b_V, b_PT[pt][m]], writes=[PB[4 + m]])
                        if not sums:
                            return
                        if kb == kb0:
                            P.add("dve", (lambda pt: lambda e: e.tensor_copy(out=acc[0][:, :], in_=PT[pt][:, 0:512]))(pt),
                                  reads=[b_PT[pt][0]], writes=[b_acc[0]])
                        else:
                            P.add("dve", (lambda pt: lambda e: e.tensor_tensor(
                                out=acc[0][:, :], in0=acc[0][:, :], in1=PT[pt][:, 0:512], op=ALU.add))(pt),
                                reads=[b_PT[pt][0], b_acc[0]], writes=[b_acc[0]])
                        P.add("pe", (lambda kb, pt: lambda e: e.matmul(
                            bank(6), lhsT=ones_b[:, :], rhs=PT[pt][:, 512:1024], start=(kb == kb0), stop=(kb == kbl)))(kb, pt),
                            reads=[b_PT[pt][1]], writes=[PB[6]])

                    emit_qk(kbs[0], us[0])
                    for n_, kb in enumerate(kbs):
                        emit_exp(kb, us[n_])
                        for m in range(2):
                            if n_ + 1 < len(kbs):
                                emit_qk(kbs[n_ + 1], us[n_ + 1], maps=(m,))
                            emit_pv(kb, us[n_], maps=(m,), sums=(m == 1))

                def finalize(g):
                    i = g % 2
                    P.add("pe", lambda e: e.matmul(bank(7), lhsT=ones_f[:, :], rhs=acc[0][:, :], start=True, stop=True),
                          reads=[b_acc[0]], writes=[PB[7]])
                    for m in range(2):
                        P.add("dve", (lambda m: lambda e: e.reciprocal(out=rr[m][:, :], in_=bank(7 - m)))(m),
                              reads=[PB[7 - m]], writes=[b_rr[m]])
                        P.add("dve", (lambda m: lambda e: e.tensor_tensor(out=tt[m][:, :], in0=bank(4 + m), in1=rr[m][:, :],
                                                                         op=ALU.mult))(m),
                              reads=[PB[4 + m], b_rr[m]], writes=[b_tt[m]])
                    P.add("dve", lambda e: e.scalar_tensor_tensor(out=dd[:, :], in0=tt[1][:, :], scalar=neglam[:, 0:1],
                                                                 in1=tt[0][:, :], op0=ALU.mult, op1=ALU.add),
                          reads=[b_tt[0], b_tt[1]], writes=[b_dd])
                    P.add("act", lambda e: e.activation(out=dsq[:, :], in_=dd[:, :], func=AF.Square), reads=[b_dd], writes=[b_dsq])
                    rstd_from_sq(P, lambda c: dsq[:, :], 1, ones_v, 6, b_dsq, r2t, r2, b_r2t, b_r2, NORM_EPS)
                    P.add("dve", lambda e: e.scalar_tensor_tensor(out=oT[i][:, :], in0=dd[:, :], scalar=subg[:, 0:1],
                                                                 in1=r2[:, :], op0=ALU.mult, op1=ALU.mult),
                          reads=[b_dd, b_r2], writes=[b_oT[i]])
                    row0 = (slot * NT + g) * 128
                    P.add("sp", lambda e: e.dma_start(out=ag_in.ap()[row0:row0 + 128, :], in_=oT[i][:, :]),
                          reads=[b_oT[i]], dma=True, key=f"st_oT{i}")

                prep(0)
                for g in range(NT):
                    qkv(g)
                    if g + 1 < NT:
                        prep(g + 1)
                    attention(g)
                    finalize(g)
                P.emit()

        if _STOP < 3:
            return nc
        if True:
            csem = top.enter_context(nc.semaphore("csem"))
            with nc.Block() as block:
                @block.gpsimd
                def _(g):
                    RP = 512
                    for k in range(2 * NT * 128 // RP):
                        g.collective_compute("AllGather", ALU.bypass, replica_groups=[[0, 1, 2, 3], [4, 5, 6, 7]],
                                             ins=[ag_in.ap()[k * RP:(k + 1) * RP, :].opt()],
                                             outs=[ag_out.ap()[k * 4 * RP:(k + 1) * 4 * RP, :].opt()]).then_inc(csem)
                        g.wait_ge(csem, k + 1)

        if _STOP < 4:
            return nc
        def postnorm_residual(P, msb, b_msb, msq, b_msq, rt, rs, b_rt, b_rs, gvec, resid_fn, b_resid, out_fn, b_out, tmp, b_tmp2):
            rstd_from_sq(P, lambda c: msq[:, c, :], 8, ones_m, 7, b_msq, rt, rs, b_rt, b_rs, NORM_EPS)
            for dc in range(8):
                P.add("dve", (lambda dc: lambda e: e.scalar_tensor_tensor(
                    out=tmp[:, :], in0=msb[:, dc, :], scalar=gn[:, gvec, dc:dc + 1], in1=rs[:, :],
                    op0=ALU.mult, op1=ALU.mult))(dc), reads=[b_msb, b_rs], writes=[b_tmp2])
                P.add("dve", (lambda dc: lambda e: e.tensor_tensor(out=out_fn(dc), in0=tmp[:, :], in1=resid_fn(dc), op=ALU.add))(dc),
                      reads=[b_tmp2, b_resid], writes=[b_out])

        with contextlib.ExitStack() as st:
            wo = sb(st, "wo", [128, 8, D], BF16)
            b_wo = Buf("wo")
            with contextlib.ExitStack() as s2:
                stg = sb(s2, "stgo", [128, 8, D], F32)
                b_stg = Buf("stg")
                P = newP("wo0")
                P.add("sp", lambda e: e.dma_start(out=stg[:, :, :], in_=w_o.ap()), writes=[b_stg], dma=True)
                for c in range(8):
                    P.add("dve", (lambda c: lambda e: e.tensor_copy(out=wo[:, c, :], in_=stg[:, c, :]))(c),
                          reads=[b_stg], writes=[b_wo])
                P.emit()
            idx = sb(st, "idx", [128, NL * 8], I32)
            og = sb(st, "og", [128, 8, 512], BF16)
            xt = sb(st, "xtw", [128, 8, 512], F32)
            msb = sb(st, "msb", [128, 8, 512], F32)
            msq = sb(st, "msq", [128, 8, 512], BF16)
            rt = sb(st, "rtw", [128, 512], F32)
            rs = sb(st, "rsw", [128, 512], F32)
            tmp = sb(st, "tmpw", [128, 512], F32)
            b_idx, b_og, b_xt, b_msb, b_msq, b_rt, b_rs, b_tmp = (Buf(n) for n in ("idx", "og", "xt", "msb", "msq", "rt", "rs", "tmp"))
            P = newP("wo1")
            P.add("sp", lambda e: e.dma_start(out=idx[:, :], in_=idxtab.ap()), writes=[b_idx], dma=True, key="ld")
            for n in range(NL):
                for hd in range(8):
                    P.add("pool", (lambda n, hd: lambda e: e.indirect_dma_start(
                        out=og[:, hd, :], out_offset=None, in_=ag_out.ap(),
                        in_offset=bass.IndirectOffsetOnAxis(ap=idx[:, n * 8 + hd:n * 8 + hd + 1], axis=0)))(n, hd),
                        reads=[b_idx], writes=[b_og], dma=True, key="og")
                P.add("sp", (lambda n: lambda e: e.dma_start(out=xt[:, :, :], in_=fm(xT_loc)[:, :, n * 512:(n + 1) * 512]))(n),
                      writes=[b_xt], dma=True)
                for dc in range(8):
                    b = dc % 2
                    for hd in range(8):
                        P.add("pe", (lambda b, dc, hd: lambda e: e.matmul(bank(b), lhsT=wo[:, hd, dc * 128:(dc + 1) * 128],
                                                                          rhs=og[:, hd, :], start=(hd == 0), stop=(hd == 7)))(b, dc, hd),
                              reads=[b_og], writes=[PB[b]])
                    P.add("act", (lambda b, dc: lambda e: e.activation(out=msb[:, dc, :], in_=bank(b), func=AF.Copy))(b, dc),
                          reads=[PB[b]], writes=[b_msb])
                    P.add("act", (lambda b, dc: lambda e: e.activation(out=msq[:, dc, :], in_=bank(b), func=AF.Square))(b, dc),
                          reads=[PB[b]], writes=[b_msq])
                postnorm_residual(P, msb, b_msb, msq, b_msq, rt, rs, b_rt, b_rs, 1, lambda dc: xt[:, dc, :], b_xt,
                                  lambda dc: msb[:, dc, :], b_msb, tmp, b_tmp)
                P.add("sp", (lambda n: lambda e: e.dma_start(out=fm(xmid0)[:, :, n * 512:(n + 1) * 512], in_=msb[:, :, :]))(n),
                      reads=[b_msb], dma=True, key="st_msb")
            P.emit()

        if _STOP < 5:
            return nc
        def ffn_phase(layer, src, dst, dst_is_out):
            gpre, gpost = (2, 3) if layer == 0 else (6, 7)
            with contextlib.ExitStack() as st:
                wup = sb(st, "wup", [128, 8, 2 * FF], BF16)
                wdn = sb(st, "wdn", [128, NFC, D], BF16)
                cw = sb(st, "cw", [128, NFC, 3], F32)
                cb = sb(st, "cb", [128, NFC], F32)
                vl0 = sb(st, "vl0", [128, 512], F32)
                b_w = Buf("w")
                with contextlib.ExitStack() as s2:
                    stg = [sb(s2, f"stgf{i}", [128, 2 * FF], F32) for i in range(2)]
                    b_stg = [Buf(f"stg{i}") for i in range(2)]
                    P = newP(f"fw{layer}")
                    P.add("sp", lambda e: e.dma_start(out=cw[:, :, :], in_=convwL[layer].ap()), writes=[b_w], dma=True, key="misc")
                    P.add("sp", lambda e: e.dma_start(out=cb[:, :], in_=convbL[layer].ap()), writes=[b_w], dma=True, key="misc")
                    P.add("sp", lambda e: e.dma_start(out=vl0[:, :], in_=valid0.ap()), writes=[b_w], dma=True, key="misc")
                    for c in range(8):
                        i = c % 2
                        P.add("sp", (lambda c, i: lambda e: e.dma_start(out=stg[i][:, :], in_=w_upL[layer].ap()[:, c, :]))(c, i),
                              writes=[b_stg[i]], dma=True)
                        P.add("dve", (lambda c, i: lambda e: e.tensor_scalar(out=wup[:, c, :], in0=stg[i][:, :],
                                                                            scalar1=gn[:, gpre, c:c + 1], scalar2=None, op0=ALU.mult))(c, i),
                              reads=[b_stg[i]], writes=[b_w])
                    for q in range(5):
                        f0, f1 = q * 5, min(NFC, q * 5 + 5)
                        nf = f1 - f0
                        i = q % 2
                        P.add("sp", (lambda f0, f1, nf, i: lambda e: e.dma_start(
                            out=stg[i][:, 0:nf * D].rearrange("p (f n) -> p f n", n=D), in_=w_dnL[layer].ap()[:, f0:f1, :]))(f0, f1, nf, i),
                            writes=[b_stg[i]], dma=True)
                        P.add("pool", (lambda f0, f1, nf, i: lambda e: e.tensor_copy(
                            out=wdn[:, f0:f1, :], in_=stg[i][:, 0:nf * D].rearrange("p (f n) -> p f n", n=D)))(f0, f1, nf, i),
                            reads=[b_stg[i]], writes=[b_w])
                    P.emit()
                xf = sb(st, "xf", [128, 8, 512], F32)
                xn = sb(st, "xnf", [128, 8, 512], BF16)
                hh = sb(st, "hh", [128, NFC, 512], BF16)
                abufL = [sb(st, f"abuf{i_}", [128, 514], F32) for i_ in range(2)]
                acL = [sb(st, f"ac{i_}", [128, 512], F32) for i_ in range(2)]
                geL = [sb(st, f"ge{i_}", [128, 512], F32) for i_ in range(2)]
                xr = [sb(st, f"xr{i}", [128, 512], F32) for i in range(2)]
                rt = sb(st, "rtf", [128, 512], F32)
                rs = sb(st, "rsf", [128, 512], F32)
                tmp = sb(st, "tmpf", [128, 512], F32)
                ahalo = sb(st, "ahalo", [128, NFC, 2], F32)
                b_xf, b_xn, b_hh = (Buf(n) for n in ("xf", "xn", "hh"))
                b_abufL = [Buf(f"abuf{i_}") for i_ in range(2)]
                b_acL = [Buf(f"ac{i_}") for i_ in range(2)]
                b_geL = [Buf(f"ge{i_}") for i_ in range(2)]
                b_xr = [Buf(f"xr{i}") for i in range(2)]
                b_rt, b_rs, b_tmp, b_ah = Buf("rt"), Buf("rs"), Buf("tmp"), Buf("ahalo")
                P = newP(f"ff{layer}")
                P.add("pool", lambda e: e.memset(ahalo[:, :, :], 0.0), writes=[b_ah])
                for n in range(NL):
                    P.add("sp", (lambda n: lambda e: e.dma_start(out=xf[:, :, :], in_=fm(src)[:, :, n * 512:(n + 1) * 512]))(n),
                          writes=[b_xf], dma=True)
                    P.add("act", lambda e: e.activation(out=xn[:, :, :], in_=xf[:, :, :], func=AF.Square), reads=[b_xf], writes=[b_xn])
                    rstd_from_sq(P, lambda c: xn[:, c, :], 8, ones_m, 7, b_xn, rt, rs, b_rt, b_rs, NORM_EPS)
                    if n == 0:
                        P.add("dve", lambda e: e.tensor_tensor(out=rs[:, :], in0=rs[:, :], in1=vl0[:, :], op=ALU.mult),
                              reads=[b_rs], writes=[b_rs])
                    for c in range(8):
                        P.add("dve", (lambda c: lambda e: e.tensor_tensor(out=xn[:, c, :], in0=xf[:, c, :], in1=rs[:, :], op=ALU.mult))(c),
                              reads=[b_xf, b_rs], writes=[b_xn])
                    for fc in range(NFC):
                        ba = (2 * fc) % 4
                        bg = ba + 1
                        abuf, ac, ge = abufL[fc % 2], acL[fc % 2], geL[fc % 2]
                        b_abuf, b_ac, b_ge = b_abufL[fc % 2], b_acL[fc % 2], b_geL[fc % 2]
                        for (b, col0) in ((ba, fc * 128), (bg, FF + fc * 128)):
                            for c in range(8):
                                P.add("pe", (lambda b, col0, c: lambda e: e.matmul(bank(b), lhsT=wup[:, c, col0:col0 + 128],
                                                                                   rhs=xn[:, c, :], start=(c == 0), stop=(c == 7)))(b, col0, c),
                                      reads=[b_xn], writes=[PB[b]])
                        P.add("pool", (lambda fc, abuf: lambda e: e.tensor_copy(out=abuf[:, 0:2], in_=ahalo[:, fc, :]))(fc, abuf),
                              reads=[b_ah], writes=[b_abuf])
                        P.add("act", (lambda ba, abuf: lambda e: e.activation(out=abuf[:, 2:514], in_=bank(ba), func=AF.Copy))(ba, abuf),
                              reads=[PB[ba]], writes=[b_abuf])
                        P.add("pool", (lambda fc, abuf: lambda e: e.tensor_copy(out=ahalo[:, fc, :], in_=abuf[:, 512:514]))(fc, abuf),
                              reads=[b_abuf], writes=[b_ah])
                        P.add("dve", (lambda fc, abuf, ac: lambda e: e.tensor_scalar(out=ac[:, :], in0=abuf[:, 2:514], scalar1=cw[:, fc, 2:3],
                                                                          scalar2=cb[:, fc:fc + 1], op0=ALU.mult, op1=ALU.add))(fc, abuf, ac),
                              reads=[b_abuf], writes=[b_ac])
                        P.add("dve", (lambda fc, abuf, ac: lambda e: e.scalar_tensor_tensor(out=ac[:, :], in0=abuf[:, 1:513], scalar=cw[:, fc, 1:2],
                                                                                 in1=ac[:, :], op0=ALU.mult, op1=ALU.add))(fc, abuf, ac),
                              reads=[b_abuf, b_ac], writes=[b_ac])
                        P.add("dve", (lambda fc, abuf, ac: lambda e: e.scalar_tensor_tensor(out=ac[:, :], in0=abuf[:, 0:512], scalar=cw[:, fc, 0:1],
                                                                                 in1=ac[:, :], op0=ALU.mult, op1=ALU.add))(fc, abuf, ac),
                              reads=[b_abuf, b_ac], writes=[b_ac])
                        P.add("act", (lambda ac, ge: lambda e: e.activation(out=ge[:, :], in_=ac[:, :], func=AF.Gelu_apprx_tanh))(ac, ge), reads=[b_ac], writes=[b_ge])
                        P.add("dve", (lambda fc, bg, ge: lambda e: e.tensor_tensor(out=hh[:, fc, :], in0=bank(bg), in1=ge[:, :], op=ALU.mult))(fc, bg, ge),
                              reads=[PB[bg], b_ge], writes=[b_hh])
                    if layer == 1 and n == 0 and _SKIP0:
                        continue
                    for dc in range(8):
                        b = 4 + dc % 2
                        for fc in range(NFC):
                            P.add("pe", (lambda b, dc, fc: lambda e: e.matmul(bank(b), lhsT=wdn[:, fc, dc * 128:(dc + 1) * 128],
                                                                              rhs=hh[:, fc, :], start=(fc == 0), stop=(fc == NFC - 1)))(b, dc, fc),
                                  reads=[b_hh], writes=[PB[b]])
                        P.add("act", (lambda b, dc: lambda e: e.activation(out=xf[:, dc, :], in_=bank(b), func=AF.Copy))(b, dc),
                              reads=[PB[b]], writes=[b_xf])
                        P.add("act", (lambda b, dc: lambda e: e.activation(out=xn[:, dc, :], in_=bank(b), func=AF.Square))(b, dc),
                              reads=[PB[b]], writes=[b_xn])
                    rstd_from_sq(P, lambda c: xn[:, c, :], 8, ones_m, 7, b_xn, rt, rs, b_rt, b_rs, NORM_EPS)
                    for dc in range(8):
                        i = dc % 2
                        P.add("sp", (lambda n, dc, i: lambda e: e.dma_start(
                            out=xr[i][:, :], in_=src.ap()[dc * 128:(dc + 1) * 128, n * 512:(n + 1) * 512]))(n, dc, i),
                            writes=[b_xr[i]], dma=True)
                        P.add("dve", (lambda dc: lambda e: e.scalar_tensor_tensor(
                            out=tmp[:, :], in0=xf[:, dc, :], scalar=gn[:, gpost, dc:dc + 1], in1=rs[:, :],
                            op0=ALU.mult, op1=ALU.mult))(dc), reads=[b_xf, b_rs], writes=[b_tmp])
                        P.add("dve", (lambda dc, i: lambda e: e.tensor_tensor(out=xf[:, dc, :], in0=tmp[:, :], in1=xr[i][:, :], op=ALU.add))(dc, i),
                              reads=[b_tmp, b_xr[i]], writes=[b_xf])
                    if dst_is_out and _OUTINT:
                        pass
                    elif dst_is_out:
                        P.add("sp", (lambda n: lambda e: e.dma_start(out=fm(dst)[:, :, (n - 1) * 512:n * 512], in_=xf[:, :, :]))(n),
                              reads=[b_xf], dma=True, key="st_xf")
                    else:
                        P.add("sp", (lambda n: lambda e: e.dma_start(out=fm(dst)[:, :, n * 512:(n + 1) * 512], in_=xf[:, :, :]))(n),
                              reads=[b_xf], dma=True, key="st_xf")
                P.emit()

        ffn_phase(0, xmid0, x1s, False)
        if _TESTF == 1:
            ffn_phase(0, xmid0, x1s, False)
            return nc
        if _TESTF == 2:
            ffn_phase(1, xmid0, x1s, False)
            return nc
        if _STOP < 6:
            return nc

        with contextlib.ExitStack() as st:
            win = sb(st, "win", [128, 8, 2 * E], BF16)
            wout = sb(st, "wout", [128, NEC, D], BF16)
            wsm = sb(st, "wsm", [128, 8, 128], BF16)
            bs = sb(st, "bs", [128, 8, 128], F32)
            lg = sb(st, "lg", [128, E], F32)
            lb = sb(st, "lb", [128, E], F32)
            b_w = Buf("w")
            with contextlib.ExitStack() as s2:
                stg = [sb(s2, f"stgs{i}", [128, 2 * E], F32) for i in range(2)]
                trl = sb(s2, "trl", [128, 128], F32)
                b_stg = [Buf(f"stg{i}") for i in range(2)]
                b_trl = Buf("trl")
                P = newP("sw")
                P.add("sp", lambda e: e.dma_start(out=bs[:, :, :], in_=bsb.ap()), writes=[b_w], dma=True, key="misc")
                P.add("sp", lambda e: e.dma_start(out=lg[:, :], in_=lng.ap()), writes=[b_w], dma=True, key="misc")
                P.add("sp", lambda e: e.dma_start(out=lb[:, :], in_=lnb.ap()), writes=[b_w], dma=True, key="misc")
                P.add("sp", lambda e: e.dma_start(out=trl[:, :], in_=tril.ap()), writes=[b_trl], dma=True, key="misc")
                P.add("sp", lambda e: e.dma_start(out=stg[0][:, 0:1024].rearrange("p (g t) -> p g t", t=128), in_=wsT.ap()),
                      writes=[b_stg[0]], dma=True)
                for g8 in range(8):
                    P.add("dve", (lambda g8: lambda e: e.tensor_tensor(out=wsm[:, g8, :], in0=stg[0][:, g8 * 128:(g8 + 1) * 128],
                                                                      in1=trl[:, :], op=ALU.mult))(g8),
                          reads=[b_stg[0], b_trl], writes=[b_w])
                for c in range(8):
                    i = (c + 1) % 2
                    P.add("sp", (lambda c, i: lambda e: e.dma_start(out=stg[i][:, :], in_=w_in.ap()[:, c, :]))(c, i),
                          writes=[b_stg[i]], dma=True)
                    P.add("dve", (lambda c, i: lambda e: e.tensor_scalar(out=win[:, c, :], in0=stg[i][:, :],
                                                                        scalar1=gn[:, 4, c:c + 1], scalar2=None, op0=ALU.mult))(c, i),
                          reads=[b_stg[i]], writes=[b_w])
                for q in range(4):
                    i = (q + 1) % 2
                    P.add("sp", (lambda q, i: lambda e: e.dma_start(
                        out=stg[i][:, 0:4 * D].rearrange("p (f n) -> p f n", n=D), in_=w_out.ap()[:, q * 4:(q + 1) * 4, :]))(q, i),
                        writes=[b_stg[i]], dma=True)
                    P.add("pool", (lambda q, i: lambda e: e.tensor_copy(
                        out=wout[:, q * 4:(q + 1) * 4, :], in_=stg[i][:, 0:4 * D].rearrange("p (f n) -> p f n", n=D)))(q, i),
                        reads=[b_stg[i]], writes=[b_w])
                P.emit()
            xt = sb(st, "xts", [128, 8, 512], F32)
            xn = sb(st, "xns", [128, 8, 512], BF16)
            uu = sb(st, "uu", [128, NEC, 512], BF16)
            vt = sb(st, "vt", [128, E], F32)
            junk = sb(st, "junk", [128, E], BF16)
            vnb = sb(st, "vnb", [128, E], BF16)
            vsum = sb(st, "vsum", [128, 4], F32)
            st1 = sb(st, "st1", [128, 4], F32)
            msb = sb(st, "msbs", [128, 8, 512], F32)
            rt = sb(st, "rts", [128, 512], F32)
            rs = sb(st, "rss", [128, 512], F32)
            tmp = sb(st, "tmps", [128, 512], F32)
            ts = sb(st, "ts", [128, 128], F32)
            b_xt, b_xn, b_uu, b_vt, b_junk, b_vnb, b_vsum, b_st1, b_msb, b_rt, b_rs, b_tmp, b_ts = (
                Buf(n) for n in ("xt", "xn", "uu", "vt", "junk", "vnb", "vsum", "st1", "msb", "rt", "rs", "tmp", "ts"))
            P = newP("sg")
            for n in range(NL):
                P.add("sp", (lambda n: lambda e: e.dma_start(out=xt[:, :, :], in_=fm(x1s)[:, :, n * 512:(n + 1) * 512]))(n),
                      writes=[b_xt], dma=True)
                P.add("act", lambda e: e.activation(out=xn[:, :, :], in_=xt[:, :, :], func=AF.Square), reads=[b_xt], writes=[b_xn])
                rstd_from_sq(P, lambda c: xn[:, c, :], 8, ones_m, 7, b_xn, rt, rs, b_rt, b_rs, NORM_EPS)
                for c in range(8):
                    P.add("dve", (lambda c: lambda e: e.tensor_tensor(out=xn[:, c, :], in0=xt[:, c, :], in1=rs[:, :], op=ALU.mult))(c),
                          reads=[b_xt, b_rs], writes=[b_xn])
                for uc in range(NEC):
                    b = uc % 2
                    for c in range(8):
                        P.add("pe", (lambda b, uc, c: lambda e: e.matmul(bank(b), lhsT=win[:, c, uc * 128:(uc + 1) * 128], rhs=xn[:, c, :],
                                                                         start=(c == 0), stop=(c == 7)))(b, uc, c),
                              reads=[b_xn], writes=[PB[b]])
                    P.add("act", (lambda b, uc: lambda e: e.activation(out=uu[:, uc, :], in_=bank(b), func=AF.Gelu_apprx_tanh))(b, uc),
                          reads=[PB[b]], writes=[b_uu])
                for j in range(4):
                    for q in range(4):
                        b = 2 + q % 2
                        for c in range(8):
                            P.add("pe", (lambda b, q, c, j: lambda e: e.matmul(
                                bank(b), lhsT=xn[:, c, j * 128:(j + 1) * 128], rhs=win[:, c, E + q * 512:E + (q + 1) * 512],
                                start=(c == 0), stop=(c == 7)))(b, q, c, j),
                                reads=[b_xn], writes=[PB[b]])
                        P.add("act", (lambda b, q: lambda e: e.activation(out=vt[:, q * 512:(q + 1) * 512], in_=bank(b),
                                                                          func=AF.Gelu_apprx_tanh, accum_out=vsum[:, q:q + 1]))(b, q),
                              reads=[PB[b]], writes=[b_vt, b_vsum])
                    P.add("dve", lambda e: e.reduce_sum(out=st1[:, 0:1], in_=vsum[:, :], axis=AX.X), reads=[b_vsum], writes=[b_st1])
                    P.add("dve", lambda e: e.tensor_scalar(out=st1[:, 0:1], in0=st1[:, 0:1], scalar1=-1.0 / E, scalar2=None, op0=ALU.mult),
                          reads=[b_st1], writes=[b_st1])
                    P.add("dve", lambda e: e.tensor_scalar(out=vt[:, :], in0=vt[:, :], scalar1=st1[:, 0:1], scalar2=None, op0=ALU.add),
                          reads=[b_vt, b_st1], writes=[b_vt])
                    P.add("act", lambda e: e.activation(out=junk[:, :], in_=vt[:, :], func=AF.Square, accum_out=st1[:, 1:2]),
                          reads=[b_vt], writes=[b_junk, b_st1])
                    P.add("dve", lambda e: e.tensor_scalar(out=st1[:, 2:3], in0=st1[:, 1:2], scalar1=1.0 / E, scalar2=LN_EPS,
                                                           op0=ALU.mult, op1=ALU.add), reads=[b_st1], writes=[b_st1])
                    P.add("act", lambda e: e.activation(out=st1[:, 3:4], in_=st1[:, 2:3], func=AF.Sqrt), reads=[b_st1], writes=[b_st1])
                    P.add("dve", lambda e: e.reciprocal(out=st1[:, 2:3], in_=st1[:, 3:4]), reads=[b_st1], writes=[b_st1])
                    P.add("dve", lambda e: e.scalar_tensor_tensor(out=vt[:, :], in0=vt[:, :], scalar=st1[:, 2:3], in1=lg[:, :],
                                                                 op0=ALU.mult, op1=ALU.mult), reads=[b_vt, b_st1], writes=[b_vt])
                    P.add("dve", lambda e: e.tensor_tensor(out=vnb[:, :], in0=vt[:, :], in1=lb[:, :], op=ALU.add),
                          reads=[b_vt], writes=[b_vnb])
                    for fc in range(NEC):
                        b = 4 + fc % 2
                        g8 = fc // 2
                        P.add("pe", (lambda b, fc, g8: lambda e: e.matmul(bank(b, 128, 128), lhsT=vnb[:, fc * 128:(fc + 1) * 128],
                                                                          rhs=wsm[:, g8, :], start=True, stop=True))(b, fc, g8),
                              reads=[b_vnb], writes=[PB[b]])
                        P.add("dve", (lambda b, g8: lambda e: e.tensor_tensor(out=ts[:, :], in0=bank(b, 128, 128), in1=bs[:, g8, :], op=ALU.add))(b, g8),
                              reads=[PB[b]], writes=[b_ts])
                        P.add("dve", (lambda fc, j: lambda e: e.tensor_tensor(out=uu[:, fc, j * 128:(j + 1) * 128], in0=ts[:, :],
                                                                             in1=uu[:, fc, j * 128:(j + 1) * 128], op=ALU.mult))(fc, j),
                              reads=[b_ts, b_uu], writes=[b_uu])
                for dc in range(8):
                    b = 6 + dc % 2 if False else dc % 2
                    for fc in range(NEC):
                        P.add("pe", (lambda b, dc, fc: lambda e: e.matmul(bank(b), lhsT=wout[:, fc, dc * 128:(dc + 1) * 128], rhs=uu[:, fc, :],
                                                                          start=(fc == 0), stop=(fc == NEC - 1)))(b, dc, fc),
                              reads=[b_uu], writes=[PB[b]])
                    P.add("act", (lambda b, dc: lambda e: e.activation(out=msb[:, dc, :], in_=bank(b), func=AF.Copy))(b, dc),
                          reads=[PB[b]], writes=[b_msb])
                    P.add("act", (lambda b, dc: lambda e: e.activation(out=xn[:, dc, :], in_=bank(b), func=AF.Square))(b, dc),
                          reads=[PB[b]], writes=[b_xn])
                postnorm_residual(P, msb, b_msb, xn, b_xn, rt, rs, b_rt, b_rs, 5, lambda dc: xt[:, dc, :], b_xt,
                                  lambda dc: msb[:, dc, :], b_msb, tmp, b_tmp)
                P.add("sp", (lambda n: lambda e: e.dma_start(out=fm(xmid1)[:, :, n * 512:(n + 1) * 512], in_=msb[:, :, :]))(n),
                      reads=[b_msb], dma=True, key="st_msb")
            P.emit()

        if _STOP < 7:
            return nc
        if _TESTF == 3:
            ffn_phase(1, xmid0, x1s, False)
            return nc
        if _TESTF == 4:
            ffn_phase(0, xmid1, x1s, False)
            return nc
        ffn_phase(1, xmid1, yT, True)
    return nc


_CACHE = {}


def _prep_inputs(S, inp):
    f32 = np.float32
    T = S // 4
    TL = T + HALO
    NL = TL // 512
    NT = S // 512
    x = np.asarray(inp["x"], f32)
    wqkv_full = np.asarray(inp["attn_w_qkv"], f32)[0]
    pos = np.arange(S)
    c7, a3, b4 = pos // 128, (pos % 128) // 16, pos % 16
    tril = (np.arange(128)[:, None] <= np.arange(128)[None, :]).astype(f32)
    ii = np.arange(128)[:, None, None]
    vv = np.arange(4)[None, :, None]
    jj = np.arange(512)[None, None, :]
    maskd = np.where(jj >= 128 * vv + ii, 0.0, NEG).astype(f32)
    gl = [inp["norm_mix_pre"][0], inp["norm_mix_post"][0], inp["norm_ffn_pre"][0], inp["norm_ffn_post"][0],
          inp["norm_mix_pre"][1], inp["norm_mix_post"][1], inp["norm_ffn_pre"][1], inp["norm_ffn_post"][1]]
    gains = np.ascontiguousarray(np.stack([np.asarray(g, f32).reshape(8, 128).T for g in gl], axis=1))
    lamv = np.stack([inp["attn_lambda_q1"][0], inp["attn_lambda_k1"][0], inp["attn_lambda_q2"][0], inp["attn_lambda_k2"][0]])[None].astype(f32)
    subln = np.asarray(inp["attn_subln"], f32)[0].reshape(128, 1)
    w_o = np.ascontiguousarray(np.asarray(inp["attn_w_o"], f32)[0].reshape(8, 128, D).transpose(1, 0, 2))
    w_up = np.ascontiguousarray(np.asarray(inp["ffn_w_up"], f32).reshape(2, 8, 128, 2 * FF).transpose(0, 2, 1, 3))
    w_dn = np.ascontiguousarray(np.asarray(inp["ffn_w_down"], f32).reshape(2, NFC, 128, D).transpose(0, 2, 1, 3))
    convw = np.ascontiguousarray(np.asarray(inp["ffn_conv_w"], f32).reshape(2, 3, NFC, 128).transpose(0, 3, 2, 1))
    convb = np.ascontiguousarray(np.asarray(inp["ffn_conv_b"], f32).reshape(2, NFC, 128).transpose(0, 2, 1))
    w_in = np.ascontiguousarray(np.asarray(inp["sgu_w_in"], f32)[0].reshape(8, 128, 2 * E).transpose(1, 0, 2))
    lng = np.ascontiguousarray(np.broadcast_to(np.asarray(inp["sgu_ln_g"], f32)[0][None, :], (128, E)))
    lnb = np.ascontiguousarray(np.broadcast_to(np.asarray(inp["sgu_ln_b"], f32)[0][None, :], (128, E)))
    wsT = np.ascontiguousarray(np.asarray(inp["sgu_w_s"], f32)[0].transpose(2, 0, 1))
    bsb = np.ascontiguousarray(np.broadcast_to(np.asarray(inp["sgu_b_s"], f32)[0][None], (128, 8, 128)))
    w_out = np.ascontiguousarray(np.asarray(inp["sgu_w_out"], f32)[0].reshape(NEC, 128, D).transpose(1, 0, 2))
    xT = [np.ascontiguousarray(x[b].T) for b in range(2)]
    maps = []
    for c in range(8):
        b, r = c // 4, c % 4
        t0 = r * T - HALO
        xl = np.zeros((D, TL), f32)
        lo = max(t0, 0)
        xl[:, lo - t0:] = xT[b][:, lo:t0 + TL]
        valid0 = np.ones((128, 512), f32) if r > 0 else np.zeros((128, 512), f32)
        wq = np.zeros((2, 5, D, 128), f32)
        ka = np.zeros((2, 6, S), ml_dtypes.bfloat16)
        qa = np.zeros((2, 6, S), ml_dtypes.bfloat16)
        for s in range(2):
            hd = r if s == 0 else 4 + r
            slope = 2.0 ** (-(hd + 1))
            for part, base in ((0, 0), (1, D)):
                A = wqkv_full[:, base + hd * 128: base + (hd + 1) * 128]
                wq[s, 2 * part + 0] = A
                wq[s, 2 * part + 1] = np.concatenate([A[:, 64:], A[:, :64]], axis=1)
            wq[s, 4] = wqkv_full[:, 2 * D + hd * 128: 2 * D + (hd + 1) * 128]
            ka[s, 0:3] = 1.0
            ka[s, 3] = (slope * 128.0 * c7).astype(ml_dtypes.bfloat16)
            ka[s, 4] = (slope * 16.0 * a3).astype(ml_dtypes.bfloat16)
            ka[s, 5] = (slope * b4).astype(ml_dtypes.bfloat16)
            qa[s, 0] = (-slope * 128.0 * c7).astype(ml_dtypes.bfloat16)
            qa[s, 1] = (-slope * 16.0 * a3).astype(ml_dtypes.bfloat16)
            qa[s, 2] = (-slope * b4).astype(ml_dtypes.bfloat16)
            qa[s, 3:6] = 1.0
        idx = np.zeros((128, NL * 8), np.int32)
        for n in range(NL):
            tile = max(r * (T // 512) + n - 1, 0)
            for hd in range(8):
                rank, s = hd % 4, hd // 4
                unit = s * NT + tile
                idx[:, n * 8 + hd] = ((unit // 4) * 4 + rank) * 512 + (unit % 4) * 128 + np.arange(128)
        maps.append({
            "xT_full": xT[b], "xT_loc": xl, "valid0": valid0, "kaug0": ka[0], "kaug1": ka[1], "qaug0": qa[0], "qaug1": qa[1], "maskd": maskd,
            **{f"wqkv{s_}_{w_}": np.ascontiguousarray(wq[s_, w_]) for s_ in range(2) for w_ in range(5)},
            "lamv": lamv, "subln": subln, "gains": gains, "idxtab": idx, "w_o": w_o, "w_up0": w_up[0], "w_up1": w_up[1], "w_dn0": w_dn[0], "w_dn1": w_dn[1],
            "convw0": convw[0], "convw1": convw[1], "convb0": convb[0], "convb1": convb[1], "w_in": w_in, "lng": lng, "lnb": lnb, "wsT": wsT, "tril": tril,
            "bsb": bsb, "w_out": w_out,
        })
    return maps


def kernel(**inputs):
    x = inputs["x"]
    S = x.shape[1]
    if S not in _CACHE:
        _CACHE[S] = build_program(S)
    nc = _CACHE[S]
    maps = _prep_inputs(S, inputs)
    res = run_bass_kernel_spmd(nc, maps, core_ids=list(range(8)))
    T = S // 4
    out = np.zeros((2, S, D), np.float32)
    for c in range(8):
        b, r = c // 4, c % 4
        out[b, r * T:(r + 1) * T, :] = np.asarray(res.results[c]["yT"]).T
    return out
```
